# Optimizing a Trainium2 kernel written in Bass

```python
import jax, jax.numpy as jnp
from jax import lax
import numpy as np

D_MODEL = 1024
BATCH = 4
SEQ = 8192
DEPTH = 4
DEC_BATCH = 16
DEC_SEQ = 32
PAST_LEN = 1024

CHUNK = 64
SSM_WIDTH = 256
SSM_GROUP = 16
SSM_GROUPS = SSM_WIDTH // SSM_GROUP
SSM_STATE = 64
SGU_WIDTH = 256
SGU_CHUNK = 128
SGU_GROUPS = 4
SGU_GROUP_DIM = SGU_WIDTH // SGU_GROUPS
SB_HEADS = 8
SB_HEAD_DIM = 64
SB_WIDTH = SB_HEADS * SB_HEAD_DIM
SB_BLOCK = 128
N_BRANCH = 3
FFN_HIDDEN = ((8 * D_MODEL + 3 * 256 - 1) // (3 * 256)) * 256
IN_WIDTH = SSM_WIDTH + 2 * SGU_WIDTH + 3 * SB_WIDTH + N_BRANCH * D_MODEL
IN_SPLITS = (SSM_WIDTH,
             SSM_WIDTH + SGU_WIDTH,
             SSM_WIDTH + 2 * SGU_WIDTH,
             SSM_WIDTH + 2 * SGU_WIDTH + SB_WIDTH,
             SSM_WIDTH + 2 * SGU_WIDTH + 2 * SB_WIDTH,
             SSM_WIDTH + 2 * SGU_WIDTH + 3 * SB_WIDTH)
EPS = 1e-6

kernel_name = "hybrid_stream_s5_gmlp_stickbreak_step"


def rmsnorm(x, g):
    xf = x.astype(jnp.float32)
    y = xf * lax.rsqrt(jnp.mean(xf * xf, axis=-1, keepdims=True) + EPS)
    return (y * g.astype(jnp.float32)).astype(x.dtype)


def layernorm(x, g):
    xf = x.astype(jnp.float32)
    mu = jnp.mean(xf, axis=-1, keepdims=True)
    xc = xf - mu
    y = xc * lax.rsqrt(jnp.mean(xc * xc, axis=-1, keepdims=True) + EPS)
    return (y * g.astype(jnp.float32)).astype(x.dtype)


def s5_discretize(a_re, a_im, b_re, b_im, log_dt):
    f32 = jnp.float32
    a_re, a_im, b_re, b_im = a_re.astype(f32), a_im.astype(f32), b_re.astype(f32), b_im.astype(f32)
    dt = jnp.exp(log_dt.astype(f32))[:, None]
    mag = jnp.exp(a_re * dt)
    ang = a_im * dt
    ab_re = mag * jnp.cos(ang)
    ab_im = mag * jnp.sin(ang)
    num_re = ab_re - 1.0
    num_im = ab_im
    den = a_re * a_re + a_im * a_im
    f_re = (num_re * a_re + num_im * a_im) / den
    f_im = (num_im * a_re - num_re * a_im) / den
    bb_re = f_re[..., None] * b_re - f_im[..., None] * b_im
    bb_im = f_re[..., None] * b_im + f_im[..., None] * b_re
    return ab_re, ab_im, bb_re, bb_im


def _complex_affine_combine(e1, e2):
    a1r, a1i, b1r, b1i = e1
    a2r, a2i, b2r, b2i = e2
    return (a2r * a1r - a2i * a1i,
            a2r * a1i + a2i * a1r,
            a2r * b1r - a2i * b1i + b2r,
            a2r * b1i + a2i * b1r + b2i)


def s5_mixer(u, s0_re, s0_im, a_re, a_im, b_re, b_im, c_re, c_im, d, log_dt, w_glu):
    f32 = jnp.float32
    bsz, L, _ = u.shape
    uf = u.astype(f32)
    ug = uf.reshape(bsz, L, SSM_GROUPS, SSM_GROUP)
    ab_re, ab_im, bb_re, bb_im = s5_discretize(a_re, a_im, b_re, b_im, log_dt)
    bu_re = jnp.einsum("blgc,gpc->blgp", ug, bb_re)
    bu_im = jnp.einsum("blgc,gpc->blgp", ug, bb_im)
    s0_re = s0_re.astype(f32)
    s0_im = s0_im.astype(f32)
    bu_re = bu_re.at[:, 0].add(ab_re * s0_re - ab_im * s0_im)
    bu_im = bu_im.at[:, 0].add(ab_re * s0_im + ab_im * s0_re)
    shape = bu_re.shape
    elems = (jnp.broadcast_to(ab_re, shape), jnp.broadcast_to(ab_im, shape), bu_re, bu_im)
    _, _, st_re, st_im = lax.associative_scan(_complex_affine_combine, elems, axis=1)
    y = (jnp.einsum("blgp,gcp->blgc", st_re, c_re.astype(f32))
         - jnp.einsum("blgp,gcp->blgc", st_im, c_im.astype(f32)))
    y = y.reshape(bsz, L, SSM_WIDTH) + d.astype(f32) * uf
    z = jax.nn.gelu(y) @ w_glu.astype(f32)
    out = z[..., :SSM_WIDTH] * jax.nn.sigmoid(z[..., SSM_WIDTH:])
    return out.astype(u.dtype), st_re[:, -1], st_im[:, -1]


def sgu_mixer(u, v, ln_g, w_s, b_s):
    bsz, L, _ = v.shape
    vn = layernorm(v, ln_g)
    n = min(L, SGU_CHUNK)
    nchunks = L // n
    mask = jnp.tril(jnp.ones((n, n), dtype=bool))
    w = jnp.where(mask[None], w_s[:, :n, :n], 0.0)
    vc = vn.reshape(bsz, nchunks, n, SGU_GROUPS, SGU_GROUP_DIM)
    mixed = jnp.einsum("gts,bnsgc->bntgc", w, vc) + b_s[:, :n].T[None, None, :, :, None]
    out = u * mixed.reshape(bsz, L, SGU_WIDTH)
    return out.astype(u.dtype), vn


def stick_breaking(q, k, v, q_pos, k_pos):
    f32 = jnp.float32
    z = jnp.einsum("bhtd,bhsd->bhts", q.astype(f32), k.astype(f32)) * (SB_HEAD_DIM ** -0.5)
    causal = k_pos[None, :] < q_pos[:, None]
    log_beta = jax.nn.log_sigmoid(z)
    log_1m = jnp.where(causal, jax.nn.log_sigmoid(-z), 0.0)
    between = lax.cumsum(log_1m, axis=3, reverse=True) - log_1m
    wts = jnp.where(causal, jnp.exp(log_beta + between), 0.0)
    return jnp.einsum("bhts,bhsd->bhtd", wts, v.astype(f32))


def stick_breaking_blocks(q, k, v):
    bsz, H, L, Dh = q.shape
    nb = L // SB_BLOCK
    qb = q.reshape(bsz, H, nb, SB_BLOCK, Dh).transpose(2, 0, 1, 3, 4)
    k_pos = jnp.arange(L)

    def one_block(args):
        qi, bi = args
        q_pos = bi * SB_BLOCK + jnp.arange(SB_BLOCK)
        return stick_breaking(qi, k, v, q_pos, k_pos)

    out = lax.map(one_block, (qb, jnp.arange(nb)))
    return out.transpose(1, 2, 0, 3, 4).reshape(bsz, H, L, Dh)


def trunk_layer(h, ssm_re0, ssm_im0, k_past, v_past,
                norm_mix, w_in, a_re, a_im, b_re, b_im, c_re, c_im, d, log_dt, w_glu,
                sgu_norm, sgu_w, sgu_b, w_branch_a, w_branch_b, w_branch_c, w_out,
                norm_ffn, w_gate_up, w_down):
    bsz, L, _ = h.shape
    xn = rmsnorm(h, norm_mix)
    proj = xn @ w_in
    u_a, u_b, v_b, q, k, v, gates = jnp.split(proj, IN_SPLITS, axis=-1)
    y_a, st_re, st_im = s5_mixer(u_a, ssm_re0, ssm_im0, a_re, a_im, b_re, b_im,
                                 c_re, c_im, d, log_dt, w_glu)
    y_b, vn_b = sgu_mixer(u_b, v_b, sgu_norm, sgu_w, sgu_b)
    heads = lambda t: t.reshape(bsz, L, SB_HEADS, SB_HEAD_DIM).transpose(0, 2, 1, 3)
    qh, kh, vh = heads(q), heads(k), heads(v)
    if k_past is None:
        o = stick_breaking_blocks(qh, kh, vh)
    else:
        past = k_past.shape[2]
        k_all = jnp.concatenate([k_past.astype(kh.dtype), kh], axis=2)
        v_all = jnp.concatenate([v_past.astype(vh.dtype), vh], axis=2)
        o = stick_breaking(qh, k_all, v_all, past + jnp.arange(L), jnp.arange(past + L))
    y_c = o.transpose(0, 2, 1, 3).reshape(bsz, L, SB_WIDTH).astype(h.dtype)
    g = jax.nn.sigmoid(gates.astype(jnp.float32)).reshape(bsz, L, N_BRANCH, D_MODEL)
    merged = (g[:, :, 0] * (y_a @ w_branch_a) + g[:, :, 1] * (y_b @ w_branch_b)
              + g[:, :, 2] * (y_c @ w_branch_c))
    h = h + (merged.astype(h.dtype) @ w_out).astype(h.dtype)
    xn2 = rmsnorm(h, norm_ffn)
    gu = xn2 @ w_gate_up
    gate, up = gu[..., :FFN_HIDDEN], gu[..., FFN_HIDDEN:]
    h = h + ((jax.nn.silu(gate) * up) @ w_down).astype(h.dtype)
    return h, st_re, st_im, kh, vh, vn_b


def setup_inputs(seed: int = 0) -> dict:
    key = jax.random.key(seed)
    ks = jax.random.split(key, 32)
    f32 = jnp.float32
    nrm = lambda k, shape, s: jax.random.normal(k, shape, f32) * s
    a_im_base = jnp.pi * jnp.arange(SSM_STATE, dtype=f32)
    log_dt = jnp.log(1e-3) + jax.random.uniform(ks[10], (DEPTH, SSM_GROUPS), f32) * (jnp.log(1e-1) - jnp.log(1e-3))
    return {
        "x_prompt": nrm(ks[0], (BATCH, SEQ, D_MODEL), 1.0),
        "x_sample": nrm(ks[1], (DEC_BATCH, DEC_SEQ, D_MODEL), 1.0),
        "state_ssm_re": nrm(ks[2], (DEPTH, DEC_BATCH, SSM_GROUPS, SSM_STATE), 0.5),
        "state_ssm_im": nrm(ks[3], (DEPTH, DEC_BATCH, SSM_GROUPS, SSM_STATE), 0.5),
        "cache_sb_k": nrm(ks[4], (DEPTH, DEC_BATCH, SB_HEADS, PAST_LEN, SB_HEAD_DIM), 1.0),
        "cache_sb_v": nrm(ks[5], (DEPTH, DEC_BATCH, SB_HEADS, PAST_LEN, SB_HEAD_DIM), 1.0),
        "norm_mix": 1.0 + nrm(ks[6], (DEPTH, D_MODEL), 0.02),
        "w_in": nrm(ks[7], (DEPTH, D_MODEL, IN_WIDTH), D_MODEL ** -0.5),
        "ssm_a_re": -0.5 + nrm(ks[8], (DEPTH, SSM_GROUPS, SSM_STATE), 0.01),
        "ssm_a_im": a_im_base + nrm(ks[9], (DEPTH, SSM_GROUPS, SSM_STATE), 0.01),
        "ssm_b_re": nrm(ks[11], (DEPTH, SSM_GROUPS, SSM_STATE, SSM_GROUP), (2 * SSM_GROUP) ** -0.5),
        "ssm_b_im": nrm(ks[12], (DEPTH, SSM_GROUPS, SSM_STATE, SSM_GROUP), (2 * SSM_GROUP) ** -0.5),
        "ssm_c_re": nrm(ks[13], (DEPTH, SSM_GROUPS, SSM_GROUP, SSM_STATE), (2 * SSM_STATE) ** -0.5),
        "ssm_c_im": nrm(ks[14], (DEPTH, SSM_GROUPS, SSM_GROUP, SSM_STATE), (2 * SSM_STATE) ** -0.5),
        "ssm_d": nrm(ks[15], (DEPTH, SSM_WIDTH), 1.0),
        "ssm_log_dt": log_dt,
        "ssm_w_glu": nrm(ks[16], (DEPTH, SSM_WIDTH, 2 * SSM_WIDTH), SSM_WIDTH ** -0.5),
        "sgu_norm": 1.0 + nrm(ks[17], (DEPTH, SGU_WIDTH), 0.02),
        "sgu_w": nrm(ks[18], (DEPTH, SGU_GROUPS, SGU_CHUNK, SGU_CHUNK), SGU_CHUNK ** -0.5),
        "sgu_b": 1.0 + nrm(ks[19], (DEPTH, SGU_GROUPS, SGU_CHUNK), 0.02),
        "w_branch_a": nrm(ks[20], (DEPTH, SSM_WIDTH, D_MODEL), SSM_WIDTH ** -0.5),
        "w_branch_b": nrm(ks[21], (DEPTH, SGU_WIDTH, D_MODEL), SGU_WIDTH ** -0.5),
        "w_branch_c": nrm(ks[22], (DEPTH, SB_WIDTH, D_MODEL), SB_WIDTH ** -0.5),
        "w_out": nrm(ks[23], (DEPTH, D_MODEL, D_MODEL), D_MODEL ** -0.5),
        "norm_ffn": 1.0 + nrm(ks[24], (DEPTH, D_MODEL), 0.02),
        "w_gate_up": nrm(ks[25], (DEPTH, D_MODEL, 2 * FFN_HIDDEN), D_MODEL ** -0.5),
        "w_down": nrm(ks[26], (DEPTH, FFN_HIDDEN, D_MODEL), FFN_HIDDEN ** -0.5),
        "norm_final": 1.0 + nrm(ks[27], (D_MODEL,), 0.02),
    }


def reference(x_prompt, x_sample, state_ssm_re, state_ssm_im, cache_sb_k, cache_sb_v,
              norm_mix, w_in, ssm_a_re, ssm_a_im, ssm_b_re, ssm_b_im, ssm_c_re, ssm_c_im,
              ssm_d, ssm_log_dt, ssm_w_glu, sgu_norm, sgu_w, sgu_b,
              w_branch_a, w_branch_b, w_branch_c, w_out, norm_ffn, w_gate_up, w_down,
              norm_final):
    layer_weights = (norm_mix, w_in, ssm_a_re, ssm_a_im, ssm_b_re, ssm_b_im, ssm_c_re, ssm_c_im,
                     ssm_d, ssm_log_dt, ssm_w_glu, sgu_norm, sgu_w, sgu_b,
                     w_branch_a, w_branch_b, w_branch_c, w_out, norm_ffn, w_gate_up, w_down)
    zero_state = jnp.zeros((x_prompt.shape[0], SSM_GROUPS, SSM_STATE), jnp.float32)
    hp, hs = x_prompt, x_sample
    p_re, p_im, p_k, p_v = [], [], [], []
    s_re, s_im, s_k, s_v, s_vb = [], [], [], [], []
    for i in range(DEPTH):
        lw = [w[i] for w in layer_weights]
        hp, a_re, a_im, kh, vh, _ = trunk_layer(hp, zero_state, zero_state, None, None, *lw)
        p_re.append(a_re); p_im.append(a_im); p_k.append(kh); p_v.append(vh)
        hs, b_re, b_im, kd, vd, vb = trunk_layer(hs, state_ssm_re[i], state_ssm_im[i],
                                                 cache_sb_k[i], cache_sb_v[i], *lw)
        s_re.append(b_re); s_im.append(b_im); s_k.append(kd); s_v.append(vd); s_vb.append(vb)
    y_prompt = rmsnorm(hp, norm_final)
    y_sample = rmsnorm(hs, norm_final)
    return (y_prompt, y_sample,
            jnp.stack(p_re), jnp.stack(p_im), jnp.stack(p_k), jnp.stack(p_v),
            jnp.stack(s_re), jnp.stack(s_im), jnp.stack(s_k), jnp.stack(s_v), jnp.stack(s_vb))
```

```python
import numpy as np
from contextlib import ExitStack
import concourse.bass as bass
import concourse.mybir as mybir
from concourse.bass_utils import run_bass_kernel_spmd

F32 = mybir.dt.float32
BF16 = mybir.dt.bfloat16
I32 = mybir.dt.int32
AF = mybir.ActivationFunctionType
ALU = mybir.AluOpType

D = 1024
KC = 8
INW = 5376
FF = 2816
NF = 22
NH = 8
EPS = 1e-6
TWO_PI = 6.283185307179586
NEG = -30000.0


class Cfg:
    def __init__(self, L=4, SEQ=8192, PAST=1024, with_sample=True):
        self.L = L
        self.SEQ = SEQ
        self.PAST = PAST
        self.NT = SEQ // 512
        self.with_sample = with_sample
        self.stop = 99


class _Stop(Exception):
    pass


class Sem:
    def __init__(self, h):
        self.h = h
        self.v = 0


class Buf:
    __slots__ = ("name", "w", "r", "grp", "lo", "hi", "excl")

    def __init__(self, name, grp=None, lo=0, hi=0, excl=False):
        self.name = name
        self.excl = excl
        self.w = {}
        self.r = {}
        self.grp = grp
        self.lo = lo
        self.hi = hi
        if grp is not None:
            grp.append(self)
        Buf.ALL.append(self)


Buf.ALL = []


class Eng:
    def __init__(self, name, sem):
        self.name = name
        self.sem = sem
        self.ops = []
        self.seen = {}


class Prog:
    def __init__(self, nc, es, n_dma_sems=40):
        self.nc = nc
        self.eng = {}
        for n in ("pe", "act", "dve", "pool", "sp"):
            self.eng[n] = Eng(n, Sem(es.enter_context(nc.semaphore("sem_" + n))))
        self.dsems = [Sem(es.enter_context(nc.semaphore("dsem%d" % i))) for i in range(n_dma_sems)]
        self.barA = es.enter_context(nc.semaphore("barA"))
        self.barB = es.enter_context(nc.semaphore("barB"))
        self.drr = 0
        self.nops = 0
        self.stopped = False
        self.log = []
        self.cur_desc = ''
        self.semname = {id(e.sem): n for n, e in self.eng.items()}
        for i_, s_ in enumerate(self.dsems):
            self.semname[id(s_)] = 'd%d' % i_

    def _deps(self, reads, writes):
        deps = {}

        def need(d):
            for s, v in d.items():
                if deps.get(s, 0) < v:
                    deps[s] = v

        for b in reads:
            need(b.w)
            if b.excl:
                need(b.r)
        for b in writes:
            need(b.w)
            need(b.r)
            if b.grp is not None:
                for y in b.grp:
                    if y is not b and y.lo < b.hi and b.lo < y.hi:
                        need(y.w)
                        need(y.r)
        return deps

    def _waits(self, E, deps):
        waits = []
        for s, v in deps.items():
            if s is E.sem:
                if E.name == "pe":
                    continue
                if E.sem.v - v >= 4:
                    continue
            if E.seen.get(s, 0) >= v:
                continue
            E.seen[s] = v
            waits.append((s, v))
        return waits

    def add(self, en, fn, reads=(), writes=()):
        if self.stopped:
            return
        E = self.eng[en]
        waits = self._waits(E, self._deps(reads, writes))
        E.sem.v += 1
        val = E.sem.v
        E.ops.append((waits, fn, E.sem, 1))
        self.log.append((en, E.sem.v, [(self.semname.get(id(s_), '?'), v_) for s_, v_ in waits], self.cur_desc))
        for b in reads:
            if b.r.get(E.sem, 0) < val:
                b.r[E.sem] = val
        for b in writes:
            b.w = {E.sem: val}
            b.r = {}
        self.nops += 1

    def dma(self, q, out, in_, reads=(), writes=()):
        if self.stopped:
            return
        E = self.eng[q]
        sem = self.dsems[self.drr % len(self.dsems)]
        self.drr += 1
        deps = self._deps(reads, writes)
        if sem.v > 0 and deps.get(sem, 0) < sem.v:
            deps[sem] = sem.v
        waits = self._waits(E, deps)
        sem.v += 16
        val = sem.v
        E.ops.append((waits, ('DMA', out, in_), sem, 16))
        self.log.append((q + '-dma', (self.semname.get(id(sem)), val), [(self.semname.get(id(s_), '?'), v_) for s_, v_ in waits], 'dma'))
        for b in reads:
            if b.r.get(sem, 0) < val:
                b.r[sem] = val
        for b in writes:
            b.w = {sem: val}
            b.r = {}
        self.nops += 1

    def final_wait(self, q, bufs):
        E = self.eng[q]
        deps = {}
        for b in bufs:
            for d in (b.w, b.r):
                for s, v in d.items():
                    if deps.get(s, 0) < v:
                        deps[s] = v
        for e2 in self.eng.values():
            if e2.sem.v > 0:
                deps[e2.sem] = max(deps.get(e2.sem, 0), e2.sem.v) if e2 is not E else deps.get(e2.sem, 0)
        deps = {s: v for s, v in deps.items() if v > 0 and s is not E.sem}
        for s in self.dsems:
            if s.v > 0:
                deps[s] = s.v
        waits = [(s, v) for s, v in deps.items()]
        E.ops.append((waits, None, None, 0))

    def barrier(self):
        finals = [(s, s.v) for s in [e.sem for e in self.eng.values()] + self.dsems if s.v > 0]
        for E in self.eng.values():
            E.ops.append(([], ('BAR', finals), None, 0))
            E.seen = {}
        for s in [e.sem for e in self.eng.values()] + self.dsems:
            s.v = 0
        for b in Buf.ALL:
            b.w = {}
            b.r = {}

    def loop_begin(self, n):
        for E in self.eng.values():
            E.ops.append(([], ('LOOP', n), None, 0))

    def loop_end(self):
        for E in self.eng.values():
            E.ops.append(([], ('ENDLOOP',), None, 0))

    def replay(self, block):
        amap = {"pe": block.tensor, "act": block.scalar, "dve": block.vector, "pool": block.gpsimd, "sp": block.sync}
        NE = len(self.eng)
        allsems = [e.sem for e in self.eng.values()] + self.dsems
        for n, deco in amap.items():
            ops = self.eng[n].ops
            mysem = self.eng[n].sem

            def body(e, ops=ops, n=n, mysem=mysem):
                st = {"li": None, "nbar": 0, "ctx": None, "nloop": 0}

                def run(lst):
                    idx = 0
                    while idx < len(lst):
                        waits, fn, sem, inc = lst[idx]
                        idx += 1
                        for s, v in waits:
                            e.wait_ge(s.h, v)
                        if fn is None:
                            continue
                        if isinstance(fn, tuple):
                            kind = fn[0]
                            if kind == 'DMA':
                                o, i_ = fn[1], fn[2]
                                if callable(o):
                                    o = o(st["li"])
                                if callable(i_):
                                    i_ = i_(st["li"])
                                try:
                                    e.dma_start(out=o, in_=i_).then_inc(sem.h, inc)
                                except Exception:
                                    print('DMA FAIL', n, o.tensor.name, o.offset, list(o.ap), i_.tensor.name, i_.offset, list(i_.ap))
                                    raise
                            elif kind == 'BAR':
                                for s, v in fn[1]:
                                    if s is not mysem:
                                        e.wait_ge(s.h, v)
                                e.sem_inc(self.barA, 1)
                                if st["li"] is None:
                                    k1 = st["nbar"] + 1
                                else:
                                    k1 = st["li"] + (st["nbar"] + 1)
                                if n == "sp":
                                    e.wait_ge(self.barA, k1 * NE)
                                    for s in allsems:
                                        e.sem_clear(s.h)
                                    e.sem_inc(self.barB, 1)
                                e.wait_ge(self.barB, k1)
                                if st["li"] is None:
                                    st["nbar"] += 1
                            elif kind == 'LOOP':
                                depth, j = 1, idx
                                while True:
                                    f2 = lst[j][1]
                                    if isinstance(f2, tuple) and f2[0] == 'LOOP':
                                        depth += 1
                                    if isinstance(f2, tuple) and f2[0] == 'ENDLOOP':
                                        depth -= 1
                                        if depth == 0:
                                            break
                                    j += 1
                                inner = lst[idx:j]
                                nb_in = sum(1 for x in inner if isinstance(x[1], tuple) and x[1][0] == 'BAR')
                                with e.Fori(0, fn[1]) as li:
                                    st["li"] = li
                                    run(inner)
                                    st["li"] = None
                                st["nbar"] += nb_in * fn[1]
                                idx = j + 1
                            continue
                        fn(e).then_inc(sem.h, inc)

                run(ops)

            deco(body)

    def mm(self, out, lhsT, rhs, start, stop, reads, writes):
        self.cur_desc = 'mm %s <- %s x %s st=%s sp=%s' % (_d(out), _d(lhsT), _d(rhs), start, stop)
        self.add("pe", lambda e: e.matmul(out, lhsT=lhsT, rhs=rhs, start=start, stop=stop), reads, writes)

    def tr(self, out, in_, ident, reads, writes):
        self.cur_desc = 'tr %s <- %s' % (_d(out), _d(in_))
        self.add("pe", lambda e: e.transpose(out, in_, ident), reads, writes)

    def act(self, out, in_, func, reads, writes, bias=None, scale=None, accum=None):
        self.cur_desc = 'act %s %s <- %s b=%s' % (func, _d(out), _d(in_), bias if isinstance(bias, (float, type(None))) else _d(bias))
        kw = {}
        if bias is not None:
            kw["bias"] = bias
        if scale is not None:
            kw["scale"] = scale
        if accum is not None:
            kw["accum_out"] = accum
        self.add("act", lambda e: e.activation(out=out, in_=in_, func=func, **kw), reads, writes)

    def tt(self, en, out, in0, in1, op, reads, writes):
        self.cur_desc = 'tt %s %s <- %s , %s' % (op, _d(out), _d(in0), _d(in1))
        self.add(en, lambda e: e.tensor_tensor(out=out, in0=in0, in1=in1, op=op), reads, writes)

    def ts(self, en, out, in0, s1, op0, reads, writes, s2=None, op1=None):
        self.cur_desc = 'ts %s %s <- %s' % (op0, out.tensor.name, in0.tensor.name)
        if op1 is None:
            self.add(en, lambda e: e.tensor_scalar(out=out, in0=in0, scalar1=s1, scalar2=None, op0=op0), reads, writes)
        else:
            self.add(en, lambda e: e.tensor_scalar(out=out, in0=in0, scalar1=s1, scalar2=s2, op0=op0, op1=op1), reads, writes)

    def stt(self, out, in0, scalar, in1, op0, op1, reads, writes):
        self.add("dve", lambda e: e.scalar_tensor_tensor(out=out, in0=in0, scalar=scalar, in1=in1, op0=op0, op1=op1), reads, writes)

    def cp(self, en, out, in_, reads, writes):
        self.cur_desc = 'cp %s <- %s' % (_d(out), _d(in_))
        if en == "act":
            self.add(en, lambda e: e.activation(out=out, in_=in_, func=AF.Identity), reads, writes)
        else:
            self.add(en, lambda e: e.tensor_copy(out=out, in_=in_), reads, writes)

    def memset(self, en, ap, val, writes):
        self.add(en, lambda e: e.memset(ap, val), (), writes)


def _d(ap):
    return '%s@%s%s' % (ap.tensor.name, ap.offset, list(ap.ap))


def dap(t, off, dims):
    return bass.AP(t, off if not isinstance(off, (int, np.integer)) else int(off), [[int(s), int(c)] for s, c in dims])


def build(cfg):
    nc = bass.Bass("TRN2", target_bir_lowering=False)
    L, SEQ, PAST, NT = cfg.L, cfg.SEQ, cfg.PAST, cfg.NT
    NPB = PAST // 512

    def din(name, shape, dt=F32):
        return nc.dram_tensor(name, list(shape), dt, kind="ExternalInput")

    def dout(name, shape, dt=F32):
        return nc.dram_tensor(name, list(shape), dt, kind="ExternalOutput")

    def dscr(name, shape, dt):
        return nc.dram_tensor(name, list(shape), dt, kind="Internal")

    I = {}
    for name, shape in [
        ("xp", (SEQ, D)), ("xs", (64, D)),
        ("w_in", (L, D, INW)), ("w_gu", (L, D, 2 * FF)), ("w_dn", (L, FF, D)),
        ("w_ba", (L, 256, D)), ("w_bb", (L, 256, D)), ("w_bc", (L, 512, D)), ("w_out", (L, D, D)),
        ("w_glu", (L, 256, 512)),
        ("nm_pk", (128, L, 8)), ("nf_pk", (128, L, 8)), ("gfin_bc", (128, D)),
        ("a_re_pj", (128, L, 8)), ("a_im_pj", (128, L, 8)), ("ldt_pj", (128, L, 8)),
        ("a_re_row", (L, 1024)), ("a_im_row", (L, 1024)), ("ldt_row", (L, 1024)),
        ("bblk_re", (L, 128, 1024)), ("bblk_im", (L, 128, 1024)),
        ("cblk_re", (L, 128, 1024)), ("cblk_im", (L, 128, 1024)),
        ("d_pm", (128, L, 2)), ("s0_re", (128, L, 2, 8)), ("s0_im", (128, L, 2, 8)),
        ("sgn_bc", (128, L, 256)), ("swT", (L, 128, 4, 128)), ("swTs", (L, 64, 4, 64)),
        ("sb_bc", (128, L, 2, 128)), ("sb_bcs", (128, L, 2, 64)),
    ]:
        I[name] = din(name, shape)
    for b_ in range(2):
        I["ckT%d" % b_] = din("ckT%d" % b_, (L, 128, 4 * PAST))
        for ci in range((PAST * 4) // 2048):
            I["cv%d_%d" % (b_, ci)] = din("cv%d_%d" % (b_, ci), (L, 128, 2048))
    O = {}
    for name, shape in [
        ("yp", (SEQ, D)), ("ys", (64, D)),
        ("pre", (L, 128, 8)), ("pim", (L, 128, 8)),
        ("pk", (L, NH, SEQ, 64)), ("pv", (L, NH, SEQ, 64)),
        ("sre", (L, 2, 128, 8)), ("sim", (L, 2, 128, 8)),
        ("sk", (L, 2, NH, 32, 64)), ("sv", (L, 2, NH, 32, 64)),
        ("svb", (L, 64, 256)),
    ]:
        O[name] = dout(name, shape)
    S = {}
    NCH = (PAST * 4) // 2048
    recs = [("inA", 8 * 512), ("inB", 8 * 256), ("inQ", 8 * 512), ("inK", 8 * 512), ("inV", 8 * 512)]
    recs += [("mg%d" % c, 4096) for c in range(8)] + [("wo%d" % hh, 4096) for hh in range(2)]
    recs += [("gu%d" % pc, 4096) for pc in range(11)]
    recs += [("dn%d_%d" % (hh, fg), nf * 512) for hh in range(2) for fg, nf in enumerate((8, 8, 6))]
    for name, R in recs:
        S[name] = dscr("r_" + name, (L, 128, R), BF16)
    for name, shape, dt in [
        ("wglu_bf", (L, 256, 512), BF16),
        ("s5tab", (L, 128, 3, 1024), F32), ("s5mat", (L, 128, 4, 1024), BF16),
        ("swT_bf", (L, 128, 512), BF16), ("swTs_bf", (L, 64, 256), BF16),
        ("kT_hist", (4, 128, SEQ), BF16), ("v_hist", (SEQ, 512), BF16),
        ("rpj_d", (L, 128, 8), F32), ("hbuf", (SEQ + 64, D), F32),
        ("pk_s", (NH, SEQ, 64), F32), ("pv_s", (NH, SEQ, 64), F32),
        ("sk_s", (2, NH, 32, 64), F32), ("sv_s", (2, NH, 32, 64), F32), ("svb_s", (64, 256), F32),
        ("sre_s", (2, 128, 8), F32), ("sim_s", (2, 128, 8), F32), ("pre_s", (128, 8), F32), ("pim_s", (128, 8), F32),
    ]:
        S[name] = dscr(name, shape, dt)
    es = ExitStack()
    with es:
        P = Prog(nc, es)

        def sb(name, shape, dt):
            return es.enter_context(nc.sbuf_tensor("sb_" + name, list(shape), dt))

        zp_t = [es.enter_context(nc.psum_tensor("zpair%d" % i, [128, 1024], F32)) for i in range(2)]
        zpair = [zp_t[i][:, :] for i in range(2)]
        banks = [zpair[i // 2][:, (i % 2) * 512:(i % 2) * 512 + 512] for i in range(4)]
        banks += [es.enter_context(nc.psum_tensor("bank%d" % i, [128, 512], F32))[:, :] for i in range(4, 8)]
        bankB = [Buf("bank%d" % i, excl=True) for i in range(8)]
        banksbf = [banks[i].bitcast(BF16) for i in range(8)]
        bank_rr = [0]

        def nextbank(pool=(0, 1, 2, 3, 4, 5, 6, 7)):
            i = pool[bank_rr[0] % len(pool)]
            bank_rr[0] += 1
            return i

        h = sb("h", [128, 4, D], F32)
        hB = [Buf("h%d" % s) for s in range(4)]
        xnT = sb("xnT", [128, 8, 512], BF16)
        xnTB = [Buf("xnT%d" % s) for s in range(4)]
        uaT = sb("uaT", [128, 2, 512], F32)
        uaTB = Buf("uaT")
        uabf = sb("uabf", [128, 2, 512], BF16)
        uabfB = Buf("uabf")
        ubT = sb("ubT", [128, 2, 512], F32)
        ubTB = Buf("ubT")
        qTz = sb("qTz", [128, 4, 2, 512], BF16)
        qTzB = [Buf("qTz%d" % p) for p in range(4)]
        kT = sb("kT", [128, 4, 512], BF16)
        kTB = Buf("kT")
        vtok = sb("vtok", [128, 4, 512], BF16)
        vtokB = [Buf("vtok%d" % s) for s in range(4)]
        kout = sb("kout", [128, 512], F32)
        koutB = Buf("kout")
        vout = sb("vout", [128, 512], F32)
        voutB = Buf("vout")
        vnbf = sb("vnbf", [128, 2, 256], BF16)
        vnbfB = [Buf("vnbf0"), Buf("vnbf1")]
        vnf = sb("vnf", [128, 256], F32)
        vnfB = Buf("vnf")
        vn0 = sb("vn0", [128, 256], F32)
        vn0B = Buf("vn0")
        small = sb("small", [128, 64], F32)
        smallB = [Buf("small%d" % i) for i in range(64)]
        yaT = sb("yaT", [128, 2, 512], BF16)
        yaTB = Buf("yaT")
        ybT = sb("ybT", [128, 2, 512], BF16)
        ybTB = Buf("ybT")
        ycT = sb("ycT", [128, 4, 512], BF16)
        ycTB = Buf("ycT")
        mgT = sb("mgT", [128, 8, 512], BF16)
        mgTB = [Buf("mgT%d" % c) for c in range(8)]
        s5tab = sb("s5tab", [128, 3, 1024], F32)
        s5tabB = Buf("s5tab")
        s5mat = sb("s5mat", [128, 4, 1024], BF16)
        s5matB = Buf("s5mat")
        swTsb = sb("swTsb", [128, 512], BF16)
        swTsbB = Buf("swTsb")
        swTssb = sb("swTssb", [64, 256], BF16)
        swTssbB = Buf("swTssb")
        r_pj = sb("r_pj", [128, L, 8], F32)
        r_pjB = Buf("r_pj")
        d_l = sb("d_l", [128, 2], F32)
        s0re_l = sb("s0re_l", [128, 2, 8], F32)
        s0im_l = sb("s0im_l", [128, 2, 8], F32)
        sgn_l = sb("sgn_l", [128, 256], F32)
        sbbc_l = sb("sbbc_l", [128, 2, 128], F32)
        sbbcs_l = sb("sbbcs_l", [128, 2, 64], F32)
        r_l = sb("r_l", [128, 8], F32)
        layB = Buf("laycon")
        gfin = sb("gfin", [128, D], F32)
        constB = Buf("const")
        sprev = sb("sprev", [128, 2, 8], F32)
        sprevB = Buf("sprev")
        ssm_s = sb("ssm_s", [128, 2, 2, 8], F32)
        ssm_sB = Buf("ssm_s")
        nbias = sb("nbias", [128, 32, 2], F32)
        nbiasB = [[Buf("nb%d_%d" % (i, k)) for k in range(2)] for i in range(32)]
        tot = sb("tot", [128, 4], F32)
        totB = [Buf("tot%d" % i) for i in range(4)]
        identb = sb("identb", [128, 128], BF16)
        identf = sb("identf", [128, 128], F32)
        ones = sb("ones", [128, 512], F32)
        zerob = sb("zerob", [128, 512], BF16)
        zcol = sb("zcol", [128, 1], F32)
        Mfull = sb("Mfull", [128, 512], F32)
        Ms = sb("Ms", [32, 2, 64], F32)
        tau1 = sb("tau1", [128, 128], F32)
        tmask = sb("tmask", [128, 128], F32)
        trimask = sb("trimask", [128, 4, 128], F32)
        ckbuf = sb("ckbuf", [128, 4 * PAST], BF16)
        ckB = Buf("ckbuf")
        cvbuf = sb("cvbuf", [128, 4 * PAST], BF16)
        cvB = Buf("cvbuf")
        cstg = sb("cstg", [128, 2048], F32)
        cstgB = Buf("cstg")
        NW = 4
        wring = sb("wring", [128, NW, 4096], BF16)
        wringB = [Buf("wring%d" % i) for i in range(NW)]
        SCRW = 8208
        scr = sb("scr", [128, SCRW], F32)
        scr_grp = []
        sview_cache = {}

        class View:
            pass

        def sview(name, off_words, nwords, dt=F32):
            key = (name, off_words, nwords, dt == BF16)
            if key in sview_cache:
                return sview_cache[key]
            b = Buf(name, scr_grp, off_words, off_words + nwords)
            ap = scr[:, off_words:off_words + nwords]
            if dt == BF16:
                ap = ap.bitcast(BF16)
            sview_cache[key] = (ap, b)
            return ap, b

        def chk(n):
            if cfg.stop <= n:
                P.stopped = True

        P.memset("pool", identf[:], 0.0, [constB])
        P.add("pool", lambda e: e.affine_select(out=identf[:], in_=identf[:], pattern=[[-1, 128]], compare_op=ALU.not_equal,
                                                fill=1.0, base=0, channel_multiplier=1), [], [constB])
        P.cp("pool", identb[:], identf[:], [constB], [constB])
        P.memset("pool", ones[:], 1.0, [constB])
        P.memset("pool", zerob[:], 0.0, [constB])
        P.memset("pool", zcol[:], 0.0, [constB])
        P.memset("pool", Mfull[:], 0.0, [constB])
        P.add("pool", lambda e: e.affine_select(out=Mfull[:, 384:512], in_=Mfull[:, 384:512], pattern=[[-1, 128]],
                                                compare_op=ALU.is_gt, fill=NEG, base=0, channel_multiplier=1), [], [constB])
        P.memset("pool", Ms[:], NEG, [constB])
        for b in range(2):
            P.memset("pool", Ms[:, b, 32 * b:32 * b + 32], 0.0, [constB])
            P.add("pool", lambda e, b=b: e.affine_select(out=Ms[:, b, 32 * b:32 * b + 32], in_=Ms[:, b, 32 * b:32 * b + 32],
                                                         pattern=[[-1, 32]], compare_op=ALU.is_gt, fill=NEG, base=0,
                                                         channel_multiplier=1), [], [constB])
        P.add("pool", lambda e: e.iota(tau1[:], [[1, 128]], base=1, channel_multiplier=0, allow_small_or_imprecise_dtypes=True), [], [constB])
        P.memset("pool", tmask[:], 1.0, [constB])
        P.memset("pool", tmask[:, 0:1], 0.0, [constB])
        P.memset("pool", trimask[:], 1.0, [constB])
        for g in range(4):
            P.add("pool", lambda e, g=g: e.affine_select(out=trimask[:, g, :], in_=trimask[:, g, :], pattern=[[1, 128]],
                                                         compare_op=ALU.is_ge, fill=0.0, base=0, channel_multiplier=-1), [], [constB])
        P.memset("pool", qTz[:], 0.0, qTzB)
        P.memset("pool", nbias[:], 0.0, [b for bb in nbiasB for b in bb])
        P.dma("sp", gfin[:], I["gfin_bc"].ap(), [], [constB])

        chk(0)
        wscr = [[] for _ in range(L)]
        cur_l = [0]
        nmt, nmB = sview("nmt", 6400, 64)
        nft, nfB = sview("nft", 6464, 64)
        P.dma("sp", nmt[:, 0:L * 8], dap(I["nm_pk"], 0, [(L * 8, 128), (1, L * 8)]), [], [nmB])
        P.dma("sp", nft[:, 0:L * 8], dap(I["nf_pk"], 0, [(L * 8, 128), (1, L * 8)]), [], [nfB])
        NST = 2
        stg = [sview("stg%d" % i, i * 3072, 2048) for i in range(NST)]
        stb = [sview("stb%d" % i, i * 3072 + 2048, 1024, BF16) for i in range(NST)]
        conv_i = [0]

        def shaped(ap, dims):
            if len(dims) == 1:
                return ap
            if len(dims) == 2:
                return ap.rearrange("p (a b) -> p a b", b=dims[1])
            return ap.rearrange("p (a b c) -> p a b c", b=dims[1], c=dims[2])

        def conv(src, dst, dims, scale=None, mul=None, mask=None, rows=128):
            n = int(np.prod(dims))
            i = conv_i[0] % NST
            conv_i[0] += 1
            sa, sB = stg[i]
            ba, bB = stb[i]
            sv_ = shaped(sa[0:rows, 0:n], dims)
            bv_ = shaped(ba[0:rows, 0:n], dims)
            P.dma("sp", sv_, src, [], [sB])
            en = "dve" if (conv_i[0] % 2 == 0) else "pool"
            if scale is not None:
                sc_ap, scB = scale
                if en == "pool":
                    P.ts("pool", bv_, sv_, sc_ap, ALU.mult, [sB, scB], [bB], s2=0.0, op1=ALU.add)
                else:
                    P.ts("dve", bv_, sv_, sc_ap, ALU.mult, [sB, scB], [bB])
            elif mul is not None:
                P.ts("dve", bv_, sv_, float(mul), ALU.mult, [sB], [bB])
            elif mask is not None:
                P.tt("dve", bv_, sv_, mask, ALU.mult, [sB, constB], [bB])
            else:
                P.cp(en, bv_, sv_, [sB], [bB])
            wb_ = Buf("wscr")
            wscr[cur_l[0]].append(wb_)
            if isinstance(dst, list):
                for d_ap, sel in dst:
                    P.dma("act", d_ap, sel(bv_), [bB], [wb_])
            else:
                P.dma("act", dst, bv_, [bB], [wb_])

        for l in range(L):
            cur_l[0] = l
            win, wgu = I["w_in"], I["w_gu"]

            def rdst(name, R, off, dims):
                return dap(S[name], (l * 128) * R + off, [(R, 128)] + dims)
            for kc in range(KC):
                sc_m = (nmt[:, l * 8 + kc:l * 8 + kc + 1], nmB)
                sc_f = (nft[:, l * 8 + kc:l * 8 + kc + 1], nfB)
                ro = (l * D + kc * 128) * INW
                for nm, c0, ncol in (("inA", 0, 512), ("inB", 512, 256), ("inQ", 768, 512), ("inK", 1280, 512), ("inV", 1792, 512)):
                    conv(dap(win, ro + c0, [(INW, 128), (1, ncol)]), rdst(nm, 8 * ncol, kc * ncol, [(1, ncol)]), [ncol], scale=sc_m)
                for i in range(3):
                    conv(dap(win, ro + 2304 + i * 1024, [(INW, 128), (128, 8), (1, 128)]),
                         [(rdst("mg%d" % c, 4096, kc * 512 + i * 128, [(1, 128)]), (lambda v, c=c: v[:, c, :])) for c in range(8)],
                         [8, 128], scale=sc_m)
                rg = (l * D + kc * 128) * (2 * FF)
                for up in range(2):
                    for f0, nf in ((0, 16), (16, 6)):
                        conv(dap(wgu, rg + up * FF + f0 * 128, [(2 * FF, 128), (256, nf // 2), (128, 2), (1, 128)]),
                             [(rdst("gu%d" % (f0 // 2 + q), 4096, kc * 512 + up * 256, [(128, 2), (1, 128)]), (lambda v, q=q: v[:, q, :, :])) for q in range(nf // 2)],
                             [nf // 2, 2, 128], scale=sc_f)
            for kk in range(8):
                if kk < 2:
                    src_t, r0 = I["w_ba"], (l * 256 + kk * 128)
                elif kk < 4:
                    src_t, r0 = I["w_bb"], (l * 256 + (kk - 2) * 128)
                else:
                    src_t, r0 = I["w_bc"], (l * 512 + (kk - 4) * 128)
                conv(dap(src_t, r0 * D, [(D, 128), (128, 8), (1, 128)]),
                     [(rdst("mg%d" % c, 4096, kk * 512 + 384, [(1, 128)]), (lambda v, c=c: v[:, c, :])) for c in range(8)],
                     [8, 128])
            for kc in range(KC):
                conv(dap(I["w_out"], (l * D + kc * 128) * D, [(D, 128), (1, D)]),
                     [(rdst("wo%d" % hh, 4096, kc * 512, [(1, 512)]), (lambda v, hh=hh: v[:, hh * 512:(hh + 1) * 512])) for hh in range(2)], [D])
            for f in range(0, NF, 2):
                dsts = []
                for q in range(2):
                    fq = f + q
                    fg = 0 if fq < 8 else (1 if fq < 16 else 2)
                    f0, nf = ((0, 8), (8, 8), (16, 6))[fg]
                    for hh in range(2):
                        dsts.append((rdst("dn%d_%d" % (hh, fg), nf * 512, (fq - f0) * 512, [(1, 512)]), (lambda v, q=q, hh=hh: v[:, q, hh * 512:(hh + 1) * 512])))
                conv(dap(I["w_dn"], (l * FF + f * 128) * D, [(D, 128), (128 * D, 2), (1, D)]), dsts, [2, D])
            conv(dap(I["w_glu"], l * 256 * 512, [(512, 128), (128 * 512, 2), (1, 512)]),
                 dap(S["wglu_bf"], l * 256 * 512, [(512, 128), (128 * 512, 2), (1, 512)]), [2, 512])
            conv(dap(I["swT"], l * 128 * 512, [(512, 128), (1, 512)]),
                 dap(S["swT_bf"], l * 128 * 512, [(512, 128), (1, 512)]), [512],
                 mask=trimask[:].rearrange("p g t -> p (g t)"))
            conv(dap(I["swTs"], l * 64 * 256, [(256, 64), (64, 4), (1, 64)]),
                 dap(S["swTs_bf"], l * 64 * 256, [(256, 64), (64, 4), (1, 64)]), [4, 64],
                 mask=trimask[0:64, :, 0:64], rows=64)
            conv(dap(I["cblk_re"], l * 128 * 1024, [(1024, 128), (1, 1024)]),
                 dap(S["s5mat"], (l * 128) * 4096 + 2 * 1024, [(4096, 128), (1, 1024)]), [1024])
            conv(dap(I["cblk_im"], l * 128 * 1024, [(1024, 128), (1, 1024)]),
                 dap(S["s5mat"], (l * 128) * 4096 + 3 * 1024, [(4096, 128), (1, 1024)]), [1024], mul=-1.0)

        chk(1)
        def s5_common(pref, n, off, a_re, a_im, ldt, srcB):
            T = {}
            for k_, nm in enumerate(["dt", "mag", "ang", "y", "yi", "fr", "sn", "cs"]):
                T[nm] = sview(pref + nm, off + k_ * n, n)
            yi_ap = T["yi"][0].bitcast(I32)
            P.act(T["dt"][0], ldt, AF.Exp, [srcB], [T["dt"][1]])
            P.tt("dve", T["mag"][0], a_re, T["dt"][0], ALU.mult, [srcB, T["dt"][1]], [T["mag"][1]])
            P.act(T["mag"][0], T["mag"][0], AF.Exp, [T["mag"][1]], [T["mag"][1]])
            P.tt("dve", T["ang"][0], a_im, T["dt"][0], ALU.mult, [srcB, T["dt"][1]], [T["ang"][1]])

            def frac(dst, src, add):
                P.ts("dve", T["y"][0], src[0], 1.0 / TWO_PI, ALU.mult, [src[1]], [T["y"][1]], s2=add, op1=ALU.add)
                P.cp("dve", yi_ap, T["y"][0], [T["y"][1]], [T["yi"][1]])
                P.cp("dve", dst[0], yi_ap, [T["yi"][1]], [dst[1]])
                P.tt("dve", dst[0], T["y"][0], dst[0], ALU.subtract, [T["y"][1], dst[1]], [dst[1]])
                P.ts("dve", dst[0], dst[0], 0.5, ALU.min, [dst[1]], [dst[1]], s2=-0.5, op1=ALU.max)
            T["frac"] = frac
            return T

        for l in range(L):
            pj_src, pjB = sview("pjsrc", 6600, 3 * 8)
            P.dma("sp", pj_src[:, 0:8], dap(I["a_re_pj"], l * 8, [(L * 8, 128), (1, 8)]), [], [pjB])
            P.dma("sp", pj_src[:, 8:16], dap(I["a_im_pj"], l * 8, [(L * 8, 128), (1, 8)]), [], [pjB])
            P.dma("sp", pj_src[:, 16:24], dap(I["ldt_pj"], l * 8, [(L * 8, 128), (1, 8)]), [], [pjB])
            Tp = s5_common("pj_", 8, 6700, pj_src[:, 0:8], pj_src[:, 8:16], pj_src[:, 16:24], pjB)
            P.cp("dve", r_pj[:, l, :], Tp["mag"][0], [Tp["mag"][1]], [r_pjB])
            wb_ = Buf("wscr_rpj")
            wscr[l].append(wb_)
            P.dma("act", dap(S["rpj_d"], l * 1024, [(8, 128), (1, 8)]), r_pj[:, l, :], [r_pjB], [wb_])
            Tp["frac"](Tp["fr"], Tp["ang"], 0.0)
            ph, phB = sview("ph", 0, 1024)
            tb, tbB = sview("tb", 1024, 3072)
            yy, yyB = sview("yy", 4096, 1024)
            yyi, yyiB = sview("yyi", 5120, 1024)
            ph3 = ph.rearrange("p (j t) -> p j t", t=128)
            for j in range(8):
                P.ts("dve", ph3[:, j, :], tau1[:], Tp["fr"][0][:, j:j + 1], ALU.mult, [constB, Tp["fr"][1]], [phB])
                P.ts("dve", tb[:, 2048 + j * 128:2048 + (j + 1) * 128], tmask[:], r_pj[:, l, j:j + 1], ALU.mult,
                     [constB, r_pjB], [tbB])
            for which, add in ((1, 0.0), (0, 0.25)):
                dst = tb[:, which * 1024:(which + 1) * 1024]
                P.ts("dve", yy, ph, 1.0, ALU.mult, [phB], [yyB], s2=add, op1=ALU.add)
                P.cp("dve", yyi.bitcast(I32), yy, [yyB], [yyiB])
                P.cp("dve", dst, yyi.bitcast(I32), [yyiB], [tbB])
                P.tt("dve", dst, yy, dst, ALU.subtract, [yyB, tbB], [tbB])
                P.ts("dve", dst, dst, 0.5, ALU.min, [tbB], [tbB], s2=-0.5, op1=ALU.max)
                P.act(dst, dst, AF.Sin, [tbB], [tbB], scale=TWO_PI)
            wb_ = Buf("wscr_tab")
            wscr[l].append(wb_)
            P.dma("act", dap(S["s5tab"], l * 128 * 3072, [(3072, 128), (1, 3072)]), tb, [tbB], [wb_])
            rw, rwB = sview("rw", 0, 3072)
            P.dma("sp", rw[:, 0:1024], dap(I["a_re_row"], l * 1024, [(0, 128), (1, 1024)]), [], [rwB])
            P.dma("sp", rw[:, 1024:2048], dap(I["a_im_row"], l * 1024, [(0, 128), (1, 1024)]), [], [rwB])
            P.dma("sp", rw[:, 2048:3072], dap(I["ldt_row"], l * 1024, [(0, 128), (1, 1024)]), [], [rwB])
            a_re, a_im = rw[:, 0:1024], rw[:, 1024:2048]
            t0, t0B = sview("rt0", 3072, 1024)
            t1, t1B = sview("rt1", 4096, 1024)
            t2, t2B = sview("rt2", 5120, 1024)
            t3, t3B = sview("rt3", 6144, 1024)
            dt_ = rw[:, 2048:3072]
            P.act(dt_, dt_, AF.Exp, [rwB], [rwB])
            P.tt("dve", t1, a_re, dt_, ALU.mult, [rwB], [t1B])
            P.act(t1, t1, AF.Exp, [t1B], [t1B])
            P.tt("dve", t2, a_im, dt_, ALU.mult, [rwB], [t2B])

            def fracrow(dst, dstB, add):
                P.ts("dve", t0, t2, 1.0 / TWO_PI, ALU.mult, [t2B], [t0B], s2=add, op1=ALU.add)
                P.cp("dve", dst.bitcast(I32), t0, [t0B], [dstB])
                P.cp("dve", t3, dst.bitcast(I32), [dstB], [t3B])
                P.tt("dve", dst, t0, t3, ALU.subtract, [t0B, t3B], [dstB])
                P.ts("dve", dst, dst, 0.5, ALU.min, [dstB], [dstB], s2=-0.5, op1=ALU.max)
                P.act(dst, dst, AF.Sin, [dstB], [dstB], scale=TWO_PI)
            fracrow(dt_, rwB, 0.0)
            cs_, csB = sview("rcs", 6144, 1024)
            P.ts("dve", t0, t2, 1.0 / TWO_PI, ALU.mult, [t2B], [t0B], s2=0.25, op1=ALU.add)
            P.cp("dve", cs_.bitcast(I32), t0, [t0B], [csB])
            P.cp("dve", t2, cs_.bitcast(I32), [csB], [t2B])
            P.tt("dve", cs_, t0, t2, ALU.subtract, [t0B, t2B], [csB])
            P.ts("dve", cs_, cs_, 0.5, ALU.min, [csB], [csB], s2=-0.5, op1=ALU.max)
            P.act(cs_, cs_, AF.Sin, [csB], [csB], scale=TWO_PI)
            P.tt("dve", cs_, cs_, t1, ALU.mult, [csB, t1B], [csB])
            P.tt("dve", dt_, dt_, t1, ALU.mult, [rwB, t1B], [rwB])
            P.ts("dve", cs_, cs_, -1.0, ALU.add, [csB], [csB])
            P.tt("dve", t1, a_re, a_re, ALU.mult, [rwB], [t1B])
            P.tt("dve", t0, a_im, a_im, ALU.mult, [rwB], [t0B])
            P.tt("dve", t1, t1, t0, ALU.add, [t1B, t0B], [t1B])
            P.add("dve", lambda e, t1=t1: e.reciprocal(out=t1, in_=t1), [t1B], [t1B])
            P.tt("dve", t0, cs_, a_re, ALU.mult, [csB, rwB], [t0B])
            P.tt("dve", t2, dt_, a_im, ALU.mult, [rwB], [t2B])
            P.tt("dve", t0, t0, t2, ALU.add, [t0B, t2B], [t0B])
            P.tt("dve", t0, t0, t1, ALU.mult, [t0B, t1B], [t0B])
            P.tt("dve", t2, dt_, a_re, ALU.mult, [rwB], [t2B])
            P.tt("dve", cs_, cs_, a_im, ALU.mult, [csB, rwB], [csB])
            P.tt("dve", t2, t2, cs_, ALU.subtract, [t2B, csB], [t2B])
            P.tt("dve", t2, t2, t1, ALU.mult, [t2B, t1B], [t2B])
            P.dma("sp", rw[:, 0:1024], dap(I["bblk_re"], l * 128 * 1024, [(1024, 128), (1, 1024)]), [], [rwB])
            P.dma("sp", rw[:, 1024:2048], dap(I["bblk_im"], l * 128 * 1024, [(1024, 128), (1, 1024)]), [], [rwB])
            b_re, b_im = rw[:, 0:1024], rw[:, 1024:2048]
            bo, boB = sview("rbo", 2048, 1024, BF16)
            P.tt("dve", t1, t0, b_re, ALU.mult, [t0B, rwB], [t1B])
            P.tt("dve", cs_, t2, b_im, ALU.mult, [t2B, rwB], [csB])
            P.tt("dve", bo[:, 0:1024], t1, cs_, ALU.subtract, [t1B, csB], [boB])
            P.tt("dve", t1, t0, b_im, ALU.mult, [t0B, rwB], [t1B])
            P.tt("dve", cs_, t2, b_re, ALU.mult, [t2B, rwB], [csB])
            P.tt("dve", bo[:, 1024:2048], t1, cs_, ALU.add, [t1B, csB], [boB])
            wb_ = Buf("wscr_mat")
            wscr[l].append(wb_)
            P.dma("act", dap(S["s5mat"], (l * 128) * 4096, [(4096, 128), (1, 2048)]), bo, [boB], [wb_])
        chk(2)
        khB = [Buf("kh%d" % t) for t in range(NT)]
        vhB = [Buf("vh%d" % t) for t in range(NT)]
        outB = []

        def obuf(name):
            b = Buf(name)
            outB.append(b)
            return b

        class Item:
            pass

        class Stream:
            def __init__(self):
                self.items = []
                self.ip = 0
                self.cp_ = 0
                self.ring_issued = 0
                self.ring_released = 0

            def push(self, name, ring, dmas, q="sp", dstB=None):
                it = Item()
                it.name, it.ring, it.dmas, it.q, it.dstB, it.slot = name, ring, dmas, q, dstB, None
                self.items.append(it)

            def pump(self):
                while self.ip < len(self.items):
                    it = self.items[self.ip]
                    if it.ring:
                        if self.ring_issued - self.ring_released >= NW:
                            break
                        it.slot = self.ring_issued % NW
                        self.ring_issued += 1
                        for dst, src, rB in it.dmas(it.slot):
                            P.dma(it.q, dst, src, rB, [wringB[it.slot]])
                    else:
                        for dst, src, rB in it.dmas(None):
                            P.dma(it.q, dst, src, rB, [it.dstB])
                    self.ip += 1

            def get(self, name):
                it = self.items[self.cp_]
                assert it.name == name, (it.name, name)
                if self.ip <= self.cp_:
                    self.pump()
                assert self.ip > self.cp_, "stream stalled at " + name
                self.cp_ += 1
                return it

            def release(self, it):
                if it.ring:
                    self.ring_released += 1
                self.pump()

        ST_ = Stream()

        def wslot(slot, dims):
            n = int(np.prod(dims))
            return shaped(wring[:, slot, 0:n], dims)

        def push_layer_items(kind, t):
            W = []
            pre = "%s%d_" % (kind, t)
            ST_.push(pre + "s5tab", False, lambda s: [(s5tab[:].rearrange("p a b -> p (a b)"),
                                                      (lambda li: dap(S["s5tab"], li * (128 * 3072), [(3072, 128), (1, 3072)])), W)], dstB=s5tabB)
            ST_.push(pre + "s5mat", False, lambda s: [(s5mat[:].rearrange("p a b -> p (a b)"),
                                                      (lambda li: dap(S["s5mat"], li * (128 * 4096), [(4096, 128), (1, 4096)])), W)], dstB=s5matB)
            if kind == "p":
                ST_.push(pre + "swT", False, lambda s: [(swTsb[:], (lambda li: dap(S["swT_bf"], li * (128 * 512), [(512, 128), (1, 512)])), W)], dstB=swTsbB)
            else:
                ST_.push(pre + "swT", False, lambda s: [(swTssb[:], (lambda li: dap(S["swTs_bf"], li * (64 * 256), [(256, 64), (1, 256)])), W)], dstB=swTssbB)
            for nm, c0, ncol in (("A", 0, 512), ("B", 512, 256), ("Q", 768, 512), ("K", 1280, 512), ("V", 1792, 512)):
                ST_.push(pre + "in" + nm, True, lambda s, nm=nm, ncol=ncol: [
                    (wslot(s, [8 * ncol]), (lambda li, nm=nm, ncol=ncol: dap(S["in" + nm], li * (128 * 8 * ncol), [(8 * ncol, 128), (1, 8 * ncol)])), W)])
            ST_.push(pre + "glu", True, lambda s: [
                (wslot(s, [2, 512]), (lambda li: dap(S["wglu_bf"], li * (256 * 512), [(512, 128), (128 * 512, 2), (1, 512)])), W)])
            if kind == "p":
                for hf, kt in [(hf_, kt_) for hf_ in range(2) for kt_ in range(t - 1, -1, -1)]:
                    ST_.push(pre + "kv%d_%d" % (hf, kt), True, lambda s, kt=kt: [
                        (shaped(wring[:, s, 0:2048], [4, 512]), dap(S["kT_hist"], kt * 512, [(SEQ, 128), (128 * SEQ, 4), (1, 512)]), [khB[kt]]),
                        (shaped(wring[:, s, 2048:4096], [4, 512]), dap(S["v_hist"], (kt * 512) * 512, [(512, 128), (128 * 512, 4), (1, 512)]), [vhB[kt]])])
            for c in range(8):
                ST_.push(pre + "mg%d" % c, True, lambda s, c=c: [
                    (wslot(s, [4096]), (lambda li, c=c: dap(S["mg%d" % c], li * (128 * 4096), [(4096, 128), (1, 4096)])), W)])
            for hh in range(2):
                ST_.push(pre + "wo%d" % hh, True, lambda s, hh=hh: [
                    (wslot(s, [4096]), (lambda li, hh=hh: dap(S["wo%d" % hh], li * (128 * 4096), [(4096, 128), (1, 4096)])), W)])
            for pc in range(11):
                ST_.push(pre + "gu%d" % pc, True, lambda s, pc=pc: [
                    (wslot(s, [4096]), (lambda li, pc=pc: dap(S["gu%d" % pc], li * (128 * 4096), [(4096, 128), (1, 4096)])), W)])
            for hh in range(2):
                for fg, (f0, nf) in enumerate(((0, 8), (8, 8), (16, 6))):
                    ST_.push(pre + "dn%d_%d" % (hh, fg), True, lambda s, hh=hh, fg=fg, nf=nf: [
                        (wslot(s, [nf * 512]), (lambda li, hh=hh, fg=fg, nf=nf: dap(S["dn%d_%d" % (hh, fg)], li * (128 * nf * 512), [(nf * 512, 128), (1, nf * 512)])), W)])

        tiles = [("p", t) for t in range(NT)] + ([("s", 0)] if cfg.with_sample else [])
        for kind, t in tiles:
            push_layer_items(kind, t)

        evq = [0]

        def evac_eng():
            evq[0] += 1
            return "act" if evq[0] % 2 == 0 else "dve"

        def norm_to_xnT(ST, nst):
            junk, junkB = sview("junk", 0, 512, BF16)
            for s in range(nst):
                xnv, xnB = sview("xn%d" % (s % 2), 512 + (s % 2) * 512, 512, BF16)
                c_ss, c_sq, c_rs = s, 4 + s, 8 + s
                P.act(junk[0:ST, :], h[0:ST, s, :], AF.Square, [hB[s]], [junkB, smallB[c_ss]], accum=small[0:ST, c_ss:c_ss + 1])
                P.act(small[0:ST, c_sq:c_sq + 1], small[0:ST, c_ss:c_ss + 1], AF.Sqrt, [smallB[c_ss]], [smallB[c_sq]], bias=EPS, scale=1.0 / D)
                P.add("dve", lambda e, o=small[0:ST, c_rs:c_rs + 1], i=small[0:ST, c_sq:c_sq + 1]: e.reciprocal(out=o, in_=i), [smallB[c_sq]], [smallB[c_rs]])
                P.ts("pool", xnv[0:ST, :], h[0:ST, s, :], small[0:ST, c_rs:c_rs + 1], ALU.mult, [hB[s], smallB[c_rs]], [xnB], s2=0.0, op1=ALU.add)
                bk = nextbank()
                bkb = banksbf[bk]
                for kc in range(8):
                    P.tr(bkb[:, kc * 128:kc * 128 + ST], xnv[0:ST, kc * 128:(kc + 1) * 128], identb[0:ST, 0:ST], [xnB, constB], [bankB[bk]])
                P.cp(evac_eng(), xnT[:, :, s * ST:(s + 1) * ST], bkb.rearrange("p (k t) -> p k t", t=128)[:, :, 0:ST], [bankB[bk]], [xnTB[s]])

        def s5_segment(l, col0, n, st_re, st_im, stB, wg, wgB):
            xre, xreB = sview("xre", 0, 1024)
            xim, ximB = sview("xim", 1024, 1024)
            wre, wreB = sview("wre", 2048, 1024)
            wim, wimB = sview("wim", 3072, 1024)
            tt_ = [sview("s5t%d" % k, 4096 + k * 512, 512) for k in range(4)]
            srb, srbB = sview("srb", 6144, 512, BF16)
            sib, sibB = sview("sib", 6656, 512, BF16)
            m = 8 * n

            def v3(ap, j0=0, nj=8):
                return ap[:, 0:m].rearrange("p (j t) -> p j t", t=n)[:, j0:j0 + nj, :]

            def tab3(k, j0, nj):
                return s5tab[:, k, :].rearrange("p (j t) -> p j t", t=128)[:, j0:j0 + nj, 0:n]
            bks = [nextbank() for _ in range(4)]
            for ri in range(2):
                for j in range(8):
                    bk = bks[ri * 2 + j // 4]
                    jj = j % 4
                    P.mm(banks[bk][:, jj * 128:jj * 128 + n], s5mat[:, ri, j * 128:(j + 1) * 128], uabf[:, j // 4, col0:col0 + n],
                         True, True, [s5matB, uabfB], [bankB[bk]])
            for hf in range(2):
                bre = banks[bks[hf]][:, :].rearrange("p (j t) -> p j t", t=128)[:, :, 0:n]
                bim = banks[bks[2 + hf]][:, :].rearrange("p (j t) -> p j t", t=128)[:, :, 0:n]
                breB, bimB = bankB[bks[hf]], bankB[bks[2 + hf]]
                Ec, Es = tab3(0, 4 * hf, 4), tab3(1, 4 * hf, 4)
                tv = [tt_[k][0][:, 0:4 * n].rearrange("p (j t) -> p j t", t=n) for k in range(4)]
                tB = [tt_[k][1] for k in range(4)]
                P.tt("dve", tv[0], bre, Ec, ALU.mult, [breB, s5tabB], [tB[0]])
                P.tt("dve", tv[1], bim, Es, ALU.mult, [bimB, s5tabB], [tB[1]])
                P.tt("pool", v3(xre, 4 * hf, 4), tv[0], tv[1], ALU.add, [tB[0], tB[1]], [xreB])
                P.tt("dve", tv[2], bim, Ec, ALU.mult, [bimB, s5tabB], [tB[2]])
                P.tt("dve", tv[3], bre, Es, ALU.mult, [breB, s5tabB], [tB[3]])
                P.tt("pool", v3(xim, 4 * hf, 4), tv[2], tv[3], ALU.subtract, [tB[2], tB[3]], [ximB])
            P.tt("dve", small[:, 16:24], r_l[:, :], st_re, ALU.mult, [layB, stB], [smallB[16]])
            P.tt("dve", v3(xre)[:, :, 0], v3(xre)[:, :, 0], small[:, 16:24], ALU.add, [xreB, smallB[16]], [xreB])
            P.tt("dve", small[:, 24:32], r_l[:, :], st_im, ALU.mult, [layB, stB], [smallB[24]])
            P.tt("dve", v3(xim)[:, :, 0], v3(xim)[:, :, 0], small[:, 24:32], ALU.add, [ximB, smallB[24]], [ximB])
            if n == 128:
                rt, rtR = s5tab[:, 2, :], [s5tabB]
            else:
                rt = tt_[0][0][:, 0:m]
                P.cp("pool", rt.rearrange("p (j t) -> p j t", t=n), tab3(2, 0, 8), [s5tabB], [tt_[0][1]])
                rtR = [tt_[0][1]]
            P.add("dve", lambda e, o=wre[:, 0:m], d0=rt, d1=xre[:, 0:m]: e.tensor_tensor_scan(out=o, data0=d0, data1=d1, initial=0.0, op0=ALU.mult, op1=ALU.add),
                  rtR + [xreB], [wreB])
            P.add("dve", lambda e, o=wim[:, 0:m], d0=rt, d1=xim[:, 0:m]: e.tensor_tensor_scan(out=o, data0=d0, data1=d1, initial=0.0, op0=ALU.mult, op1=ALU.add),
                  rtR + [ximB], [wimB])
            for hf in range(2):
                Ec, Es = tab3(0, 4 * hf, 4), tab3(1, 4 * hf, 4)
                tv = [tt_[k][0][:, 0:4 * n].rearrange("p (j t) -> p j t", t=n) for k in range(4)]
                tB = [tt_[k][1] for k in range(4)]
                wr, wi = v3(wre, 4 * hf, 4), v3(wim, 4 * hf, 4)
                P.tt("dve", tv[0], Ec, wr, ALU.mult, [s5tabB, wreB], [tB[0]])
                P.tt("pool", tv[1], Es, wi, ALU.mult, [s5tabB, wimB], [tB[1]])
                P.tt("dve", v3(srb, 4 * hf, 4), tv[0], tv[1], ALU.subtract, [tB[0], tB[1]], [srbB])
                P.tt("pool", tv[2], Es, wr, ALU.mult, [s5tabB, wreB], [tB[2]])
                P.tt("dve", tv[3], Ec, wi, ALU.mult, [s5tabB, wimB], [tB[3]])
                P.tt("pool", v3(sib, 4 * hf, 4), tv[2], tv[3], ALU.add, [tB[2], tB[3]], [sibB])
            EcL, EsL = tab3(0, 0, 8)[:, :, n - 1], tab3(1, 0, 8)[:, :, n - 1]
            wrL, wiL = v3(wre)[:, :, n - 1], v3(wim)[:, :, n - 1]
            P.tt("dve", small[:, 32:40], EcL, wrL, ALU.mult, [s5tabB, wreB], [smallB[32]])
            P.tt("dve", small[:, 40:48], EsL, wiL, ALU.mult, [s5tabB, wimB], [smallB[40]])
            P.tt("dve", st_re, small[:, 32:40], small[:, 40:48], ALU.subtract, [smallB[32], smallB[40]], [stB])
            P.tt("dve", small[:, 32:40], EsL, wrL, ALU.mult, [s5tabB, wreB], [smallB[32]])
            P.tt("dve", small[:, 40:48], EcL, wiL, ALU.mult, [s5tabB, wimB], [smallB[40]])
            P.tt("dve", st_im, small[:, 32:40], small[:, 40:48], ALU.add, [smallB[32], smallB[40]], [stB])
            bky = nextbank()
            for mm_ in range(2):
                for j in range(4 * mm_, 4 * mm_ + 4):
                    P.mm(banks[bky][:, mm_ * 128:mm_ * 128 + n], s5mat[:, 2, j * 128:(j + 1) * 128], v3(srb)[:, j, :],
                         j == 4 * mm_, False, [s5matB, srbB], [bankB[bky]])
                    P.mm(banks[bky][:, mm_ * 128:mm_ * 128 + n], s5mat[:, 3, j * 128:(j + 1) * 128], v3(sib)[:, j, :],
                         False, j == 4 * mm_ + 3, [s5matB, sibB], [bankB[bky]])
            yv, yvB = tt_[0]
            y2, y2B = tt_[1]
            sg, sgB = tt_[2]
            gl, glB = sview("glbf", 4096 + 3 * 512, 256, BF16)
            yv3 = yv[:, 0:2 * n].rearrange("p (a t) -> p a t", t=n)
            y23 = y2[:, 0:2 * n].rearrange("p (a t) -> p a t", t=n)
            sg3 = sg[:, 0:2 * n].rearrange("p (a t) -> p a t", t=n)
            gl3 = gl[:, 0:2 * n].rearrange("p (a t) -> p a t", t=n)
            for mm_ in range(2):
                P.stt(yv3[:, mm_, :], uaT[:, mm_, col0:col0 + n], d_l[:, mm_:mm_ + 1], banks[bky][:, mm_ * 128:mm_ * 128 + n],
                      ALU.mult, ALU.add, [uaTB, layB, bankB[bky]], [yvB])
            P.tt("dve", y23, yv3, yv3, ALU.mult, [yvB], [y2B])
            P.ts("dve", y23, y23, 0.044715, ALU.mult, [y2B], [y2B], s2=1.0, op1=ALU.add)
            P.tt("dve", y23, y23, yv3, ALU.mult, [y2B, yvB], [y2B])
            P.act(sg3, y23, AF.Sigmoid, [y2B], [sgB], scale=1.5957691216057308)
            P.tt("dve", gl3, yv3, sg3, ALU.mult, [yvB, sgB], [glB])
            bkz = nextbank()
            for oc in range(4):
                for kc in range(2):
                    P.mm(banks[bkz][:, oc * 128:oc * 128 + n], wg[:, kc, oc * 128:(oc + 1) * 128], gl3[:, kc, :], kc == 0, kc == 1,
                         [wgB, glB], [bankB[bkz]])
            z3 = banks[bkz][:, :].rearrange("p (a t) -> p a t", t=128)
            P.act(sg3, z3[:, 2:4, 0:n], AF.Sigmoid, [bankB[bkz]], [sgB])
            P.tt("dve", yaT[:, :, col0:col0 + n], z3[:, 0:2, 0:n], sg3, ALU.mult, [bankB[bkz], sgB], [yaTB])

        class Blk:
            pass

        def run_attention(blocks, Kmax):
            eV = [sview("at_e%d" % i, i * 1024, 1024) for i in range(2)]
            sp, spB = sview("at_sp", 2048, 1024)
            pf, pfB = sview("at_pf", 3072, 1040)
            arg, argB = sview("at_arg", 4112, 1024)
            pbV = [sview("at_pb%d" % i, 5136 + i * 512, 512, BF16) for i in range(2)]
            ptV = [sview("at_pt%d" % i, 6160 + i * 512, 512, BF16) for i in range(2)]
            zmVV = [sview("at_zm%d" % i, 7184 + i * 512, 512) for i in range(2)]
            pf3 = pf.rearrange("p (g t) -> p g t", t=520)
            P.memset("pool", pf3[:, :, 0:1], 0.0, [pfB])
            nB = len(blocks)

            def zbuf(i):
                k = i % 2
                return zpair[k], [bankB[2 * k], bankB[2 * k + 1]]

            def S1(i):
                b = blocks[i]
                zt, zB = zbuf(i)
                for g, sub in enumerate(b.subs):
                    P.mm(zt[0:b.nq, g * 512:g * 512 + b.w], sub[0], b.kT, True, True, b.qkB, zB)
                if b.mask is not None:
                    zmV = zmVV[i % 2]
                    P.tt("dve", zmV[0][0:b.nq, 0:b.w], zt[0:b.nq, 0:b.w], b.mask, ALU.add, zB + [constB], [zmV[1]])

            def S2(i):
                b = blocks[i]
                zt, zB = zbuf(i)
                e, eB = eV[i % 2]
                G = len(b.subs)
                if G == 2:
                    P.act(e[0:b.nq, 0:1024], zt[0:b.nq, 0:1024], AF.Exp, zB, [eB])
                    P.act(sp[0:b.nq, 0:1024], e[0:b.nq, 0:1024], AF.Ln, [eB], [spB], bias=1.0)
                    return
                if b.mask is not None:
                    zmV = zmVV[i % 2]
                    zsrc, zR = zmV[0][0:b.nq, 0:b.w], [zmV[1]]
                else:
                    zsrc, zR = zt[0:b.nq, 0:b.w], zB
                P.act(e[0:b.nq, 0:b.w], zsrc, AF.Exp, zR, [eB])
                ts_ = i % 4
                P.act(sp[0:b.nq, 0:b.w], e[0:b.nq, 0:b.w], AF.Ln, [eB], [spB, totB[ts_]], bias=1.0, accum=tot[0:b.nq, ts_:ts_ + 1])
                if b.first:
                    old, oldB = zcol[0:b.nq, 0:1], constB
                else:
                    old, oldB = nbias[0:b.nq, b.idx0, 1 - b.par:2 - b.par], nbiasB[b.idx0][1 - b.par]
                P.tt("dve", nbias[0:b.nq, b.idx0, b.par:b.par + 1], old, tot[0:b.nq, ts_:ts_ + 1], ALU.subtract,
                     [oldB, totB[ts_]], [nbiasB[b.idx0][b.par]])

            def S3(i):
                b = blocks[i]
                e, eB = eV[i % 2]
                pb, pbB = pbV[i % 2]
                G = len(b.subs)
                for g in range(G):
                    P.add("dve", lambda e_, o=pf3[0:b.nq, g, 1:b.w + 1], d0=ones[0:b.nq, 0:b.w], d1=sp[0:b.nq, g * 512:g * 512 + b.w]:
                          e_.tensor_tensor_scan(out=o, data0=d0, data1=d1, initial=0.0, op0=ALU.mult, op1=ALU.add), [constB, spB], [pfB])
                if G == 2:
                    nbw = [nbiasB[b.idx0][b.par], nbiasB[b.idx0 + 1][b.par]]
                    nbr = [nbiasB[b.idx0][1 - b.par], nbiasB[b.idx0 + 1][1 - b.par]]
                    P.tt("dve", nbias[0:b.nq, b.idx0:b.idx0 + 2, b.par], nbias[0:b.nq, b.idx0:b.idx0 + 2, 1 - b.par], pf3[0:b.nq, 0:2, 512],
                         ALU.subtract, nbr + [pfB], nbw)
                    for g in range(2):
                        P.ts("pool", arg[0:b.nq, g * 512:(g + 1) * 512], pf3[0:b.nq, g, 0:512], 1.0, ALU.mult, [pfB, nbw[g]], [argB],
                             s2=nbias[0:b.nq, b.idx0 + g, b.par:b.par + 1], op1=ALU.add)
                    P.act(arg[0:b.nq, 0:1024], arg[0:b.nq, 0:1024], AF.Exp, [argB], [argB])
                    P.tt("pool", pb[0:b.nq, 0:1024], e[0:b.nq, 0:1024], arg[0:b.nq, 0:1024], ALU.mult, [eB, argB], [pbB])
                else:
                    P.act(arg[0:b.nq, 0:b.w], pf3[0:b.nq, 0, 0:b.w], AF.Exp, [pfB, nbiasB[b.idx0][b.par]], [argB], bias=nbias[0:b.nq, b.idx0, b.par:b.par + 1])
                    P.tt("pool", pb[0:b.nq, 0:b.w], e[0:b.nq, 0:b.w], arg[0:b.nq, 0:b.w], ALU.mult, [eB, argB], [pbB])

            def S4(i):
                b = blocks[i]
                pb, pbB = pbV[i % 2]
                pt, ptB = ptV[i % 2]
                bk = 4 + (i % 2)
                bkb = banksbf[bk]
                G = len(b.subs)
                vs = b.vs
                for g in range(G):
                    for j, (vap, K) in enumerate(vs):
                        P.tr(bkb[0:K, g * 512 + j * 128:g * 512 + j * 128 + b.nq], pb[0:b.nq, g * 512 + j * 128:g * 512 + j * 128 + K],
                             identb[0:b.nq, 0:b.nq], [pbB, constB], [bankB[bk]])
                ncol = 1024 if G == 2 else len(vs) * 128
                P.cp("dve", pt[0:Kmax, 0:ncol], bkb[0:Kmax, 0:ncol], [bankB[bk]], [ptB])

            def S5(i):
                b = blocks[i]
                pt, ptB = ptV[i % 2]
                vs = b.vs
                nsub = len(vs)
                for g, sub in enumerate(b.subs):
                    for j, (vap, K) in enumerate(vs):
                        P.mm(sub[1], pt[0:K, g * 512 + j * 128:g * 512 + j * 128 + b.nq], vap, False, b.last and j == nsub - 1, [ptB] + b.vB, [sub[2]])
                if b.done is not None:
                    b.done()

            for it in range(nB + 4):
                if 0 <= it - 4 < nB:
                    S5(it - 4)
                if 0 <= it - 3 < nB:
                    S4(it - 3)
                if 0 <= it - 2 < nB:
                    S3(it - 2)
                if 0 <= it - 1 < nB:
                    S2(it - 1)
                if it < nB:
                    if blocks[it].pre is not None:
                        blocks[it].pre()
                    S1(it)

        def layer_tile(kind, t, l):
            prompt = kind == "p"
            T = 512 if prompt else 64
            ST = 128 if prompt else 64
            nst = T // ST
            pre = "%s%d_" % (kind, t)
            xall = xnTB[0:nst]
            norm_to_xnT(ST, nst)
            chk(3)
            it_tab = ST_.get(pre + "s5tab")
            it_mat = ST_.get(pre + "s5mat")
            it_sw = ST_.get(pre + "swT")
            chk(3.05)
            itA = ST_.get(pre + "inA")
            chk(3.07)
            wA = wslot(itA.slot, [8, 512])
            wAB = wringB[itA.slot]
            for ct in range(4):
                bk = nextbank()
                for kc in range(8):
                    P.mm(banks[bk][:, 0:T], wA[:, kc, ct * 128:(ct + 1) * 128], xnT[:, kc, 0:T], kc == 0, kc == 7, [wAB] + xall, [bankB[bk]])
                if ct < 2:
                    P.cp("act", uaT[:, ct, 0:T], banks[bk][:, 0:T], [bankB[bk]], [uaTB])
                    P.cp("dve", uabf[:, ct, 0:T], banks[bk][:, 0:T], [bankB[bk]], [uabfB])
                else:
                    P.cp(evac_eng(), ubT[:, ct - 2, 0:T], banks[bk][:, 0:T], [bankB[bk]], [ubTB])
            ST_.release(itA)
            chk(3.1)
            itB = ST_.get(pre + "inB")
            wB_ = wslot(itB.slot, [8, 256])
            vb_banks = []
            for s in range(nst):
                bk = nextbank((2, 3, 4, 5))
                vb_banks.append(bk)
                for kc in range(8):
                    P.mm(banks[bk][0:ST, 0:256], xnT[:, kc, s * ST:(s + 1) * ST], wB_[:, kc, :], kc == 0, kc == 7, [wringB[itB.slot], xnTB[s]], [bankB[bk]])
            ST_.release(itB)
            chk(3.2)
            WT = swTsb[:].rearrange("p (g t) -> p g t", t=128) if prompt else swTssb[:].rearrange("p (g t) -> p g t", t=64)
            WTB = swTsbB if prompt else swTssbB
            sbias = sbbc_l if prompt else sbbcs_l
            sgt = [sview("sg_t%d" % k, 6144 + k * 128, 128) for k in range(2)]
            for s in range(nst):
                bk = vb_banks[s]
                vb = banks[bk][0:ST, 0:256]
                P.add("dve", lambda e, o=small[0:ST, 48:54], i=vb: e.bn_stats(out=o, in_=i), [bankB[bk]], [smallB[48]])
                P.add("dve", lambda e, o=small[0:ST, 54:56], i=small[0:ST, 48:54]: e.bn_aggr(out=o, in_=i), [smallB[48]], [smallB[54]])
                P.act(small[0:ST, 56:57], small[0:ST, 55:56], AF.Sqrt, [smallB[54]], [smallB[56]], bias=EPS, scale=1.0)
                P.add("dve", lambda e, o=small[0:ST, 57:58], i=small[0:ST, 56:57]: e.reciprocal(out=o, in_=i), [smallB[56]], [smallB[57]])
                P.ts("dve", small[0:ST, 58:59], small[0:ST, 54:55], small[0:ST, 57:58], ALU.mult, [smallB[54], smallB[57]], [smallB[58]], s2=-1.0, op1=ALU.mult)
                P.act(vn0[0:ST, :], vb, AF.Identity, [bankB[bk], smallB[57], smallB[58]], [vn0B], bias=small[0:ST, 58:59], scale=small[0:ST, 57:58])
                sl = s % 2
                if prompt:
                    P.tt("pool", vnbf[0:ST, sl, :], vn0[0:ST, :], sgn_l[0:ST, :], ALU.mult, [vn0B, layB], [vnbfB[sl]])
                else:
                    P.tt("pool", vnf[0:ST, :], vn0[0:ST, :], sgn_l[0:ST, :], ALU.mult, [vn0B, layB], [vnfB])
                    P.cp("pool", vnbf[0:ST, sl, :], vnf[0:ST, :], [vnfB], [vnbfB[sl]])
                    P.dma("act", dap(S["svb_s"], 0, [(256, 64), (1, 256)]), vnf[0:ST, :], [vnfB], [obuf("svb")])
                bkm = nextbank((0, 1, 6, 7))
                for g in range(4):
                    P.mm(banks[bkm][64 * (g % 2):64 * (g % 2) + 64, (g // 2) * 128:(g // 2) * 128 + ST],
                         vnbf[0:ST, sl, g * 64:(g + 1) * 64], WT[0:ST, g, 0:ST], True, True, [vnbfB[sl], WTB], [bankB[bkm]])
                for i2 in range(2):
                    tmp, tmpB = sgt[i2]
                    P.tt("dve", tmp[:, 0:ST], banks[bkm][:, i2 * 128:i2 * 128 + ST], sbias[:, i2, 0:ST], ALU.add, [bankB[bkm], layB], [tmpB])
                    P.tt("dve", ybT[:, i2, s * ST:(s + 1) * ST], tmp[:, 0:ST], ubT[:, i2, s * ST:(s + 1) * ST], ALU.mult, [tmpB, ubTB], [ybTB])
            ST_.release(it_sw)
            chk(3.3)
            itQ = ST_.get(pre + "inQ")
            wQ = wslot(itQ.slot, [8, 512])
            for p_ in range(4):
                bk = nextbank()
                for kc in range(8):
                    P.mm(banks[bk][:, 0:T], wQ[:, kc, p_ * 128:(p_ + 1) * 128], xnT[:, kc, 0:T], kc == 0, kc == 7, [wringB[itQ.slot]] + xall, [bankB[bk]])
                P.ts("dve", qTz[0:64, p_, 0, 0:T], banks[bk][0:64, 0:T], 0.125, ALU.mult, [bankB[bk]], [qTzB[p_]])
                P.act(qTz[64:128, p_, 1, 0:T], banks[bk][64:128, 0:T], AF.Identity, [bankB[bk]], [qTzB[p_]], scale=0.125)
            ST_.release(itQ)
            chk(3.4)
            itK = ST_.get(pre + "inK")
            wK = wslot(itK.slot, [8, 512])
            for p_ in range(4):
                bk = nextbank()
                for kc in range(8):
                    P.mm(banks[bk][:, 0:T], wK[:, kc, p_ * 128:(p_ + 1) * 128], xnT[:, kc, 0:T], kc == 0, kc == 7, [wringB[itK.slot]] + xall, [bankB[bk]])
                P.cp(evac_eng(), kT[:, p_, 0:T], banks[bk][:, 0:T], [bankB[bk]], [kTB])
            for s in range(nst):
                bk = nextbank()
                for kc in range(8):
                    P.mm(banks[bk][0:ST, :], xnT[:, kc, s * ST:(s + 1) * ST], wK[:, kc, :], kc == 0, kc == 7, [wringB[itK.slot], xnTB[s]], [bankB[bk]])
                P.cp("act", kout[0:ST, :], banks[bk][0:ST, :], [bankB[bk]], [koutB])
                if prompt:
                    tok0 = t * 512 + s * 128
                    P.dma("act", dap(S["pk_s"], tok0 * 64, [(64, 128), (SEQ * 64, NH), (1, 64)]),
                          kout[:].rearrange("p (h d) -> p h d", d=64), [koutB], [obuf("pk")])
                else:
                    for b in range(2):
                        P.dma("act", dap(S["sk_s"], b * NH * 32 * 64, [(64, 32), (32 * 64, NH), (1, 64)]),
                              kout[32 * b:32 * b + 32, :].rearrange("p (h d) -> p h d", d=64), [koutB], [obuf("sk")])
            ST_.release(itK)
            chk(3.5)
            if prompt and t < NT - 1:
                P.dma("act", dap(S["kT_hist"], t * 512, [(SEQ, 128), (128 * SEQ, 4), (1, 512)]), kT[:], [kTB], [khB[t]])
            itV = ST_.get(pre + "inV")
            wV = wslot(itV.slot, [8, 512])
            for s in range(nst):
                bk = nextbank()
                for kc in range(8):
                    P.mm(banks[bk][0:ST, :], xnT[:, kc, s * ST:(s + 1) * ST], wV[:, kc, :], kc == 0, kc == 7, [wringB[itV.slot], xnTB[s]], [bankB[bk]])
                P.cp("act", vout[0:ST, :], banks[bk][0:ST, :], [bankB[bk]], [voutB])
                P.cp("dve", vtok[0:ST, s, :], banks[bk][0:ST, :], [bankB[bk]], [vtokB[s]])
                if prompt:
                    tok0 = t * 512 + s * 128
                    P.dma("act", dap(S["pv_s"], tok0 * 64, [(64, 128), (SEQ * 64, NH), (1, 64)]),
                          vout[:].rearrange("p (h d) -> p h d", d=64), [voutB], [obuf("pv")])
                else:
                    for b in range(2):
                        P.dma("act", dap(S["sv_s"], b * NH * 32 * 64, [(64, 32), (32 * 64, NH), (1, 64)]),
                              vout[32 * b:32 * b + 32, :].rearrange("p (h d) -> p h d", d=64), [voutB], [obuf("sv")])
            ST_.release(itV)
            if prompt and t < NT - 1:
                P.dma("act", dap(S["v_hist"], (t * 512) * 512, [(512, 128), (128 * 512, 4), (1, 512)]), vtok[:], vtokB, [vhB[t]])
            chk(4)
            itG = ST_.get(pre + "glu")
            wg = wslot(itG.slot, [2, 512])
            if prompt:
                for s in range(nst):
                    s5_segment(l, s * 128, 128, sprev[:, 0, :], sprev[:, 1, :], sprevB, wg, wringB[itG.slot])
            else:
                for b in range(2):
                    P.cp("pool", ssm_s[:, b, 0, :], s0re_l[:, b, :], [layB], [ssm_sB])
                    P.cp("pool", ssm_s[:, b, 1, :], s0im_l[:, b, :], [layB], [ssm_sB])
                    s5_segment(l, 32 * b, 32, ssm_s[:, b, 0, :], ssm_s[:, b, 1, :], ssm_sB, wg, wringB[itG.slot])
                    P.dma("act", dap(S["sre_s"], b * 1024, [(8, 128), (1, 8)]), ssm_s[:, b, 0, :], [ssm_sB], [obuf("sre")])
                    P.dma("act", dap(S["sim_s"], b * 1024, [(8, 128), (1, 8)]), ssm_s[:, b, 1, :], [ssm_sB], [obuf("sim")])
            ST_.release(itG)
            ST_.release(it_tab)
            ST_.release(it_mat)
            chk(5)
            held = {}
            osb, osbB = sview("at_osb", 0, 1024, BF16)
            if prompt:
                obk = (6, 7)
                for half in range(2):
                    blocks = []
                    for o in obk:
                        P.mm(banks[o][:, :], zerob[:, 0:128], zerob[:, :], True, False, [constB], [bankB[o]])
                    order = [("cur", None)] + [("hist", kt) for kt in range(t - 1, -1, -1)]
                    hds = range(4 * half, 4 * half + 4)
                    for oi, (ty, kt) in enumerate(order):
                        for hd in hds:
                            ob = obk[hd // 2 - 2 * half]
                            if ty == "cur":
                                for qs in range(4):
                                    w = 128 * (qs + 1)
                                    b = Blk()
                                    b.nq, b.w = 128, w
                                    b.subs = [(qTz[:, hd // 2, hd % 2, qs * 128:(qs + 1) * 128],
                                               banks[ob][:, qs * 128 + (hd % 2) * 64:qs * 128 + (hd % 2) * 64 + 64], bankB[ob])]
                                    b.idx0 = hd * 4 + qs
                                    b.par, b.first, b.last = oi % 2, True, oi == len(order) - 1
                                    b.pre, b.done = None, None
                                    b.kT = kT[:, hd // 2, 0:w]
                                    b.qkB = [qTzB[hd // 2], kTB]
                                    b.mask = Mfull[:, 512 - w:512]
                                    b.vs = [(vtok[:, j, hd * 64:(hd + 1) * 64], 128) for j in range(qs + 1)]
                                    b.vB = [vtokB[j] for j in range(qs + 1)]
                                    b.hd = hd
                                    blocks.append(b)
                            else:
                                nm = pre + "kv%d_%d" % (half, kt)
                                for pq in range(2):
                                    b = Blk()
                                    b.nq, b.w = 128, 512
                                    b.subs = []
                                    for qs in (2 * pq, 2 * pq + 1):
                                        b.subs.append((qTz[:, hd // 2, hd % 2, qs * 128:(qs + 1) * 128],
                                                       banks[ob][:, qs * 128 + (hd % 2) * 64:qs * 128 + (hd % 2) * 64 + 64], bankB[ob]))
                                    b.idx0 = hd * 4 + 2 * pq
                                    b.par, b.first, b.last = oi % 2, False, oi == len(order) - 1
                                    b.pre, b.done = None, None
                                    b.mask = None
                                    if hd == hds[0] and pq == 0:
                                        def pre_fn(nm=nm):
                                            held[nm] = ST_.get(nm)
                                        b.pre = pre_fn
                                    if hd == hds[-1] and pq == 1:
                                        def done_fn(nm=nm):
                                            ST_.release(held[nm])
                                        b.done = done_fn
                                    b.lazy = nm
                                    b.hd = hd
                                    b.__class__ = LazyBlk
                                    b.held = held
                                    blocks.append(b)
                    run_attention(blocks, 128)
                    for pi in range(2):
                        p_ = 2 * half + pi
                        o = obk[pi]
                        ov = osb[:, pi * 512:(pi + 1) * 512]
                        P.cp("dve", ov, banks[o][:, :], [bankB[o]], [osbB])
                        bk = 4 + pi
                        bkb = banksbf[bk]
                        for qs in range(4):
                            P.tr(bkb[:, qs * 128:(qs + 1) * 128], ov[:, qs * 128:(qs + 1) * 128], identb[:], [osbB, constB], [bankB[bk]])
                        P.cp("dve", ycT[:, p_, :], bkb[:, 0:512], [bankB[bk]], [ycTB])
            else:
                obk = (6, 7)
                for o in obk:
                    P.mm(banks[o][:, :], zerob[:, 0:128], zerob[:, :], True, False, [constB], [bankB[o]])
                for b_ in range(2):
                    for (dstbuf, dstB_, nm_) in ((ckbuf, ckB, "ckT%d" % b_), (cvbuf, cvB, None)):
                        for ci in range(NCH):
                            if nm_ is not None:
                                src = (lambda li, nm_=nm_, ci=ci: dap(I[nm_], li * (128 * 4 * PAST) + ci * 2048, [(4 * PAST, 128), (1, 2048)]))
                            else:
                                src = (lambda li, b_=b_, ci=ci: dap(I["cv%d_%d" % (b_, ci)], li * (128 * 2048), [(2048, 128), (1, 2048)]))
                            P.dma("sp", cstg[:], src, [], [cstgB])
                            P.cp("pool", dstbuf[:, ci * 2048:(ci + 1) * 2048], cstg[:], [cstgB], [dstB_])
                    blocks = []
                    order = [("cur", None)] + [("hist", kb) for kb in range(NPB - 1, -1, -1)]
                    for oi, (ty, kb) in enumerate(order):
                        for hd in range(NH):
                            b = Blk()
                            b.nq = 32
                            b.subs = [(qTz[:, hd // 2, hd % 2, 32 * b_:32 * b_ + 32], banks[obk[b_]][0:32, hd * 64:(hd + 1) * 64], bankB[obk[b_]])]
                            b.idx0 = b_ * 8 + hd
                            b.par = oi % 2
                            b.first = oi == 0
                            b.last = oi == len(order) - 1
                            b.pre = None
                            b.done = None
                            b.hd = hd
                            if ty == "cur":
                                b.w = 64
                                b.kT = kT[:, hd // 2, 0:64]
                                b.qkB = [qTzB[hd // 2], kTB]
                                b.mask = Ms[:, b_, :]
                                b.vs = [(vtok[0:64, 0, hd * 64:(hd + 1) * 64], 64)]
                                b.vB = [vtokB[0]]
                            else:
                                b.w = 512
                                b.mask = None
                                b.kT = ckbuf[:].rearrange("p (a t) -> p a t", t=PAST)[:, hd // 2, kb * 512:(kb + 1) * 512]
                                b.qkB = [qTzB[hd // 2], ckB]
                                v4 = cvbuf[:].rearrange("p (s h d) -> p s h d", h=NH, d=64)
                                b.vs = [(v4[:, kb * 4 + j, hd, :], 128) for j in range(4)]
                                b.vB = [cvB]
                            blocks.append(b)
                    run_attention(blocks, 128)
                bk = 4
                bkb = banksbf[bk]
                for b_ in range(2):
                    ov = osb[0:32, b_ * 512:(b_ + 1) * 512]
                    P.cp(evac_eng(), ov, banks[obk[b_]][0:32, :], [bankB[obk[b_]]], [osbB])
                    for p_ in range(4):
                        P.tr(bkb[:, p_ * 64 + 32 * b_:p_ * 64 + 32 * b_ + 32], ov[:, p_ * 128:(p_ + 1) * 128], identb[0:32, 0:32], [osbB, constB], [bankB[bk]])
                P.cp(evac_eng(), ycT[:, :, 0:64], bkb[:, 0:256].rearrange("p (a t) -> p a t", t=64), [bankB[bk]], [ycTB])
            chk(6)
            sgV = [sview("mg_sg%d" % i, i * 512, 512) for i in range(3)]
            mV = [sview("mg_m%d" % i, 1536 + i * 512, 512) for i in range(3)]
            for c in range(8):
                itM = ST_.get(pre + "mg%d" % c)
                wM = wslot(itM.slot, [8, 512])
                wMB = wringB[itM.slot]
                gb = []
                for i in range(3):
                    bk = nextbank()
                    gb.append(bk)
                    for kc in range(8):
                        P.mm(banks[bk][:, 0:T], wM[:, kc, i * 128:(i + 1) * 128], xnT[:, kc, 0:T], kc == 0, kc == 7, [wMB] + xall, [bankB[bk]])
                    P.act(sgV[i][0][:, 0:T], banks[bk][:, 0:T], AF.Sigmoid, [bankB[bk]], [sgV[i][1]])
                bb = []
                for i, (k0, nk, src, srcB) in enumerate(((0, 2, yaT, yaTB), (2, 2, ybT, ybTB), (4, 4, ycT, ycTB))):
                    bk = nextbank()
                    bb.append(bk)
                    for k in range(nk):
                        P.mm(banks[bk][:, 0:T], wM[:, k0 + k, 384:512], src[:, k, 0:T], k == 0, k == nk - 1, [wMB, srcB], [bankB[bk]])
                    P.tt("dve", mV[i][0][:, 0:T], banks[bk][:, 0:T], sgV[i][0][:, 0:T], ALU.mult, [bankB[bk], sgV[i][1]], [mV[i][1]])
                ST_.release(itM)
                P.tt("pool", mV[0][0][:, 0:T], mV[0][0][:, 0:T], mV[1][0][:, 0:T], ALU.add, [mV[0][1], mV[1][1]], [mV[0][1]])
                P.tt("pool", mgT[:, c, 0:T], mV[0][0][:, 0:T], mV[2][0][:, 0:T], ALU.add, [mV[0][1], mV[2][1]], [mgTB[c]])
            if getattr(cfg, "debug", False) and prompt and t == 0 and l == 0:
                P.dma("act", dap(DBG, 0, [(512, 128), (128 * 512, 2), (1, 512)]), yaT[:], [yaTB], [obuf("dbg")])
                P.dma("act", dap(DBG, 2 * 128 * 512, [(512, 128), (128 * 512, 2), (1, 512)]), ybT[:], [ybTB], [obuf("dbg")])
                P.dma("act", dap(DBG, 4 * 128 * 512, [(512, 128), (128 * 512, 4), (1, 512)]), ycT[:], [ycTB], [obuf("dbg")])
                P.dma("act", dap(DBG, 8 * 128 * 512, [(512, 128), (128 * 512, 8), (1, 512)]), mgT[:], mgTB, [obuf("dbg")])
            chk(7)
            for hh in range(2):
                itO = ST_.get(pre + "wo%d" % hh)
                wO = wslot(itO.slot, [8, 512])
                for s in range(nst):
                    bk = nextbank()
                    for kc in range(8):
                        P.mm(banks[bk][0:ST, :], mgT[:, kc, s * ST:(s + 1) * ST], wO[:, kc, :], kc == 0, kc == 7, [wringB[itO.slot], mgTB[kc]], [bankB[bk]])
                    P.tt("dve", h[0:ST, s, hh * 512:(hh + 1) * 512], h[0:ST, s, hh * 512:(hh + 1) * 512], banks[bk][0:ST, :], ALU.add, [hB[s], bankB[bk]], [hB[s]])
                ST_.release(itO)
            chk(8)
            norm_to_xnT(ST, nst)
            actT, actTB = sview("actT", 0, 5632, BF16)
            act3 = actT.rearrange("p (f t) -> p f t", t=512)
            slV = [sview("ffn_sl%d" % i, 5632 + i * 512, 512) for i in range(2)]
            for pc in range(11):
                itU = ST_.get(pre + "gu%d" % pc)
                wU = wslot(itU.slot, [8, 512])
                for sl in range(2):
                    f = 2 * pc + sl
                    bg, bu = nextbank(), nextbank()
                    for kc in range(8):
                        P.mm(banks[bg][:, 0:T], wU[:, kc, sl * 128:(sl + 1) * 128], xnT[:, kc, 0:T], kc == 0, kc == 7, [wringB[itU.slot]] + xall, [bankB[bg]])
                    for kc in range(8):
                        P.mm(banks[bu][:, 0:T], wU[:, kc, 256 + sl * 128:256 + (sl + 1) * 128], xnT[:, kc, 0:T], kc == 0, kc == 7, [wringB[itU.slot]] + xall, [bankB[bu]])
                    sv_, svB = slV[f % 2]
                    P.act(sv_[:, 0:T], banks[bg][:, 0:T], AF.Silu, [bankB[bg]], [svB])
                    P.tt("dve", act3[:, f, 0:T], banks[bu][:, 0:T], sv_[:, 0:T], ALU.mult, [bankB[bu], svB], [actTB])
                ST_.release(itU)
            for hh in range(2):
                bs = [nextbank((0, 1, 2, 3)) if hh == 0 else nextbank((4, 5, 6, 7)) for s in range(nst)]
                for fg, (f0, nf) in enumerate(((0, 8), (8, 8), (16, 6))):
                    itD = ST_.get(pre + "dn%d_%d" % (hh, fg))
                    wD = wslot(itD.slot, [nf, 512])
                    for s in range(nst):
                        for f in range(f0, f0 + nf):
                            P.mm(banks[bs[s]][0:ST, :], act3[:, f, s * ST:(s + 1) * ST], wD[:, f - f0, :], f == 0, f == NF - 1, [wringB[itD.slot], actTB], [bankB[bs[s]]])
                    ST_.release(itD)
                for s in range(nst):
                    P.tt("dve", h[0:ST, s, hh * 512:(hh + 1) * 512], h[0:ST, s, hh * 512:(hh + 1) * 512], banks[bs[s]][0:ST, :], ALU.add, [hB[s], bankB[bs[s]]], [hB[s]])

        class LazyBlk(Blk):
            @property
            def kT(self):
                s = self.held[self.lazy].slot
                return shaped(wring[:, s, 0:2048], [4, 512])[:, self.hd // 2, :]

            @property
            def qkB(self):
                return [qTzB[self.hd // 2], wringB[self.held[self.lazy].slot]]

            @property
            def vs(self):
                s = self.held[self.lazy].slot
                v4 = shaped(wring[:, s, 2048:4096], [4, 512])
                return [(v4[:, j, self.hd * 64:(self.hd + 1) * 64], 128) for j in range(4)]

            @property
            def vB(self):
                return [wringB[self.held[self.lazy].slot]]

        class LazyBlkS(Blk):
            @property
            def kT(self):
                nmk, nmv, kb = self.lazy_s
                s = self.held[nmk].slot
                return shaped(wring[:, s, 0:4 * self.PAST], [4, self.PAST])[:, self.hd // 2, kb * 512:(kb + 1) * 512]

            @property
            def qkB(self):
                return [qTzB[self.hd // 2], wringB[self.held[self.lazy_s[0]].slot]]

            @property
            def vs(self):
                nmk, nmv, kb = self.lazy_s
                s = self.held[nmv].slot
                v4 = shaped(wring[:, s, 0:self.PAST * 4], [self.PAST // 128, NH, 64])
                return [(v4[:, kb * 4 + j, self.hd, :], 128) for j in range(4)]

            @property
            def vB(self):
                return [wringB[self.held[self.lazy_s[1]].slot]]

        def final_out(kind, t):
            prompt = kind == "p"
            ST = 128 if prompt else 64
            nst = 4 if prompt else 1
            junk, junkB = sview("junk", 0, 512, BF16)
            for s in range(nst):
                yo, yoB = sview("yo%d" % (s % 2), 1024 + (s % 2) * 1024, 1024)
                c_ss, c_sq, c_rs = s, 4 + s, 8 + s
                P.act(junk[0:ST, :], h[0:ST, s, :], AF.Square, [hB[s]], [junkB, smallB[c_ss]], accum=small[0:ST, c_ss:c_ss + 1])
                P.act(small[0:ST, c_sq:c_sq + 1], small[0:ST, c_ss:c_ss + 1], AF.Sqrt, [smallB[c_ss]], [smallB[c_sq]], bias=EPS, scale=1.0 / D)
                P.add("dve", lambda e, o=small[0:ST, c_rs:c_rs + 1], i=small[0:ST, c_sq:c_sq + 1]: e.reciprocal(out=o, in_=i), [smallB[c_sq]], [smallB[c_rs]])
                P.stt(yo[0:ST, :], h[0:ST, s, :], small[0:ST, c_rs:c_rs + 1], gfin[0:ST, :], ALU.mult, ALU.mult, [hB[s], smallB[c_rs], constB], [yoB])
                if prompt:
                    P.dma("act", dap(O["yp"], (t * 512 + s * 128) * D, [(D, 128), (1, D)]), yo[:, :], [yoB], [obuf("yp")])
                else:
                    P.dma("act", dap(O["ys"], 0, [(D, 64), (1, D)]), yo[0:64, :], [yoB], [obuf("ys")])

        hbB = [Buf("hb%d" % i) for i in range(NT + 1)]
        for t in range(NT):
            P.dma("sp", dap(S["hbuf"], t * 512 * D, [(4 * D, 128), (1, 4 * D)]), dap(I["xp"], t * 512 * D, [(4 * D, 128), (1, 4 * D)]), [], [hbB[t]])
        P.dma("sp", dap(S["hbuf"], SEQ * D, [(D, 64), (1, D)]), dap(I["xs"], 0, [(D, 64), (1, D)]), [], [hbB[NT]])
        chk(2.2)
        P.barrier()
        P.loop_begin(L)
        l = None
        P.dma("sp", r_l[:], (lambda li: dap(S["rpj_d"], li * 1024, [(8, 128), (1, 8)])), [], [layB])
        P.dma("sp", d_l[:], (lambda li: dap(I["d_pm"], li * 2, [(L * 2, 128), (1, 2)])), [], [layB])
        P.dma("sp", s0re_l[:].rearrange("p a b -> p (a b)"), (lambda li: dap(I["s0_re"], li * 16, [(L * 16, 128), (1, 16)])), [], [layB])
        P.dma("sp", s0im_l[:].rearrange("p a b -> p (a b)"), (lambda li: dap(I["s0_im"], li * 16, [(L * 16, 128), (1, 16)])), [], [layB])
        P.dma("sp", sgn_l[:], (lambda li: dap(I["sgn_bc"], li * 256, [(L * 256, 128), (1, 256)])), [], [layB])
        P.dma("sp", sbbc_l[:].rearrange("p a b -> p (a b)"), (lambda li: dap(I["sb_bc"], li * 256, [(L * 256, 128), (1, 256)])), [], [layB])
        P.dma("sp", sbbcs_l[:].rearrange("p a b -> p (a b)"), (lambda li: dap(I["sb_bcs"], li * 128, [(L * 128, 128), (1, 128)])), [], [layB])
        P.memset("pool", sprev[:], 0.0, [sprevB])
        chk(2.5)
        for kind, t in tiles:
            if kind == "p":
                P.dma("sp", h[:], dap(S["hbuf"], t * 512 * D, [(D, 128), (128 * D, 4), (1, D)]), [hbB[t]], hB)
            else:
                P.dma("sp", h[0:64, 0, :], dap(S["hbuf"], SEQ * D, [(D, 64), (1, D)]), [hbB[NT]], [hB[0]])
            layer_tile(kind, t, l)
            if kind == "p":
                P.dma("act", dap(S["hbuf"], t * 512 * D, [(D, 128), (128 * D, 4), (1, D)]), h[:], hB, [hbB[t]])
            else:
                P.dma("act", dap(S["hbuf"], SEQ * D, [(D, 64), (1, D)]), h[0:64, 0, :], [hB[0]], [hbB[NT]])
        P.dma("act", dap(S["pre_s"], 0, [(8, 128), (1, 8)]), sprev[:, 0, :], [sprevB], [obuf("pre")])
        P.dma("act", dap(S["pim_s"], 0, [(8, 128), (1, 8)]), sprev[:, 1, :], [sprevB], [obuf("pim")])
        lay_out = list(outB)

        def cp_out(oname, sname, total):
            row = total // 128
            if row > 16384:
                dims = [(row, 128), (16384, row // 16384), (1, 16384)]
            else:
                dims = [(row, 128), (1, row)]
            P.dma("sp", (lambda li, oname=oname, total=total, dims=dims: dap(O[oname], li * total, dims)), dap(S[sname], 0, dims), lay_out, [obuf("o_" + oname)])
        cp_out("pk", "pk_s", NH * SEQ * 64)
        cp_out("pv", "pv_s", NH * SEQ * 64)
        if cfg.with_sample:
            for oname, total in (("sk", 2 * NH * 32 * 64), ("sv", 2 * NH * 32 * 64), ("svb", 64 * 256), ("sre", 2048), ("sim", 2048)):
                cp_out(oname, oname + "_s", total)
        cp_out("pre", "pre_s", 1024)
        cp_out("pim", "pim_s", 1024)
        P.barrier()
        P.loop_end()
        for kind, t in tiles:
            if kind == "p":
                P.dma("sp", h[:], dap(S["hbuf"], t * 512 * D, [(D, 128), (128 * D, 4), (1, D)]), [], hB)
            else:
                P.dma("sp", h[0:64, 0, :], dap(S["hbuf"], SEQ * D, [(D, 64), (1, D)]), [], [hB[0]])
            final_out(kind, t)
        assert ST_.cp_ == len(ST_.items), (ST_.cp_, len(ST_.items))
        P.final_wait("sp", outB)
        block = es.enter_context(nc.Block())
        P.replay(block)
        print("built: ops=%d" % P.nops)
        if getattr(cfg, "dump", None):
            with open(cfg.dump, "w") as f_:
                for rec in P.log:
                    f_.write(repr(rec) + "\n")
    return nc


def _prep_shared(inp, L):
    f = lambda a: np.ascontiguousarray(np.asarray(a, dtype=np.float32))
    sh = {}
    sh["w_in"] = f(inp["w_in"][:L])
    sh["w_gu"] = f(inp["w_gate_up"][:L])
    sh["w_dn"] = f(inp["w_down"][:L])
    sh["w_ba"] = f(inp["w_branch_a"][:L])
    sh["w_bb"] = f(inp["w_branch_b"][:L])
    sh["w_bc"] = f(inp["w_branch_c"][:L])
    sh["w_out"] = f(inp["w_out"][:L])
    sh["w_glu"] = f(inp["ssm_w_glu"][:L])
    pk = lambda a: f(np.asarray(a)[:L].reshape(L, 8, 128).transpose(2, 0, 1))
    sh["nm_pk"] = pk(inp["norm_mix"])
    sh["nf_pk"] = pk(inp["norm_ffn"])
    sh["gfin_bc"] = f(np.broadcast_to(np.asarray(inp["norm_final"])[None, :], (128, D)))
    def pj(a):
        a = np.asarray(a)[:L].reshape(L, 8, 2, 64)
        return f(a.transpose(2, 3, 0, 1).reshape(128, L, 8))
    sh["a_re_pj"] = pj(inp["ssm_a_re"])
    sh["a_im_pj"] = pj(inp["ssm_a_im"])
    ldt_full = np.repeat(np.asarray(inp["ssm_log_dt"])[:L, :, None], 64, axis=2)
    sh["ldt_pj"] = pj(ldt_full)
    sh["a_re_row"] = f(np.asarray(inp["ssm_a_re"])[:L].reshape(L, 1024))
    sh["a_im_row"] = f(np.asarray(inp["ssm_a_im"])[:L].reshape(L, 1024))
    sh["ldt_row"] = f(ldt_full.reshape(L, 1024))
    b_re, b_im = np.asarray(inp["ssm_b_re"])[:L], np.asarray(inp["ssm_b_im"])[:L]
    c_re, c_im = np.asarray(inp["ssm_c_re"])[:L], np.asarray(inp["ssm_c_im"])[:L]
    bb_re = np.zeros((L, 128, 8, 128), np.float32)
    bb_im = np.zeros((L, 128, 8, 128), np.float32)
    cb_re = np.zeros((L, 128, 8, 128), np.float32)
    cb_im = np.zeros((L, 128, 8, 128), np.float32)
    for g in range(16):
        j, gi = g // 2, g % 2
        ch0 = 16 * (g % 8)
        st0 = 64 * gi
        bb_re[:, ch0:ch0 + 16, j, st0:st0 + 64] = b_re[:, g].transpose(0, 2, 1)
        bb_im[:, ch0:ch0 + 16, j, st0:st0 + 64] = b_im[:, g].transpose(0, 2, 1)
        cb_re[:, st0:st0 + 64, j, ch0:ch0 + 16] = c_re[:, g].transpose(0, 2, 1)
        cb_im[:, st0:st0 + 64, j, ch0:ch0 + 16] = c_im[:, g].transpose(0, 2, 1)
    sh["bblk_re"] = bb_re.reshape(L, 128, 1024)
    sh["bblk_im"] = bb_im.reshape(L, 128, 1024)
    sh["cblk_re"] = cb_re.reshape(L, 128, 1024)
    sh["cblk_im"] = cb_im.reshape(L, 128, 1024)
    sh["d_pm"] = f(np.asarray(inp["ssm_d"])[:L].reshape(L, 2, 128).transpose(2, 0, 1))
    sh["sgn_bc"] = f(np.broadcast_to(np.asarray(inp["sgu_norm"])[:L][None], (128, L, 256)))
    sw = np.asarray(inp["sgu_w"])[:L]
    sh["swT"] = f(sw.transpose(0, 3, 1, 2))
    swTs = np.zeros((L, 64, 4, 64), np.float32)
    for b in range(2):
        swTs[:, 32 * b:32 * b + 32, :, 32 * b:32 * b + 32] = sw[:, :, :32, :32].transpose(0, 3, 1, 2)
    sh["swTs"] = swTs
    sbv = np.asarray(inp["sgu_b"])[:L]
    sb_bc = np.zeros((128, L, 2, 128), np.float32)
    sb_bcs = np.zeros((128, L, 2, 64), np.float32)
    for g in range(4):
        sb_bc[64 * (g % 2):64 * (g % 2) + 64, :, g // 2, :] = sbv[None, :, g, :]
        sb_bcs[64 * (g % 2):64 * (g % 2) + 64, :, g // 2, :] = np.concatenate([sbv[:, g, :32], sbv[:, g, :32]], axis=-1)[None]
    sh["sb_bc"] = sb_bc
    sh["sb_bcs"] = sb_bcs
    return sh


def _prep_core(inp, c, L, PAST, nprompt):
    f = lambda a: np.ascontiguousarray(np.asarray(a, dtype=np.float32))
    m = {}
    m["xp"] = f(inp["x_prompt"][c % nprompt])
    sb = slice(2 * c, 2 * c + 2)
    m["xs"] = f(np.asarray(inp["x_sample"])[sb].reshape(64, D))
    def st(a):
        a = np.asarray(a)[:L, sb].reshape(L, 2, 8, 2, 64)
        return f(a.transpose(3, 4, 0, 1, 2).reshape(128, L, 2, 8))
    m["s0_re"] = st(inp["state_ssm_re"])
    m["s0_im"] = st(inp["state_ssm_im"])
    ck = np.asarray(inp["cache_sb_k"])[:L, sb]
    ckT = ck.reshape(L, 2, 4, 2, PAST, 64).transpose(0, 1, 3, 5, 2, 4).reshape(L, 2, 128, 4 * PAST)
    cvv = np.asarray(inp["cache_sb_v"])[:L, sb]
    cv = cvv.reshape(L, 2, NH, PAST // 128, 128, 64).transpose(0, 1, 4, 3, 2, 5).reshape(L, 2, 128, PAST * 4)
    for b in range(2):
        m["ckT%d" % b] = f(ckT[:, b])
        for ci in range((PAST * 4) // 2048):
            m["cv%d_%d" % (b, ci)] = f(cv[:, b, :, ci * 2048:(ci + 1) * 2048])
    return m


def _unstate(a):
    sh = a.shape[:-2]
    a = a.reshape(sh + (2, 64, 8))
    return np.ascontiguousarray(np.moveaxis(a, -1, -3).reshape(sh + (16, 64)))


_NC_CACHE = {}


def run(inp, cfg, n_cores=8, nprompt=4):
    key = (cfg.L, cfg.SEQ, cfg.PAST, cfg.with_sample, getattr(cfg, 'debug', False), cfg.stop)
    if key not in _NC_CACHE:
        _NC_CACHE[key] = build(cfg)
    nc = _NC_CACHE[key]
    L = cfg.L
    sh = _prep_shared(inp, L)
    in_maps = []
    for c in range(n_cores):
        m = dict(sh)
        m.update(_prep_core(inp, c, L, cfg.PAST, nprompt))
        in_maps.append(m)
    res = run_bass_kernel_spmd(nc, in_maps, core_ids=list(range(n_cores)))
    R = res.results
    global LAST_R
    LAST_R = R
    npr = min(nprompt, n_cores)
    y_prompt = np.stack([R[b]["yp"] for b in range(npr)])
    y_sample = np.concatenate([R[c]["ys"].reshape(2, 32, D) for c in range(n_cores)], axis=0)
    p_re = np.stack([_unstate(R[b]["pre"]) for b in range(npr)], axis=1)
    p_im = np.stack([_unstate(R[b]["pim"]) for b in range(npr)], axis=1)
    p_k = np.stack([R[b]["pk"] for b in range(npr)], axis=1)
    p_v = np.stack([R[b]["pv"] for b in range(npr)], axis=1)
    s_re = np.concatenate([_unstate(R[c]["sre"]) for c in range(n_cores)], axis=1)
    s_im = np.concatenate([_unstate(R[c]["sim"]) for c in range(n_cores)], axis=1)
    s_k = np.concatenate([R[c]["sk"] for c in range(n_cores)], axis=1)
    s_v = np.concatenate([R[c]["sv"] for c in range(n_cores)], axis=1)
    s_vb = np.concatenate([R[c]["svb"].reshape(L, 2, 32, 256) for c in range(n_cores)], axis=1)
    f = lambda a: np.ascontiguousarray(a, dtype=np.float32)
    return tuple(f(a) for a in (y_prompt, y_sample, p_re, p_im, p_k, p_v, s_re, s_im, s_k, s_v, s_vb))


def kernel(**inputs):
    cfg = Cfg(L=4, SEQ=8192, PAST=1024, with_sample=True)
    return run(inputs, cfg, n_cores=8, nprompt=4)
```

```python
import numpy as np
from contextlib import ExitStack
import concourse.bass as bass
import concourse.mybir as mybir
from concourse.bass_utils import run_bass_kernel_spmd

F32 = mybir.dt.float32
BF16 = mybir.dt.bfloat16
I32 = mybir.dt.int32
AF = mybir.ActivationFunctionType
ALU = mybir.AluOpType

D = 1024
KC = 8
INW = 5376
FF = 2816
NF = 22
NH = 8
EPS = 1e-6
TWO_PI = 6.283185307179586
NEG = -30000.0


class Cfg:
    def __init__(self, L=4, SEQ=8192, PAST=1024, with_sample=True):
        self.L = L
        self.SEQ = SEQ
        self.PAST = PAST
        self.NT = SEQ // 512
        self.with_sample = with_sample
        self.stop = 99


class _Stop(Exception):
    pass


class Sem:
    def __init__(self, h):
        self.h = h
        self.v = 0


class Buf:
    __slots__ = ("name", "w", "r", "grp", "lo", "hi", "excl")

    def __init__(self, name, grp=None, lo=0, hi=0, excl=False):
        self.name = name
        self.excl = excl
        self.w = {}
        self.r = {}
        self.grp = grp
        self.lo = lo
        self.hi = hi
        if grp is not None:
            grp.append(self)
        Buf.ALL.append(self)


Buf.ALL = []


class Eng:
    def __init__(self, name, sem):
        self.name = name
        self.sem = sem
        self.ops = []
        self.seen = {}


class Prog:
    def __init__(self, nc, es, n_dma_sems=40):
        self.nc = nc
        self.eng = {}
        for n in ("pe", "act", "dve", "pool", "sp"):
            self.eng[n] = Eng(n, Sem(es.enter_context(nc.semaphore("sem_" + n))))
        self.dsems = [Sem(es.enter_context(nc.semaphore("dsem%d" % i))) for i in range(n_dma_sems)]
        self.barA = es.enter_context(nc.semaphore("barA"))
        self.barB = es.enter_context(nc.semaphore("barB"))
        self.drr = 0
        self.nops = 0
        self.stopped = False
        self.log = []
        self.cur_desc = ''
        self.semname = {id(e.sem): n for n, e in self.eng.items()}
        for i_, s_ in enumerate(self.dsems):
            self.semname[id(s_)] = 'd%d' % i_

    def _deps(self, reads, writes):
        deps = {}

        def need(d):
            for s, v in d.items():
                if deps.get(s, 0) < v:
                    deps[s] = v

        for b in reads:
            need(b.w)
            if b.excl:
                need(b.r)
        for b in writes:
            need(b.w)
            need(b.r)
            if b.grp is not None:
                for y in b.grp:
                    if y is not b and y.lo < b.hi and b.lo < y.hi:
                        need(y.w)
                        need(y.r)
        return deps

    def _waits(self, E, deps):
        waits = []
        for s, v in deps.items():
            if s is E.sem:
                if E.name == "pe":
                    continue
                if E.sem.v - v >= 4:
                    continue
            if E.seen.get(s, 0) >= v:
                continue
            E.seen[s] = v
            waits.append((s, v))
        return waits

    def add(self, en, fn, reads=(), writes=()):
        if self.stopped:
            return
        E = self.eng[en]
        waits = self._waits(E, self._deps(reads, writes))
        E.sem.v += 1
        val = E.sem.v
        E.ops.append((waits, fn, E.sem, 1))
        self.log.append((en, E.sem.v, [(self.semname.get(id(s_), '?'), v_) for s_, v_ in waits], self.cur_desc))
        for b in reads:
            if b.r.get(E.sem, 0) < val:
                b.r[E.sem] = val
        for b in writes:
            b.w = {E.sem: val}
            b.r = {}
        self.nops += 1

    def dma(self, q, out, in_, reads=(), writes=()):
        if self.stopped:
            return
        E = self.eng[q]
        sem = self.dsems[self.drr % len(self.dsems)]
        self.drr += 1
        deps = self._deps(reads, writes)
        if sem.v > 0 and deps.get(sem, 0) < sem.v:
            deps[sem] = sem.v
        waits = self._waits(E, deps)
        sem.v += 16
        val = sem.v
        E.ops.append((waits, ('DMA', out, in_), sem, 16))
        self.log.append((q + '-dma', (self.semname.get(id(sem)), val), [(self.semname.get(id(s_), '?'), v_) for s_, v_ in waits], 'dma'))
        for b in reads:
            if b.r.get(sem, 0) < val:
                b.r[sem] = val
        for b in writes:
            b.w = {sem: val}
            b.r = {}
        self.nops += 1

    def final_wait(self, q, bufs):
        E = self.eng[q]
        deps = {}
        for b in bufs:
            for d in (b.w, b.r):
                for s, v in d.items():
                    if deps.get(s, 0) < v:
                        deps[s] = v
        for e2 in self.eng.values():
            if e2.sem.v > 0:
                deps[e2.sem] = max(deps.get(e2.sem, 0), e2.sem.v) if e2 is not E else deps.get(e2.sem, 0)
        deps = {s: v for s, v in deps.items() if v > 0 and s is not E.sem}
        for s in self.dsems:
            if s.v > 0:
                deps[s] = s.v
        waits = [(s, v) for s, v in deps.items()]
        E.ops.append((waits, None, None, 0))

    def barrier(self):
        finals = [(s, s.v) for s in [e.sem for e in self.eng.values()] + self.dsems if s.v > 0]
        for E in self.eng.values():
            E.ops.append(([], ('BAR', finals), None, 0))
            E.seen = {}
        for s in [e.sem for e in self.eng.values()] + self.dsems:
            s.v = 0
        for b in Buf.ALL:
            b.w = {}
            b.r = {}

    def loop_begin(self, n):
        for E in self.eng.values():
            E.ops.append(([], ('LOOP', n), None, 0))

    def loop_end(self):
        for E in self.eng.values():
            E.ops.append(([], ('ENDLOOP',), None, 0))

    def replay(self, block):
        amap = {"pe": block.tensor, "act": block.scalar, "dve": block.vector, "pool": block.gpsimd, "sp": block.sync}
        NE = len(self.eng)
        allsems = [e.sem for e in self.eng.values()] + self.dsems
        for n, deco in amap.items():
            ops = self.eng[n].ops
            mysem = self.eng[n].sem

            def body(e, ops=ops, n=n, mysem=mysem):
                st = {"li": None, "nbar": 0, "ctx": None, "nloop": 0}

                def run(lst):
                    idx = 0
                    while idx < len(lst):
                        waits, fn, sem, inc = lst[idx]
                        idx += 1
                        for s, v in waits:
                            e.wait_ge(s.h, v)
                        if fn is None:
                            continue
                        if isinstance(fn, tuple):
                            kind = fn[0]
                            if kind == 'DMA':
                                o, i_ = fn[1], fn[2]
                                if callable(o):
                                    o = o(st["li"])
                                if callable(i_):
                                    i_ = i_(st["li"])
                                try:
                                    e.dma_start(out=o, in_=i_).then_inc(sem.h, inc)
                                except Exception:
                                    print('DMA FAIL', n, o.tensor.name, o.offset, list(o.ap), i_.tensor.name, i_.offset, list(i_.ap))
                                    raise
                            elif kind == 'BAR':
                                for s, v in fn[1]:
                                    if s is not mysem:
                                        e.wait_ge(s.h, v)
                                e.sem_inc(self.barA, 1)
                                if st["li"] is None:
                                    k1 = st["nbar"] + 1
                                else:
                                    k1 = st["li"] + (st["nbar"] + 1)
                                if n == "sp":
                                    e.wait_ge(self.barA, k1 * NE)
                                    for s in allsems:
                                        e.sem_clear(s.h)
                                    e.sem_inc(self.barB, 1)
                                e.wait_ge(self.barB, k1)
                                if st["li"] is None:
                                    st["nbar"] += 1
                            elif kind == 'LOOP':
                                depth, j = 1, idx
                                while True:
                                    f2 = lst[j][1]
                                    if isinstance(f2, tuple) and f2[0] == 'LOOP':
                                        depth += 1
                                    if isinstance(f2, tuple) and f2[0] == 'ENDLOOP':
                                        depth -= 1
                                        if depth == 0:
                                            break
                                    j += 1
                                inner = lst[idx:j]
                                nb_in = sum(1 for x in inner if isinstance(x[1], tuple) and x[1][0] == 'BAR')
                                with e.Fori(0, fn[1]) as li:
                                    st["li"] = li
                                    run(inner)
                                    st["li"] = None
                                st["nbar"] += nb_in * fn[1]
                                idx = j + 1
                            continue
                        fn(e).then_inc(sem.h, inc)

                run(ops)

            deco(body)

    def mm(self, out, lhsT, rhs, start, stop, reads, writes):
        self.cur_desc = 'mm %s <- %s x %s st=%s sp=%s' % (_d(out), _d(lhsT), _d(rhs), start, stop)
        self.add("pe", lambda e: e.matmul(out, lhsT=lhsT, rhs=rhs, start=start, stop=stop), reads, writes)

    def tr(self, out, in_, ident, reads, writes):
        self.cur_desc = 'tr %s <- %s' % (_d(out), _d(in_))
        self.add("pe", lambda e: e.transpose(out, in_, ident), reads, writes)

    def act(self, out, in_, func, reads, writes, bias=None, scale=None, accum=None):
        self.cur_desc = 'act %s %s <- %s b=%s' % (func, _d(out), _d(in_), bias if isinstance(bias, (float, type(None))) else _d(bias))
        kw = {}
        if bias is not None:
            kw["bias"] = bias
        if scale is not None:
            kw["scale"] = scale
        if accum is not None:
            kw["accum_out"] = accum
        self.add("act", lambda e: e.activation(out=out, in_=in_, func=func, **kw), reads, writes)

    def tt(self, en, out, in0, in1, op, reads, writes):
        self.cur_desc = 'tt %s %s <- %s , %s' % (op, _d(out), _d(in0), _d(in1))
        self.add(en, lambda e: e.tensor_tensor(out=out, in0=in0, in1=in1, op=op), reads, writes)

    def ts(self, en, out, in0, s1, op0, reads, writes, s2=None, op1=None):
        self.cur_desc = 'ts %s %s <- %s' % (op0, out.tensor.name, in0.tensor.name)
        if op1 is None:
            self.add(en, lambda e: e.tensor_scalar(out=out, in0=in0, scalar1=s1, scalar2=None, op0=op0), reads, writes)
        else:
            self.add(en, lambda e: e.tensor_scalar(out=out, in0=in0, scalar1=s1, scalar2=s2, op0=op0, op1=op1), reads, writes)

    def stt(self, out, in0, scalar, in1, op0, op1, reads, writes):
        self.add("dve", lambda e: e.scalar_tensor_tensor(out=out, in0=in0, scalar=scalar, in1=in1, op0=op0, op1=op1), reads, writes)

    def cp(self, en, out, in_, reads, writes):
        self.cur_desc = 'cp %s <- %s' % (_d(out), _d(in_))
        if en == "act":
            self.add(en, lambda e: e.activation(out=out, in_=in_, func=AF.Identity), reads, writes)
        else:
            self.add(en, lambda e: e.tensor_copy(out=out, in_=in_), reads, writes)

    def memset(self, en, ap, val, writes):
        self.add(en, lambda e: e.memset(ap, val), (), writes)


def _d(ap):
    return '%s@%s%s' % (ap.tensor.name, ap.offset, list(ap.ap))


def dap(t, off, dims):
    return bass.AP(t, off if not isinstance(off, (int, np.integer)) else int(off), [[int(s), int(c)] for s, c in dims])


def build(cfg):
    nc = bass.Bass("TRN2", target_bir_lowering=False)
    L, SEQ, PAST, NT = cfg.L, cfg.SEQ, cfg.PAST, cfg.NT
    NPB = PAST // 512

    def din(name, shape, dt=F32):
        return nc.dram_tensor(name, list(shape), dt, kind="ExternalInput")

    def dout(name, shape, dt=F32):
        return nc.dram_tensor(name, list(shape), dt, kind="ExternalOutput")

    def dscr(name, shape, dt):
        return nc.dram_tensor(name, list(shape), dt, kind="Internal")

    I = {}
    for name, shape in [
        ("xp", (SEQ, D)), ("xs", (64, D)),
        ("w_in", (L, D, INW)), ("w_gu", (L, D, 2 * FF)), ("w_dn", (L, FF, D)),
        ("w_ba", (L, 256, D)), ("w_bb", (L, 256, D)), ("w_bc", (L, 512, D)), ("w_out", (L, D, D)),
        ("w_glu", (L, 256, 512)),
        ("nm_pk", (128, L, 8)), ("nf_pk", (128, L, 8)), ("gfin_bc", (128, D)),
        ("a_re_pj", (128, L, 8)), ("a_im_pj", (128, L, 8)), ("ldt_pj", (128, L, 8)),
        ("a_re_row", (L, 1024)), ("a_im_row", (L, 1024)), ("ldt_row", (L, 1024)),
        ("bblk_re", (L, 128, 1024)), ("bblk_im", (L, 128, 1024)),
        ("cblk_re", (L, 128, 1024)), ("cblk_im", (L, 128, 1024)),
        ("d_pm", (128, L, 2)), ("s0_re", (128, L, 2, 8)), ("s0_im", (128, L, 2, 8)),
        ("sgn_bc", (128, L, 256)), ("swT", (L, 128, 4, 128)), ("swTs", (L, 64, 4, 64)),
        ("sb_bc", (128, L, 2, 128)), ("sb_bcs", (128, L, 2, 64)),
    ]:
        I[name] = din(name, shape)
    for b_ in range(2):
        I["ckT%d" % b_] = din("ckT%d" % b_, (L, 128, 4 * PAST))
        for ci in range((PAST * 4) // 2048):
            I["cv%d_%d" % (b_, ci)] = din("cv%d_%d" % (b_, ci), (L, 128, 2048))
    O = {}
    for name, shape in [
        ("yp", (SEQ, D)), ("ys", (64, D)),
        ("pre", (L, 128, 8)), ("pim", (L, 128, 8)),
        ("pk", (L, NH, SEQ, 64)), ("pv", (L, NH, SEQ, 64)),
        ("sre", (L, 2, 128, 8)), ("sim", (L, 2, 128, 8)),
        ("sk", (L, 2, NH, 32, 64)), ("sv", (L, 2, NH, 32, 64)),
        ("svb", (L, 64, 256)),
    ]:
        O[name] = dout(name, shape)
    S = {}
    NCH = (PAST * 4) // 2048
    recs = [("inA", 8 * 512), ("inB", 8 * 256), ("inQ", 8 * 512), ("inK", 8 * 512), ("inV", 8 * 512)]
    recs += [("mg%d" % c, 4096) for c in range(8)] + [("wo%d" % hh, 4096) for hh in range(2)]
    recs += [("gu%d" % pc, 4096) for pc in range(11)]
    recs += [("dn%d_%d" % (hh, fg), nf * 512) for hh in range(2) for fg, nf in enumerate((8, 8, 6))]
    for name, R in recs:
        S[name] = dscr("r_" + name, (L, 128, R), BF16)
    for name, shape, dt in [
        ("wglu_bf", (L, 256, 512), BF16),
        ("s5tab", (L, 128, 3, 1024), F32), ("s5mat", (L, 128, 4, 1024), BF16),
        ("swT_bf", (L, 128, 512), BF16), ("swTs_bf", (L, 64, 256), BF16),
        ("kT_hist", (4, 128, SEQ), BF16), ("v_hist", (SEQ, 512), BF16),
        ("rpj_d", (L, 128, 8), F32), ("hbuf", (SEQ + 64, D), F32),
        ("pk_s", (NH, SEQ, 64), F32), ("pv_s", (NH, SEQ, 64), F32),
        ("sk_s", (2, NH, 32, 64), F32), ("sv_s", (2, NH, 32, 64), F32), ("svb_s", (64, 256), F32),
        ("sre_s", (2, 128, 8), F32), ("sim_s", (2, 128, 8), F32), ("pre_s", (128, 8), F32), ("pim_s", (128, 8), F32),
    ]:
        S[name] = dscr(name, shape, dt)
    es = ExitStack()
    with es:
        P = Prog(nc, es)

        def sb(name, shape, dt):
            return es.enter_context(nc.sbuf_tensor("sb_" + name, list(shape), dt))

        zp_t = [es.enter_context(nc.psum_tensor("zpair%d" % i, [128, 1024], F32)) for i in range(2)]
        zpair = [zp_t[i][:, :] for i in range(2)]
        banks = [zpair[i // 2][:, (i % 2) * 512:(i % 2) * 512 + 512] for i in range(4)]
        banks += [es.enter_context(nc.psum_tensor("bank%d" % i, [128, 512], F32))[:, :] for i in range(4, 8)]
        bankB = [Buf("bank%d" % i, excl=True) for i in range(8)]
        banksbf = [banks[i].bitcast(BF16) for i in range(8)]
        bank_rr = [0]

        def nextbank(pool=(0, 1, 2, 3, 4, 5, 6, 7)):
            i = pool[bank_rr[0] % len(pool)]
            bank_rr[0] += 1
            return i

        h = sb("h", [128, 4, D], F32)
        hB = [Buf("h%d" % s) for s in range(4)]
        xnT = sb("xnT", [128, 8, 512], BF16)
        xnTB = [Buf("xnT%d" % s) for s in range(4)]
        uaT = sb("uaT", [128, 2, 512], F32)
        uaTB = Buf("uaT")
        uabf = sb("uabf", [128, 2, 512], BF16)
        uabfB = Buf("uabf")
        ubT = sb("ubT", [128, 2, 512], F32)
        ubTB = Buf("ubT")
        qTz = sb("qTz", [128, 4, 2, 512], BF16)
        qTzB = [Buf("qTz%d" % p) for p in range(4)]
        kT = sb("kT", [128, 4, 512], BF16)
        kTB = Buf("kT")
        vtok = sb("vtok", [128, 4, 512], BF16)
        vtokB = [Buf("vtok%d" % s) for s in range(4)]
        kout = sb("kout", [128, 512], F32)
        koutB = Buf("kout")
        vout = sb("vout", [128, 512], F32)
        voutB = Buf("vout")
        vnbf = sb("vnbf", [128, 2, 256], BF16)
        vnbfB = [Buf("vnbf0"), Buf("vnbf1")]
        vnf = sb("vnf", [128, 256], F32)
        vnfB = Buf("vnf")
        vn0 = sb("vn0", [128, 256], F32)
        vn0B = Buf("vn0")
        small = sb("small", [128, 64], F32)
        smallB = [Buf("small%d" % i) for i in range(64)]
        yaT = sb("yaT", [128, 2, 512], BF16)
        yaTB = Buf("yaT")
        ybT = sb("ybT", [128, 2, 512], BF16)
        ybTB = Buf("ybT")
        ycT = sb("ycT", [128, 4, 512], BF16)
        ycTB = Buf("ycT")
        mgT = sb("mgT", [128, 8, 512], BF16)
        mgTB = [Buf("mgT%d" % c) for c in range(8)]
        s5tab = sb("s5tab", [128, 3, 1024], F32)
        s5tabB = Buf("s5tab")
        s5mat = sb("s5mat", [128, 4, 1024], BF16)
        s5matB = Buf("s5mat")
        swTsb = sb("swTsb", [128, 512], BF16)
        swTsbB = Buf("swTsb")
        swTssb = sb("swTssb", [64, 256], BF16)
        swTssbB = Buf("swTssb")
        r_pj = sb("r_pj", [128, L, 8], F32)
        r_pjB = Buf("r_pj")
        d_l = sb("d_l", [128, 2], F32)
        s0re_l = sb("s0re_l", [128, 2, 8], F32)
        s0im_l = sb("s0im_l", [128, 2, 8], F32)
        sgn_l = sb("sgn_l", [128, 256], F32)
        sbbc_l = sb("sbbc_l", [128, 2, 128], F32)
        sbbcs_l = sb("sbbcs_l", [128, 2, 64], F32)
        r_l = sb("r_l", [128, 8], F32)
        layB = Buf("laycon")
        gfin = sb("gfin", [128, D], F32)
        constB = Buf("const")
        sprev = sb("sprev", [128, 2, 8], F32)
        sprevB = Buf("sprev")
        ssm_s = sb("ssm_s", [128, 2, 2, 8], F32)
        ssm_sB = Buf("ssm_s")
        nbias = sb("nbias", [128, 32, 2], F32)
        nbiasB = [[Buf("nb%d_%d" % (i, k)) for k in range(2)] for i in range(32)]
        tot = sb("tot", [128, 4], F32)
        totB = [Buf("tot%d" % i) for i in range(4)]
        identb = sb("identb", [128, 128], BF16)
        identf = sb("identf", [128, 128], F32)
        ones = sb("ones", [128, 512], F32)
        zerob = sb("zerob", [128, 512], BF16)
        zcol = sb("zcol", [128, 1], F32)
        Mfull = sb("Mfull", [128, 512], F32)
        Ms = sb("Ms", [32, 2, 64], F32)
        tau1 = sb("tau1", [128, 128], F32)
        tmask = sb("tmask", [128, 128], F32)
        trimask = sb("trimask", [128, 4, 128], F32)
        ckbuf = sb("ckbuf", [128, 4 * PAST], BF16)
        ckB = Buf("ckbuf")
        cvbuf = sb("cvbuf", [128, 4 * PAST], BF16)
        cvB = Buf("cvbuf")
        cstg = sb("cstg", [128, 2048], F32)
        cstgB = Buf("cstg")
        NW = 4
        wring = sb("wring", [128, NW, 4096], BF16)
        wringB = [Buf("wring%d" % i) for i in range(NW)]
        SCRW = 8208
        scr = sb("scr", [128, SCRW], F32)
        scr_grp = []
        sview_cache = {}

        class View:
            pass

        def sview(name, off_words, nwords, dt=F32):
            key = (name, off_words, nwords, dt == BF16)
            if key in sview_cache:
                return sview_cache[key]
            b = Buf(name, scr_grp, off_words, off_words + nwords)
            ap = scr[:, off_words:off_words + nwords]
            if dt == BF16:
                ap = ap.bitcast(BF16)
            sview_cache[key] = (ap, b)
            return ap, b

        def chk(n):
            if cfg.stop <= n:
                P.stopped = True

        P.memset("pool", identf[:], 0.0, [constB])
        P.add("pool", lambda e: e.affine_select(out=identf[:], in_=identf[:], pattern=[[-1, 128]], compare_op=ALU.not_equal,
                                                fill=1.0, base=0, channel_multiplier=1), [], [constB])
        P.cp("pool", identb[:], identf[:], [constB], [constB])
        P.memset("pool", ones[:], 1.0, [constB])
        P.memset("pool", zerob[:], 0.0, [constB])
        P.memset("pool", zcol[:], 0.0, [constB])
        P.memset("pool", Mfull[:], 0.0, [constB])
        P.add("pool", lambda e: e.affine_select(out=Mfull[:, 384:512], in_=Mfull[:, 384:512], pattern=[[-1, 128]],
                                                compare_op=ALU.is_gt, fill=NEG, base=0, channel_multiplier=1), [], [constB])
        P.memset("pool", Ms[:], NEG, [constB])
        for b in range(2):
            P.memset("pool", Ms[:, b, 32 * b:32 * b + 32], 0.0, [constB])
            P.add("pool", lambda e, b=b: e.affine_select(out=Ms[:, b, 32 * b:32 * b + 32], in_=Ms[:, b, 32 * b:32 * b + 32],
                                                         pattern=[[-1, 32]], compare_op=ALU.is_gt, fill=NEG, base=0,
                                                         channel_multiplier=1), [], [constB])
        P.add("pool", lambda e: e.iota(tau1[:], [[1, 128]], base=1, channel_multiplier=0, allow_small_or_imprecise_dtypes=True), [], [constB])
        P.memset("pool", tmask[:], 1.0, [constB])
        P.memset("pool", tmask[:, 0:1], 0.0, [constB])
        P.memset("pool", trimask[:], 1.0, [constB])
        for g in range(4):
            P.add("pool", lambda e, g=g: e.affine_select(out=trimask[:, g, :], in_=trimask[:, g, :], pattern=[[1, 128]],
                                                         compare_op=ALU.is_ge, fill=0.0, base=0, channel_multiplier=-1), [], [constB])
        P.memset("pool", qTz[:], 0.0, qTzB)
        P.memset("pool", nbias[:], 0.0, [b for bb in nbiasB for b in bb])
        P.dma("sp", gfin[:], I["gfin_bc"].ap(), [], [constB])

        chk(0)
        wscr = [[] for _ in range(L)]
        cur_l = [0]
        nmt, nmB = sview("nmt", 6400, 64)
        nft, nfB = sview("nft", 6464, 64)
        P.dma("sp", nmt[:, 0:L * 8], dap(I["nm_pk"], 0, [(L * 8, 128), (1, L * 8)]), [], [nmB])
        P.dma("sp", nft[:, 0:L * 8], dap(I["nf_pk"], 0, [(L * 8, 128), (1, L * 8)]), [], [nfB])
        NST = 2
        stg = [sview("stg%d" % i, i * 3072, 2048) for i in range(NST)]
        stb = [sview("stb%d" % i, i * 3072 + 2048, 1024, BF16) for i in range(NST)]
        conv_i = [0]

        def shaped(ap, dims):
            if len(dims) == 1:
                return ap
            if len(dims) == 2:
                return ap.rearrange("p (a b) -> p a b", b=dims[1])
            return ap.rearrange("p (a b c) -> p a b c", b=dims[1], c=dims[2])

        def conv(src, dst, dims, scale=None, mul=None, mask=None, rows=128):
            n = int(np.prod(dims))
            i = conv_i[0] % NST
            conv_i[0] += 1
            sa, sB = stg[i]
            ba, bB = stb[i]
            sv_ = shaped(sa[0:rows, 0:n], dims)
            bv_ = shaped(ba[0:rows, 0:n], dims)
            P.dma("sp", sv_, src, [], [sB])
            en = "dve" if (conv_i[0] % 2 == 0) else "pool"
            if scale is not None:
                sc_ap, scB = scale
                if en == "pool":
                    P.ts("pool", bv_, sv_, sc_ap, ALU.mult, [sB, scB], [bB], s2=0.0, op1=ALU.add)
                else:
                    P.ts("dve", bv_, sv_, sc_ap, ALU.mult, [sB, scB], [bB])
            elif mul is not None:
                P.ts("dve", bv_, sv_, float(mul), ALU.mult, [sB], [bB])
            elif mask is not None:
                P.tt("dve", bv_, sv_, mask, ALU.mult, [sB, constB], [bB])
            else:
                P.cp(en, bv_, sv_, [sB], [bB])
            wb_ = Buf("wscr")
            wscr[cur_l[0]].append(wb_)
            if isinstance(dst, list):
                for d_ap, sel in dst:
                    P.dma("act", d_ap, sel(bv_), [bB], [wb_])
            else:
                P.dma("act", dst, bv_, [bB], [wb_])

        for l in range(L):
            cur_l[0] = l
            win, wgu = I["w_in"], I["w_gu"]

            def rdst(name, R, off, dims):
                return dap(S[name], (l * 128) * R + off, [(R, 128)] + dims)
            for kc in range(KC):
                sc_m = (nmt[:, l * 8 + kc:l * 8 + kc + 1], nmB)
                sc_f = (nft[:, l * 8 + kc:l * 8 + kc + 1], nfB)
                ro = (l * D + kc * 128) * INW
                for nm, c0, ncol in (("inA", 0, 512), ("inB", 512, 256), ("inQ", 768, 512), ("inK", 1280, 512), ("inV", 1792, 512)):
                    conv(dap(win, ro + c0, [(INW, 128), (1, ncol)]), rdst(nm, 8 * ncol, kc * ncol, [(1, ncol)]), [ncol], scale=sc_m)
                for i in range(3):
                    conv(dap(win, ro + 2304 + i * 1024, [(INW, 128), (128, 8), (1, 128)]),
                         [(rdst("mg%d" % c, 4096, kc * 512 + i * 128, [(1, 128)]), (lambda v, c=c: v[:, c, :])) for c in range(8)],
                         [8, 128], scale=sc_m)
                rg = (l * D + kc * 128) * (2 * FF)
                for up in range(2):
                    for f0, nf in ((0, 16), (16, 6)):
                        conv(dap(wgu, rg + up * FF + f0 * 128, [(2 * FF, 128), (256, nf // 2), (128, 2), (1, 128)]),
                             [(rdst("gu%d" % (f0 // 2 + q), 4096, kc * 512 + up * 256, [(128, 2), (1, 128)]), (lambda v, q=q: v[:, q, :, :])) for q in range(nf // 2)],
                             [nf // 2, 2, 128], scale=sc_f)
            for kk in range(8):
                if kk < 2:
                    src_t, r0 = I["w_ba"], (l * 256 + kk * 128)
                elif kk < 4:
                    src_t, r0 = I["w_bb"], (l * 256 + (kk - 2) * 128)
                else:
                    src_t, r0 = I["w_bc"], (l * 512 + (kk - 4) * 128)
                conv(dap(src_t, r0 * D, [(D, 128), (128, 8), (1, 128)]),
                     [(rdst("mg%d" % c, 4096, kk * 512 + 384, [(1, 128)]), (lambda v, c=c: v[:, c, :])) for c in range(8)],
                     [8, 128])
            for kc in range(KC):
                conv(dap(I["w_out"], (l * D + kc * 128) * D, [(D, 128), (1, D)]),
                     [(rdst("wo%d" % hh, 4096, kc * 512, [(1, 512)]), (lambda v, hh=hh: v[:, hh * 512:(hh + 1) * 512])) for hh in range(2)], [D])
            for f in range(0, NF, 2):
                dsts = []
                for q in range(2):
                    fq = f + q
                    fg = 0 if fq < 8 else (1 if fq < 16 else 2)
                    f0, nf = ((0, 8), (8, 8), (16, 6))[fg]
                    for hh in range(2):
                        dsts.append((rdst("dn%d_%d" % (hh, fg), nf * 512, (fq - f0) * 512, [(1, 512)]), (lambda v, q=q, hh=hh: v[:, q, hh * 512:(hh + 1) * 512])))
                conv(dap(I["w_dn"], (l * FF + f * 128) * D, [(D, 128), (128 * D, 2), (1, D)]), dsts, [2, D])
            conv(dap(I["w_glu"], l * 256 * 512, [(512, 128), (128 * 512, 2), (1, 512)]),
                 dap(S["wglu_bf"], l * 256 * 512, [(512, 128), (128 * 512, 2), (1, 512)]), [2, 512])
            conv(dap(I["swT"], l * 128 * 512, [(512, 128), (1, 512)]),
                 dap(S["swT_bf"], l * 128 * 512, [(512, 128), (1, 512)]), [512],
                 mask=trimask[:].rearrange("p g t -> p (g t)"))
            conv(dap(I["swTs"], l * 64 * 256, [(256, 64), (64, 4), (1, 64)]),
                 dap(S["swTs_bf"], l * 64 * 256, [(256, 64), (64, 4), (1, 64)]), [4, 64],
                 mask=trimask[0:64, :, 0:64], rows=64)
            conv(dap(I["cblk_re"], l * 128 * 1024, [(1024, 128), (1, 1024)]),
                 dap(S["s5mat"], (l * 128) * 4096 + 2 * 1024, [(4096, 128), (1, 1024)]), [1024])
            conv(dap(I["cblk_im"], l * 128 * 1024, [(1024, 128), (1, 1024)]),
                 dap(S["s5mat"], (l * 128) * 4096 + 3 * 1024, [(4096, 128), (1, 1024)]), [1024], mul=-1.0)

        chk(1)
        def s5_common(pref, n, off, a_re, a_im, ldt, srcB):
            T = {}
            for k_, nm in enumerate(["dt", "mag", "ang", "y", "yi", "fr", "sn", "cs"]):
                T[nm] = sview(pref + nm, off + k_ * n, n)
            yi_ap = T["yi"][0].bitcast(I32)
            P.act(T["dt"][0], ldt, AF.Exp, [srcB], [T["dt"][1]])
            P.tt("dve", T["mag"][0], a_re, T["dt"][0], ALU.mult, [srcB, T["dt"][1]], [T["mag"][1]])
            P.act(T["mag"][0], T["mag"][0], AF.Exp, [T["mag"][1]], [T["mag"][1]])
            P.tt("dve", T["ang"][0], a_im, T["dt"][0], ALU.mult, [srcB, T["dt"][1]], [T["ang"][1]])

            def frac(dst, src, add):
                P.ts("dve", T["y"][0], src[0], 1.0 / TWO_PI, ALU.mult, [src[1]], [T["y"][1]], s2=add, op1=ALU.add)
                P.cp("dve", yi_ap, T["y"][0], [T["y"][1]], [T["yi"][1]])
                P.cp("dve", dst[0], yi_ap, [T["yi"][1]], [dst[1]])
                P.tt("dve", dst[0], T["y"][0], dst[0], ALU.subtract, [T["y"][1], dst[1]], [dst[1]])
                P.ts("dve", dst[0], dst[0], 0.5, ALU.min, [dst[1]], [dst[1]], s2=-0.5, op1=ALU.max)
            T["frac"] = frac
            return T

        for l in range(L):
            pj_src, pjB = sview("pjsrc", 6600, 3 * 8)
            P.dma("sp", pj_src[:, 0:8], dap(I["a_re_pj"], l * 8, [(L * 8, 128), (1, 8)]), [], [pjB])
            P.dma("sp", pj_src[:, 8:16], dap(I["a_im_pj"], l * 8, [(L * 8, 128), (1, 8)]), [], [pjB])
            P.dma("sp", pj_src[:, 16:24], dap(I["ldt_pj"], l * 8, [(L * 8, 128), (1, 8)]), [], [pjB])
            Tp = s5_common("pj_", 8, 6700, pj_src[:, 0:8], pj_src[:, 8:16], pj_src[:, 16:24], pjB)
            P.cp("dve", r_pj[:, l, :], Tp["mag"][0], [Tp["mag"][1]], [r_pjB])
            wb_ = Buf("wscr_rpj")
            wscr[l].append(wb_)
            P.dma("act", dap(S["rpj_d"], l * 1024, [(8, 128), (1, 8)]), r_pj[:, l, :], [r_pjB], [wb_])
            Tp["frac"](Tp["fr"], Tp["ang"], 0.0)
            ph, phB = sview("ph", 0, 1024)
            tb, tbB = sview("tb", 1024, 3072)
            yy, yyB = sview("yy", 4096, 1024)
            yyi, yyiB = sview("yyi", 5120, 1024)
            ph3 = ph.rearrange("p (j t) -> p j t", t=128)
            for j in range(8):
                P.ts("dve", ph3[:, j, :], tau1[:], Tp["fr"][0][:, j:j + 1], ALU.mult, [constB, Tp["fr"][1]], [phB])
                P.ts("dve", tb[:, 2048 + j * 128:2048 + (j + 1) * 128], tmask[:], r_pj[:, l, j:j + 1], ALU.mult,
                     [constB, r_pjB], [tbB])
            for which, add in ((1, 0.0), (0, 0.25)):
                dst = tb[:, which * 1024:(which + 1) * 1024]
                P.ts("dve", yy, ph, 1.0, ALU.mult, [phB], [yyB], s2=add, op1=ALU.add)
                P.cp("dve", yyi.bitcast(I32), yy, [yyB], [yyiB])
                P.cp("dve", dst, yyi.bitcast(I32), [yyiB], [tbB])
                P.tt("dve", dst, yy, dst, ALU.subtract, [yyB, tbB], [tbB])
                P.ts("dve", dst, dst, 0.5, ALU.min, [tbB], [tbB], s2=-0.5, op1=ALU.max)
                P.act(dst, dst, AF.Sin, [tbB], [tbB], scale=TWO_PI)
            wb_ = Buf("wscr_tab")
            wscr[l].append(wb_)
            P.dma("act", dap(S["s5tab"], l * 128 * 3072, [(3072, 128), (1, 3072)]), tb, [tbB], [wb_])
            rw, rwB = sview("rw", 0, 3072)
            P.dma("sp", rw[:, 0:1024], dap(I["a_re_row"], l * 1024, [(0, 128), (1, 1024)]), [], [rwB])
            P.dma("sp", rw[:, 1024:2048], dap(I["a_im_row"], l * 1024, [(0, 128), (1, 1024)]), [], [rwB])
            P.dma("sp", rw[:, 2048:3072], dap(I["ldt_row"], l * 1024, [(0, 128), (1, 1024)]), [], [rwB])
            a_re, a_im = rw[:, 0:1024], rw[:, 1024:2048]
            t0, t0B = sview("rt0", 3072, 1024)
            t1, t1B = sview("rt1", 4096, 1024)
            t2, t2B = sview("rt2", 5120, 1024)
            t3, t3B = sview("rt3", 6144, 1024)
            dt_ = rw[:, 2048:3072]
            P.act(dt_, dt_, AF.Exp, [rwB], [rwB])
            P.tt("dve", t1, a_re, dt_, ALU.mult, [rwB], [t1B])
            P.act(t1, t1, AF.Exp, [t1B], [t1B])
            P.tt("dve", t2, a_im, dt_, ALU.mult, [rwB], [t2B])

            def fracrow(dst, dstB, add):
                P.ts("dve", t0, t2, 1.0 / TWO_PI, ALU.mult, [t2B], [t0B], s2=add, op1=ALU.add)
                P.cp("dve", dst.bitcast(I32), t0, [t0B], [dstB])
                P.cp("dve", t3, dst.bitcast(I32), [dstB], [t3B])
                P.tt("dve", dst, t0, t3, ALU.subtract, [t0B, t3B], [dstB])
                P.ts("dve", dst, dst, 0.5, ALU.min, [dstB], [dstB], s2=-0.5, op1=ALU.max)
                P.act(dst, dst, AF.Sin, [dstB], [dstB], scale=TWO_PI)
            fracrow(dt_, rwB, 0.0)
            cs_, csB = sview("rcs", 6144, 1024)
            P.ts("dve", t0, t2, 1.0 / TWO_PI, ALU.mult, [t2B], [t0B], s2=0.25, op1=ALU.add)
            P.cp("dve", cs_.bitcast(I32), t0, [t0B], [csB])
            P.cp("dve", t2, cs_.bitcast(I32), [csB], [t2B])
            P.tt("dve", cs_, t0, t2, ALU.subtract, [t0B, t2B], [csB])
            P.ts("dve", cs_, cs_, 0.5, ALU.min, [csB], [csB], s2=-0.5, op1=ALU.max)
            P.act(cs_, cs_, AF.Sin, [csB], [csB], scale=TWO_PI)
            P.tt("dve", cs_, cs_, t1, ALU.mult, [csB, t1B], [csB])
            P.tt("dve", dt_, dt_, t1, ALU.mult, [rwB, t1B], [rwB])
            P.ts("dve", cs_, cs_, -1.0, ALU.add, [csB], [csB])
            P.tt("dve", t1, a_re, a_re, ALU.mult, [rwB], [t1B])
            P.tt("dve", t0, a_im, a_im, ALU.mult, [rwB], [t0B])
            P.tt("dve", t1, t1, t0, ALU.add, [t1B, t0B], [t1B])
            P.add("dve", lambda e, t1=t1: e.reciprocal(out=t1, in_=t1), [t1B], [t1B])
            P.tt("dve", t0, cs_, a_re, ALU.mult, [csB, rwB], [t0B])
            P.tt("dve", t2, dt_, a_im, ALU.mult, [rwB], [t2B])
            P.tt("dve", t0, t0, t2, ALU.add, [t0B, t2B], [t0B])
            P.tt("dve", t0, t0, t1, ALU.mult, [t0B, t1B], [t0B])
            P.tt("dve", t2, dt_, a_re, ALU.mult, [rwB], [t2B])
            P.tt("dve", cs_, cs_, a_im, ALU.mult, [csB, rwB], [csB])
            P.tt("dve", t2, t2, cs_, ALU.subtract, [t2B, csB], [t2B])
            P.tt("dve", t2, t2, t1, ALU.mult, [t2B, t1B], [t2B])
            P.dma("sp", rw[:, 0:1024], dap(I["bblk_re"], l * 128 * 1024, [(1024, 128), (1, 1024)]), [], [rwB])
            P.dma("sp", rw[:, 1024:2048], dap(I["bblk_im"], l * 128 * 1024, [(1024, 128), (1, 1024)]), [], [rwB])
            b_re, b_im = rw[:, 0:1024], rw[:, 1024:2048]
            bo, boB = sview("rbo", 2048, 1024, BF16)
            P.tt("dve", t1, t0, b_re, ALU.mult, [t0B, rwB], [t1B])
            P.tt("dve", cs_, t2, b_im, ALU.mult, [t2B, rwB], [csB])
            P.tt("dve", bo[:, 0:1024], t1, cs_, ALU.subtract, [t1B, csB], [boB])
            P.tt("dve", t1, t0, b_im, ALU.mult, [t0B, rwB], [t1B])
            P.tt("dve", cs_, t2, b_re, ALU.mult, [t2B, rwB], [csB])
            P.tt("dve", bo[:, 1024:2048], t1, cs_, ALU.add, [t1B, csB], [boB])
            wb_ = Buf("wscr_mat")
            wscr[l].append(wb_)
            P.dma("act", dap(S["s5mat"], (l * 128) * 4096, [(4096, 128), (1, 2048)]), bo, [boB], [wb_])
        chk(2)
        khB = [Buf("kh%d" % t) for t in range(NT)]
        vhB = [Buf("vh%d" % t) for t in range(NT)]
        outB = []

        def obuf(name):
            b = Buf(name)
            outB.append(b)
            return b

        class Item:
            pass

        class Stream:
            def __init__(self):
                self.items = []
                self.ip = 0
                self.cp_ = 0
                self.ring_issued = 0
                self.ring_released = 0

            def push(self, name, ring, dmas, q="sp", dstB=None):
                it = Item()
                it.name, it.ring, it.dmas, it.q, it.dstB, it.slot = name, ring, dmas, q, dstB, None
                self.items.append(it)

            def pump(self):
                while self.ip < len(self.items):
                    it = self.items[self.ip]
                    if it.ring:
                        if self.ring_issued - self.ring_released >= NW:
                            break
                        it.slot = self.ring_issued % NW
                        self.ring_issued += 1
                        for dst, src, rB in it.dmas(it.slot):
                            P.dma(it.q, dst, src, rB, [wringB[it.slot]])
                    else:
                        for dst, src, rB in it.dmas(None):
                            P.dma(it.q, dst, src, rB, [it.dstB])
                    self.ip += 1

            def get(self, name):
                it = self.items[self.cp_]
                assert it.name == name, (it.name, name)
                if self.ip <= self.cp_:
                    self.pump()
                assert self.ip > self.cp_, "stream stalled at " + name
                self.cp_ += 1
                return it

            def release(self, it):
                if it.ring:
                    self.ring_released += 1
                self.pump()

        ST_ = Stream()

        def wslot(slot, dims):
            n = int(np.prod(dims))
            return shaped(wring[:, slot, 0:n], dims)

        def push_layer_items(kind, t):
            W = []
            pre = "%s%d_" % (kind, t)
            ST_.push(pre + "s5tab", False, lambda s: [(s5tab[:].rearrange("p a b -> p (a b)"),
                                                      (lambda li: dap(S["s5tab"], li * (128 * 3072), [(3072, 128), (1, 3072)])), W)], dstB=s5tabB)
            ST_.push(pre + "s5mat", False, lambda s: [(s5mat[:].rearrange("p a b -> p (a b)"),
                                                      (lambda li: dap(S["s5mat"], li * (128 * 4096), [(4096, 128), (1, 4096)])), W)], dstB=s5matB)
            if kind == "p":
                ST_.push(pre + "swT", False, lambda s: [(swTsb[:], (lambda li: dap(S["swT_bf"], li * (128 * 512), [(512, 128), (1, 512)])), W)], dstB=swTsbB)
            else:
                ST_.push(pre + "swT", False, lambda s: [(swTssb[:], (lambda li: dap(S["swTs_bf"], li * (64 * 256), [(256, 64), (1, 256)])), W)], dstB=swTssbB)
            for nm, c0, ncol in (("A", 0, 512), ("B", 512, 256), ("Q", 768, 512), ("K", 1280, 512), ("V", 1792, 512)):
                ST_.push(pre + "in" + nm, True, lambda s, nm=nm, ncol=ncol: [
                    (wslot(s, [8 * ncol]), (lambda li, nm=nm, ncol=ncol: dap(S["in" + nm], li * (128 * 8 * ncol), [(8 * ncol, 128), (1, 8 * ncol)])), W)])
            ST_.push(pre + "glu", True, lambda s: [
                (wslot(s, [2, 512]), (lambda li: dap(S["wglu_bf"], li * (256 * 512), [(512, 128), (128 * 512, 2), (1, 512)])), W)])
            if kind == "p":
                for hf, kt in [(hf_, kt_) for hf_ in range(2) for kt_ in range(t - 1, -1, -1)]:
                    ST_.push(pre + "kv%d_%d" % (hf, kt), True, lambda s, kt=kt: [
                        (shaped(wring[:, s, 0:2048], [4, 512]), dap(S["kT_hist"], kt * 512, [(SEQ, 128), (128 * SEQ, 4), (1, 512)]), [khB[kt]]),
                        (shaped(wring[:, s, 2048:4096], [4, 512]), dap(S["v_hist"], (kt * 512) * 512, [(512, 128), (128 * 512, 4), (1, 512)]), [vhB[kt]])])
            for c in range(8):
                ST_.push(pre + "mg%d" % c, True, lambda s, c=c: [
                    (wslot(s, [4096]), (lambda li, c=c: dap(S["mg%d" % c], li * (128 * 4096), [(4096, 128), (1, 4096)])), W)])
            for hh in range(2):
                ST_.push(pre + "wo%d" % hh, True, lambda s, hh=hh: [
                    (wslot(s, [4096]), (lambda li, hh=hh: dap(S["wo%d" % hh], li * (128 * 4096), [(4096, 128), (1, 4096)])), W)])
            for pc in range(11):
                ST_.push(pre + "gu%d" % pc, True, lambda s, pc=pc: [
                    (wslot(s, [4096]), (lambda li, pc=pc: dap(S["gu%d" % pc], li * (128 * 4096), [(4096, 128), (1, 4096)])), W)])
            for hh in range(2):
                for fg, (f0, nf) in enumerate(((0, 8), (8, 8), (16, 6))):
                    ST_.push(pre + "dn%d_%d" % (hh, fg), True, lambda s, hh=hh, fg=fg, nf=nf: [
                        (wslot(s, [nf * 512]), (lambda li, hh=hh, fg=fg, nf=nf: dap(S["dn%d_%d" % (hh, fg)], li * (128 * nf * 512), [(nf * 512, 128), (1, nf * 512)])), W)])

        tiles = [("p", t) for t in range(NT)] + ([("s", 0)] if cfg.with_sample else [])
        for kind, t in tiles:
            push_layer_items(kind, t)

        evq = [0]

        def evac_eng():
            evq[0] += 1
            return "act" if evq[0] % 2 == 0 else "dve"

        def norm_to_xnT(ST, nst):
            junk, junkB = sview("junk", 0, 512, BF16)
            for s in range(nst):
                xnv, xnB = sview("xn%d" % (s % 2), 512 + (s % 2) * 512, 512, BF16)
                c_ss, c_sq, c_rs = s, 4 + s, 8 + s
                P.act(junk[0:ST, :], h[0:ST, s, :], AF.Square, [hB[s]], [junkB, smallB[c_ss]], accum=small[0:ST, c_ss:c_ss + 1])
                P.act(small[0:ST, c_sq:c_sq + 1], small[0:ST, c_ss:c_ss + 1], AF.Sqrt, [smallB[c_ss]], [smallB[c_sq]], bias=EPS, scale=1.0 / D)
                P.add("dve", lambda e, o=small[0:ST, c_rs:c_rs + 1], i=small[0:ST, c_sq:c_sq + 1]: e.reciprocal(out=o, in_=i), [smallB[c_sq]], [smallB[c_rs]])
                P.ts("pool", xnv[0:ST, :], h[0:ST, s, :], small[0:ST, c_rs:c_rs + 1], ALU.mult, [hB[s], smallB[c_rs]], [xnB], s2=0.0, op1=ALU.add)
                bk = nextbank()
                bkb = banksbf[bk]
                for kc in range(8):
                    P.tr(bkb[:, kc * 128:kc * 128 + ST], xnv[0:ST, kc * 128:(kc + 1) * 128], identb[0:ST, 0:ST], [xnB, constB], [bankB[bk]])
                P.cp(evac_eng(), xnT[:, :, s * ST:(s + 1) * ST], bkb.rearrange("p (k t) -> p k t", t=128)[:, :, 0:ST], [bankB[bk]], [xnTB[s]])

        def s5_segment(l, col0, n, st_re, st_im, stB, wg, wgB):
            xre, xreB = sview("xre", 0, 1024)
            xim, ximB = sview("xim", 1024, 1024)
            wre, wreB = sview("wre", 2048, 1024)
            wim, wimB = sview("wim", 3072, 1024)
            tt_ = [sview("s5t%d" % k, 4096 + k * 512, 512) for k in range(4)]
            srb, srbB = sview("srb", 6144, 512, BF16)
            sib, sibB = sview("sib", 6656, 512, BF16)
            m = 8 * n

            def v3(ap, j0=0, nj=8):
                return ap[:, 0:m].rearrange("p (j t) -> p j t", t=n)[:, j0:j0 + nj, :]

            def tab3(k, j0, nj):
                return s5tab[:, k, :].rearrange("p (j t) -> p j t", t=128)[:, j0:j0 + nj, 0:n]
            bks = [nextbank() for _ in range(4)]
            for ri in range(2):
                for j in range(8):
                    bk = bks[ri * 2 + j // 4]
                    jj = j % 4
                    P.mm(banks[bk][:, jj * 128:jj * 128 + n], s5mat[:, ri, j * 128:(j + 1) * 128], uabf[:, j // 4, col0:col0 + n],
                         True, True, [s5matB, uabfB], [bankB[bk]])
            for hf in range(2):
                bre = banks[bks[hf]][:, :].rearrange("p (j t) -> p j t", t=128)[:, :, 0:n]
                bim = banks[bks[2 + hf]][:, :].rearrange("p (j t) -> p j t", t=128)[:, :, 0:n]
                breB, bimB = bankB[bks[hf]], bankB[bks[2 + hf]]
                Ec, Es = tab3(0, 4 * hf, 4), tab3(1, 4 * hf, 4)
                tv = [tt_[k][0][:, 0:4 * n].rearrange("p (j t) -> p j t", t=n) for k in range(4)]
                tB = [tt_[k][1] for k in range(4)]
                P.tt("dve", tv[0], bre, Ec, ALU.mult, [breB, s5tabB], [tB[0]])
                P.tt("dve", tv[1], bim, Es, ALU.mult, [bimB, s5tabB], [tB[1]])
                P.tt("pool", v3(xre, 4 * hf, 4), tv[0], tv[1], ALU.add, [tB[0], tB[1]], [xreB])
                P.tt("dve", tv[2], bim, Ec, ALU.mult, [bimB, s5tabB], [tB[2]])
                P.tt("dve", tv[3], bre, Es, ALU.mult, [breB, s5tabB], [tB[3]])
                P.tt("pool", v3(xim, 4 * hf, 4), tv[2], tv[3], ALU.subtract, [tB[2], tB[3]], [ximB])
            P.tt("dve", small[:, 16:24], r_l[:, :], st_re, ALU.mult, [layB, stB], [smallB[16]])
            P.tt("dve", v3(xre)[:, :, 0], v3(xre)[:, :, 0], small[:, 16:24], ALU.add, [xreB, smallB[16]], [xreB])
            P.tt("dve", small[:, 24:32], r_l[:, :], st_im, ALU.mult, [layB, stB], [smallB[24]])
            P.tt("dve", v3(xim)[:, :, 0], v3(xim)[:, :, 0], small[:, 24:32], ALU.add, [ximB, smallB[24]], [ximB])
            if n == 128:
                rt, rtR = s5tab[:, 2, :], [s5tabB]
            else:
                rt = tt_[0][0][:, 0:m]
                P.cp("pool", rt.rearrange("p (j t) -> p j t", t=n), tab3(2, 0, 8), [s5tabB], [tt_[0][1]])
                rtR = [tt_[0][1]]
            P.add("dve", lambda e, o=wre[:, 0:m], d0=rt, d1=xre[:, 0:m]: e.tensor_tensor_scan(out=o, data0=d0, data1=d1, initial=0.0, op0=ALU.mult, op1=ALU.add),
                  rtR + [xreB], [wreB])
            P.add("dve", lambda e, o=wim[:, 0:m], d0=rt, d1=xim[:, 0:m]: e.tensor_tensor_scan(out=o, data0=d0, data1=d1, initial=0.0, op0=ALU.mult, op1=ALU.add),
                  rtR + [ximB], [wimB])
            for hf in range(2):
                Ec, Es = tab3(0, 4 * hf, 4), tab3(1, 4 * hf, 4)
                tv = [tt_[k][0][:, 0:4 * n].rearrange("p (j t) -> p j t", t=n) for k in range(4)]
                tB = [tt_[k][1] for k in range(4)]
                wr, wi = v3(wre, 4 * hf, 4), v3(wim, 4 * hf, 4)
                P.tt("dve", tv[0], Ec, wr, ALU.mult, [s5tabB, wreB], [tB[0]])
                P.tt("pool", tv[1], Es, wi, ALU.mult, [s5tabB, wimB], [tB[1]])
                P.tt("dve", v3(srb, 4 * hf, 4), tv[0], tv[1], ALU.subtract, [tB[0], tB[1]], [srbB])
                P.tt("pool", tv[2], Es, wr, ALU.mult, [s5tabB, wreB], [tB[2]])
                P.tt("dve", tv[3], Ec, wi, ALU.mult, [s5tabB, wimB], [tB[3]])
                P.tt("pool", v3(sib, 4 * hf, 4), tv[2], tv[3], ALU.add, [tB[2], tB[3]], [sibB])
            EcL, EsL = tab3(0, 0, 8)[:, :, n - 1], tab3(1, 0, 8)[:, :, n - 1]
            wrL, wiL = v3(wre)[:, :, n - 1], v3(wim)[:, :, n - 1]
            P.tt("dve", small[:, 32:40], EcL, wrL, ALU.mult, [s5tabB, wreB], [smallB[32]])
            P.tt("dve", small[:, 40:48], EsL, wiL, ALU.mult, [s5tabB, wimB], [smallB[40]])
            P.tt("dve", st_re, small[:, 32:40], small[:, 40:48], ALU.subtract, [smallB[32], smallB[40]], [stB])
            P.tt("dve", small[:, 32:40], EsL, wrL, ALU.mult, [s5tabB, wreB], [smallB[32]])
            P.tt("dve", small[:, 40:48], EcL, wiL, ALU.mult, [s5tabB, wimB], [smallB[40]])
            P.tt("dve", st_im, small[:, 32:40], small[:, 40:48], ALU.add, [smallB[32], smallB[40]], [stB])
            bky = nextbank()
            for mm_ in range(2):
                for j in range(4 * mm_, 4 * mm_ + 4):
                    P.mm(banks[bky][:, mm_ * 128:mm_ * 128 + n], s5mat[:, 2, j * 128:(j + 1) * 128], v3(srb)[:, j, :],
                         j == 4 * mm_, False, [s5matB, srbB], [bankB[bky]])
                    P.mm(banks[bky][:, mm_ * 128:mm_ * 128 + n], s5mat[:, 3, j * 128:(j + 1) * 128], v3(sib)[:, j, :],
                         False, j == 4 * mm_ + 3, [s5matB, sibB], [bankB[bky]])
            yv, yvB = tt_[0]
            y2, y2B = tt_[1]
            sg, sgB = tt_[2]
            gl, glB = sview("glbf", 4096 + 3 * 512, 256, BF16)
            yv3 = yv[:, 0:2 * n].rearrange("p (a t) -> p a t", t=n)
            y23 = y2[:, 0:2 * n].rearrange("p (a t) -> p a t", t=n)
            sg3 = sg[:, 0:2 * n].rearrange("p (a t) -> p a t", t=n)
            gl3 = gl[:, 0:2 * n].rearrange("p (a t) -> p a t", t=n)
            for mm_ in range(2):
                P.stt(yv3[:, mm_, :], uaT[:, mm_, col0:col0 + n], d_l[:, mm_:mm_ + 1], banks[bky][:, mm_ * 128:mm_ * 128 + n],
                      ALU.mult, ALU.add, [uaTB, layB, bankB[bky]], [yvB])
            P.tt("dve", y23, yv3, yv3, ALU.mult, [yvB], [y2B])
            P.ts("dve", y23, y23, 0.044715, ALU.mult, [y2B], [y2B], s2=1.0, op1=ALU.add)
            P.tt("dve", y23, y23, yv3, ALU.mult, [y2B, yvB], [y2B])
            P.act(sg3, y23, AF.Sigmoid, [y2B], [sgB], scale=1.5957691216057308)
            P.tt("dve", gl3, yv3, sg3, ALU.mult, [yvB, sgB], [glB])
            bkz = nextbank()
            for oc in range(4):
                for kc in range(2):
                    P.mm(banks[bkz][:, oc * 128:oc * 128 + n], wg[:, kc, oc * 128:(oc + 1) * 128], gl3[:, kc, :], kc == 0, kc == 1,
                         [wgB, glB], [bankB[bkz]])
            z3 = banks[bkz][:, :].rearrange("p (a t) -> p a t", t=128)
            P.act(sg3, z3[:, 2:4, 0:n], AF.Sigmoid, [bankB[bkz]], [sgB])
            P.tt("dve", yaT[:, :, col0:col0 + n], z3[:, 0:2, 0:n], sg3, ALU.mult, [bankB[bkz], sgB], [yaTB])

        class Blk:
            pass

        def run_attention(blocks, Kmax):
            eV = [sview("at_e%d" % i, i * 1024, 1024) for i in range(2)]
            sp, spB = sview("at_sp", 2048, 1024)
            pf, pfB = sview("at_pf", 3072, 1040)
            arg, argB = sview("at_arg", 4112, 1024)
            pbV = [sview("at_pb%d" % i, 5136 + i * 512, 512, BF16) for i in range(2)]
            ptV = [sview("at_pt%d" % i, 6160 + i * 512, 512, BF16) for i in range(2)]
            zmVV = [sview("at_zm%d" % i, 7184 + i * 512, 512) for i in range(2)]
            pf3 = pf.rearrange("p (g t) -> p g t", t=520)
            P.memset("pool", pf3[:, :, 0:1], 0.0, [pfB])
            nB = len(blocks)

            def zbuf(i):
                k = i % 2
                return zpair[k], [bankB[2 * k], bankB[2 * k + 1]]

            def S1(i):
                b = blocks[i]
                zt, zB = zbuf(i)
                for g, sub in enumerate(b.subs):
                    P.mm(zt[0:b.nq, g * 512:g * 512 + b.w], sub[0], b.kT, True, True, b.qkB, zB)
                if b.mask is not None:
                    zmV = zmVV[i % 2]
                    P.tt("dve", zmV[0][0:b.nq, 0:b.w], zt[0:b.nq, 0:b.w], b.mask, ALU.add, zB + [constB], [zmV[1]])

            def S2(i):
                b = blocks[i]
                zt, zB = zbuf(i)
                e, eB = eV[i % 2]
                G = len(b.subs)
                if G == 2:
                    P.act(e[0:b.nq, 0:1024], zt[0:b.nq, 0:1024], AF.Exp, zB, [eB])
                    P.act(sp[0:b.nq, 0:1024], e[0:b.nq, 0:1024], AF.Ln, [eB], [spB], bias=1.0)
                    return
                if b.mask is not None:
                    zmV = zmVV[i % 2]
                    zsrc, zR = zmV[0][0:b.nq, 0:b.w], [zmV[1]]
                else:
                    zsrc, zR = zt[0:b.nq, 0:b.w], zB
                P.act(e[0:b.nq, 0:b.w], zsrc, AF.Exp, zR, [eB])
                ts_ = i % 4
                P.act(sp[0:b.nq, 0:b.w], e[0:b.nq, 0:b.w], AF.Ln, [eB], [spB, totB[ts_]], bias=1.0, accum=tot[0:b.nq, ts_:ts_ + 1])
                if b.first:
                    old, oldB = zcol[0:b.nq, 0:1], constB
                else:
                    old, oldB = nbias[0:b.nq, b.idx0, 1 - b.par:2 - b.par], nbiasB[b.idx0][1 - b.par]
                P.tt("dve", nbias[0:b.nq, b.idx0, b.par:b.par + 1], old, tot[0:b.nq, ts_:ts_ + 1], ALU.subtract,
                     [oldB, totB[ts_]], [nbiasB[b.idx0][b.par]])

            def S3(i):
                b = blocks[i]
                e, eB = eV[i % 2]
                pb, pbB = pbV[i % 2]
                G = len(b.subs)
                for g in range(G):
                    P.add("dve", lambda e_, o=pf3[0:b.nq, g, 1:b.w + 1], d0=ones[0:b.nq, 0:b.w], d1=sp[0:b.nq, g * 512:g * 512 + b.w]:
                          e_.tensor_tensor_scan(out=o, data0=d0, data1=d1, initial=0.0, op0=ALU.mult, op1=ALU.add), [constB, spB], [pfB])
                if G == 2:
                    nbw = [nbiasB[b.idx0][b.par], nbiasB[b.idx0 + 1][b.par]]
                    nbr = [nbiasB[b.idx0][1 - b.par], nbiasB[b.idx0 + 1][1 - b.par]]
                    P.tt("dve", nbias[0:b.nq, b.idx0:b.idx0 + 2, b.par], nbias[0:b.nq, b.idx0:b.idx0 + 2, 1 - b.par], pf3[0:b.nq, 0:2, 512],
                         ALU.subtract, nbr + [pfB], nbw)
                    for g in range(2):
                        P.act(arg[0:b.nq, g * 512:(g + 1) * 512], pf3[0:b.nq, g, 0:512], AF.Exp, [pfB, nbw[g]], [argB],
                              bias=nbias[0:b.nq, b.idx0 + g, b.par:b.par + 1])
                    P.tt("pool", pb[0:b.nq, 0:1024], e[0:b.nq, 0:1024], arg[0:b.nq, 0:1024], ALU.mult, [eB, argB], [pbB])
                else:
                    P.act(arg[0:b.nq, 0:b.w], pf3[0:b.nq, 0, 0:b.w], AF.Exp, [pfB, nbiasB[b.idx0][b.par]], [argB], bias=nbias[0:b.nq, b.idx0, b.par:b.par + 1])
                    P.tt("pool", pb[0:b.nq, 0:b.w], e[0:b.nq, 0:b.w], arg[0:b.nq, 0:b.w], ALU.mult, [eB, argB], [pbB])

            def S4(i):
                b = blocks[i]
                pb, pbB = pbV[i % 2]
                pt, ptB = ptV[i % 2]
                bk = 4 + (i % 2)
                bkb = banksbf[bk]
                G = len(b.subs)
                vs = b.vs
                for g in range(G):
                    for j, (vap, K) in enumerate(vs):
                        P.tr(bkb[0:K, g * 512 + j * 128:g * 512 + j * 128 + b.nq], pb[0:b.nq, g * 512 + j * 128:g * 512 + j * 128 + K],
                             identb[0:b.nq, 0:b.nq], [pbB, constB], [bankB[bk]])
                ncol = 1024 if G == 2 else len(vs) * 128
                P.cp("act" if (G == 2 and i % 2 == 0) else "dve", pt[0:Kmax, 0:ncol], bkb[0:Kmax, 0:ncol], [bankB[bk]], [ptB])

            def S5(i):
                b = blocks[i]
                pt, ptB = ptV[i % 2]
                vs = b.vs
                nsub = len(vs)
                for g, sub in enumerate(b.subs):
                    for j, (vap, K) in enumerate(vs):
                        P.mm(sub[1], pt[0:K, g * 512 + j * 128:g * 512 + j * 128 + b.nq], vap, False, b.last and j == nsub - 1, [ptB] + b.vB, [sub[2]])
                if b.done is not None:
                    b.done()

            for it in range(nB + 4):
                if 0 <= it - 4 < nB:
                    S5(it - 4)
                if 0 <= it - 3 < nB:
                    S4(it - 3)
                if 0 <= it - 2 < nB:
                    S3(it - 2)
                if 0 <= it - 1 < nB:
                    S2(it - 1)
                if it < nB:
                    if blocks[it].pre is not None:
                        blocks[it].pre()
                    S1(it)

        def layer_tile(kind, t, l):
            prompt = kind == "p"
            T = 512 if prompt else 64
            ST = 128 if prompt else 64
            nst = T // ST
            pre = "%s%d_" % (kind, t)
            xall = xnTB[0:nst]
            norm_to_xnT(ST, nst)
            chk(3)
            it_tab = ST_.get(pre + "s5tab")
            it_mat = ST_.get(pre + "s5mat")
            it_sw = ST_.get(pre + "swT")
            chk(3.05)
            itA = ST_.get(pre + "inA")
            chk(3.07)
            wA = wslot(itA.slot, [8, 512])
            wAB = wringB[itA.slot]
            for ct in range(4):
                bk = nextbank()
                for kc in range(8):
                    P.mm(banks[bk][:, 0:T], wA[:, kc, ct * 128:(ct + 1) * 128], xnT[:, kc, 0:T], kc == 0, kc == 7, [wAB] + xall, [bankB[bk]])
                if ct < 2:
                    P.cp("act", uaT[:, ct, 0:T], banks[bk][:, 0:T], [bankB[bk]], [uaTB])
                    P.cp("dve", uabf[:, ct, 0:T], banks[bk][:, 0:T], [bankB[bk]], [uabfB])
                else:
                    P.cp(evac_eng(), ubT[:, ct - 2, 0:T], banks[bk][:, 0:T], [bankB[bk]], [ubTB])
            ST_.release(itA)
            chk(3.1)
            itB = ST_.get(pre + "inB")
            wB_ = wslot(itB.slot, [8, 256])
            vb_banks = []
            for s in range(nst):
                bk = nextbank((2, 3, 4, 5))
                vb_banks.append(bk)
                for kc in range(8):
                    P.mm(banks[bk][0:ST, 0:256], xnT[:, kc, s * ST:(s + 1) * ST], wB_[:, kc, :], kc == 0, kc == 7, [wringB[itB.slot], xnTB[s]], [bankB[bk]])
            ST_.release(itB)
            chk(3.2)
            WT = swTsb[:].rearrange("p (g t) -> p g t", t=128) if prompt else swTssb[:].rearrange("p (g t) -> p g t", t=64)
            WTB = swTsbB if prompt else swTssbB
            sbias = sbbc_l if prompt else sbbcs_l
            sgt = [sview("sg_t%d" % k, 6144 + k * 128, 128) for k in range(2)]
            for s in range(nst):
                bk = vb_banks[s]
                vb = banks[bk][0:ST, 0:256]
                P.add("dve", lambda e, o=small[0:ST, 48:54], i=vb: e.bn_stats(out=o, in_=i), [bankB[bk]], [smallB[48]])
                P.add("dve", lambda e, o=small[0:ST, 54:56], i=small[0:ST, 48:54]: e.bn_aggr(out=o, in_=i), [smallB[48]], [smallB[54]])
                P.act(small[0:ST, 56:57], small[0:ST, 55:56], AF.Sqrt, [smallB[54]], [smallB[56]], bias=EPS, scale=1.0)
                P.add("dve", lambda e, o=small[0:ST, 57:58], i=small[0:ST, 56:57]: e.reciprocal(out=o, in_=i), [smallB[56]], [smallB[57]])
                P.ts("dve", small[0:ST, 58:59], small[0:ST, 54:55], small[0:ST, 57:58], ALU.mult, [smallB[54], smallB[57]], [smallB[58]], s2=-1.0, op1=ALU.mult)
                P.act(vn0[0:ST, :], vb, AF.Identity, [bankB[bk], smallB[57], smallB[58]], [vn0B], bias=small[0:ST, 58:59], scale=small[0:ST, 57:58])
                sl = s % 2
                if prompt:
                    P.tt("pool", vnbf[0:ST, sl, :], vn0[0:ST, :], sgn_l[0:ST, :], ALU.mult, [vn0B, layB], [vnbfB[sl]])
                else:
                    P.tt("pool", vnf[0:ST, :], vn0[0:ST, :], sgn_l[0:ST, :], ALU.mult, [vn0B, layB], [vnfB])
                    P.cp("pool", vnbf[0:ST, sl, :], vnf[0:ST, :], [vnfB], [vnbfB[sl]])
                    P.dma("act", dap(S["svb_s"], 0, [(256, 64), (1, 256)]), vnf[0:ST, :], [vnfB], [obuf("svb")])
                bkm = nextbank((0, 1, 6, 7))
                for g in range(4):
                    P.mm(banks[bkm][64 * (g % 2):64 * (g % 2) + 64, (g // 2) * 128:(g // 2) * 128 + ST],
                         vnbf[0:ST, sl, g * 64:(g + 1) * 64], WT[0:ST, g, 0:ST], True, True, [vnbfB[sl], WTB], [bankB[bkm]])
                for i2 in range(2):
                    tmp, tmpB = sgt[i2]
                    P.tt("dve", tmp[:, 0:ST], banks[bkm][:, i2 * 128:i2 * 128 + ST], sbias[:, i2, 0:ST], ALU.add, [bankB[bkm], layB], [tmpB])
                    P.tt("dve", ybT[:, i2, s * ST:(s + 1) * ST], tmp[:, 0:ST], ubT[:, i2, s * ST:(s + 1) * ST], ALU.mult, [tmpB, ubTB], [ybTB])
            ST_.release(it_sw)
            chk(3.3)
            itQ = ST_.get(pre + "inQ")
            wQ = wslot(itQ.slot, [8, 512])
            for p_ in range(4):
                bk = nextbank()
                for kc in range(8):
                    P.mm(banks[bk][:, 0:T], wQ[:, kc, p_ * 128:(p_ + 1) * 128], xnT[:, kc, 0:T], kc == 0, kc == 7, [wringB[itQ.slot]] + xall, [bankB[bk]])
                P.ts("dve", qTz[0:64, p_, 0, 0:T], banks[bk][0:64, 0:T], 0.125, ALU.mult, [bankB[bk]], [qTzB[p_]])
                P.act(qTz[64:128, p_, 1, 0:T], banks[bk][64:128, 0:T], AF.Identity, [bankB[bk]], [qTzB[p_]], scale=0.125)
            ST_.release(itQ)
            chk(3.4)
            itK = ST_.get(pre + "inK")
            wK = wslot(itK.slot, [8, 512])
            for p_ in range(4):
                bk = nextbank()
                for kc in range(8):
                    P.mm(banks[bk][:, 0:T], wK[:, kc, p_ * 128:(p_ + 1) * 128], xnT[:, kc, 0:T], kc == 0, kc == 7, [wringB[itK.slot]] + xall, [bankB[bk]])
                P.cp(evac_eng(), kT[:, p_, 0:T], banks[bk][:, 0:T], [bankB[bk]], [kTB])
            for s in range(nst):
                bk = nextbank()
                for kc in range(8):
                    P.mm(banks[bk][0:ST, :], xnT[:, kc, s * ST:(s + 1) * ST], wK[:, kc, :], kc == 0, kc == 7, [wringB[itK.slot], xnTB[s]], [bankB[bk]])
                P.cp("act", kout[0:ST, :], banks[bk][0:ST, :], [bankB[bk]], [koutB])
                if prompt:
                    tok0 = t * 512 + s * 128
                    P.dma("act", dap(S["pk_s"], tok0 * 64, [(64, 128), (SEQ * 64, NH), (1, 64)]),
                          kout[:].rearrange("p (h d) -> p h d", d=64), [koutB], [obuf("pk")])
                else:
                    for b in range(2):
                        P.dma("act", dap(S["sk_s"], b * NH * 32 * 64, [(64, 32), (32 * 64, NH), (1, 64)]),
                              kout[32 * b:32 * b + 32, :].rearrange("p (h d) -> p h d", d=64), [koutB], [obuf("sk")])
            ST_.release(itK)
            chk(3.5)
            if prompt and t < NT - 1:
                P.dma("act", dap(S["kT_hist"], t * 512, [(SEQ, 128), (128 * SEQ, 4), (1, 512)]), kT[:], [kTB], [khB[t]])
            itV = ST_.get(pre + "inV")
            wV = wslot(itV.slot, [8, 512])
            for s in range(nst):
                bk = nextbank()
                for kc in range(8):
                    P.mm(banks[bk][0:ST, :], xnT[:, kc, s * ST:(s + 1) * ST], wV[:, kc, :], kc == 0, kc == 7, [wringB[itV.slot], xnTB[s]], [bankB[bk]])
                P.cp("act", vout[0:ST, :], banks[bk][0:ST, :], [bankB[bk]], [voutB])
                P.cp("dve", vtok[0:ST, s, :], banks[bk][0:ST, :], [bankB[bk]], [vtokB[s]])
                if prompt:
                    tok0 = t * 512 + s * 128
                    P.dma("act", dap(S["pv_s"], tok0 * 64, [(64, 128), (SEQ * 64, NH), (1, 64)]),
                          vout[:].rearrange("p (h d) -> p h d", d=64), [voutB], [obuf("pv")])
                else:
                    for b in range(2):
                        P.dma("act", dap(S["sv_s"], b * NH * 32 * 64, [(64, 32), (32 * 64, NH), (1, 64)]),
                              vout[32 * b:32 * b + 32, :].rearrange("p (h d) -> p h d", d=64), [voutB], [obuf("sv")])
            ST_.release(itV)
            if prompt and t < NT - 1:
                P.dma("act", dap(S["v_hist"], (t * 512) * 512, [(512, 128), (128 * 512, 4), (1, 512)]), vtok[:], vtokB, [vhB[t]])
            chk(4)
            itG = ST_.get(pre + "glu")
            wg = wslot(itG.slot, [2, 512])
            if prompt:
                for s in range(nst):
                    s5_segment(l, s * 128, 128, sprev[:, 0, :], sprev[:, 1, :], sprevB, wg, wringB[itG.slot])
            else:
                for b in range(2):
                    P.cp("pool", ssm_s[:, b, 0, :], s0re_l[:, b, :], [layB], [ssm_sB])
                    P.cp("pool", ssm_s[:, b, 1, :], s0im_l[:, b, :], [layB], [ssm_sB])
                    s5_segment(l, 32 * b, 32, ssm_s[:, b, 0, :], ssm_s[:, b, 1, :], ssm_sB, wg, wringB[itG.slot])
                    P.dma("act", dap(S["sre_s"], b * 1024, [(8, 128), (1, 8)]), ssm_s[:, b, 0, :], [ssm_sB], [obuf("sre")])
                    P.dma("act", dap(S["sim_s"], b * 1024, [(8, 128), (1, 8)]), ssm_s[:, b, 1, :], [ssm_sB], [obuf("sim")])
            ST_.release(itG)
            ST_.release(it_tab)
            ST_.release(it_mat)
            chk(5)
            held = {}
            osb, osbB = sview("at_osb", 0, 1024, BF16)
            if prompt:
                obk = (6, 7)
                for half in range(2):
                    blocks = []
                    for o in obk:
                        P.mm(banks[o][:, :], zerob[:, 0:128], zerob[:, :], True, False, [constB], [bankB[o]])
                    order = [("cur", None)] + [("hist", kt) for kt in range(t - 1, -1, -1)]
                    hds = range(4 * half, 4 * half + 4)
                    for oi, (ty, kt) in enumerate(order):
                        for hd in hds:
                            ob = obk[hd // 2 - 2 * half]
                            if ty == "cur":
                                for qs in range(4):
                                    w = 128 * (qs + 1)
                                    b = Blk()
                                    b.nq, b.w = 128, w
                                    b.subs = [(qTz[:, hd // 2, hd % 2, qs * 128:(qs + 1) * 128],
                                               banks[ob][:, qs * 128 + (hd % 2) * 64:qs * 128 + (hd % 2) * 64 + 64], bankB[ob])]
                                    b.idx0 = hd * 4 + qs
                                    b.par, b.first, b.last = oi % 2, True, oi == len(order) - 1
                                    b.pre, b.done = None, None
                                    b.kT = kT[:, hd // 2, 0:w]
                                    b.qkB = [qTzB[hd // 2], kTB]
                                    b.mask = Mfull[:, 512 - w:512]
                                    b.vs = [(vtok[:, j, hd * 64:(hd + 1) * 64], 128) for j in range(qs + 1)]
                                    b.vB = [vtokB[j] for j in range(qs + 1)]
                                    b.hd = hd
                                    blocks.append(b)
                            else:
                                nm = pre + "kv%d_%d" % (half, kt)
                                for pq in range(2):
                                    b = Blk()
                                    b.nq, b.w = 128, 512
                                    b.subs = []
                                    for qs in (2 * pq, 2 * pq + 1):
                                        b.subs.append((qTz[:, hd // 2, hd % 2, qs * 128:(qs + 1) * 128],
                                                       banks[ob][:, qs * 128 + (hd % 2) * 64:qs * 128 + (hd % 2) * 64 + 64], bankB[ob]))
                                    b.idx0 = hd * 4 + 2 * pq
                                    b.par, b.first, b.last = oi % 2, False, oi == len(order) - 1
                                    b.pre, b.done = None, None
                                    b.mask = None
                                    if hd == hds[0] and pq == 0:
                                        def pre_fn(nm=nm):
                                            held[nm] = ST_.get(nm)
                                        b.pre = pre_fn
                                    if hd == hds[-1] and pq == 1:
                                        def done_fn(nm=nm):
                                            ST_.release(held[nm])
                                        b.done = done_fn
                                    b.lazy = nm
                                    b.hd = hd
                                    b.__class__ = LazyBlk
                                    b.held = held
                                    blocks.append(b)
                    run_attention(blocks, 128)
                    for pi in range(2):
                        p_ = 2 * half + pi
                        o = obk[pi]
                        ov = osb[:, pi * 512:(pi + 1) * 512]
                        P.cp("dve", ov, banks[o][:, :], [bankB[o]], [osbB])
                        bk = 4 + pi
                        bkb = banksbf[bk]
                        for qs in range(4):
                            P.tr(bkb[:, qs * 128:(qs + 1) * 128], ov[:, qs * 128:(qs + 1) * 128], identb[:], [osbB, constB], [bankB[bk]])
                        P.cp("dve", ycT[:, p_, :], bkb[:, 0:512], [bankB[bk]], [ycTB])
            else:
                obk = (6, 7)
                for o in obk:
                    P.mm(banks[o][:, :], zerob[:, 0:128], zerob[:, :], True, False, [constB], [bankB[o]])
                for b_ in range(2):
                    for (dstbuf, dstB_, nm_) in ((ckbuf, ckB, "ckT%d" % b_), (cvbuf, cvB, None)):
                        for ci in range(NCH):
                            if nm_ is not None:
                                src = (lambda li, nm_=nm_, ci=ci: dap(I[nm_], li * (128 * 4 * PAST) + ci * 2048, [(4 * PAST, 128), (1, 2048)]))
                            else:
                                src = (lambda li, b_=b_, ci=ci: dap(I["cv%d_%d" % (b_, ci)], li * (128 * 2048), [(2048, 128), (1, 2048)]))
                            P.dma("sp", cstg[:], src, [], [cstgB])
                            P.cp("pool", dstbuf[:, ci * 2048:(ci + 1) * 2048], cstg[:], [cstgB], [dstB_])
                    blocks = []
                    order = [("cur", None)] + [("hist", kb) for kb in range(NPB - 1, -1, -1)]
                    for oi, (ty, kb) in enumerate(order):
                        for hd in range(NH):
                            b = Blk()
                            b.nq = 32
                            b.subs = [(qTz[:, hd // 2, hd % 2, 32 * b_:32 * b_ + 32], banks[obk[b_]][0:32, hd * 64:(hd + 1) * 64], bankB[obk[b_]])]
                            b.idx0 = b_ * 8 + hd
                            b.par = oi % 2
                            b.first = oi == 0
                            b.last = oi == len(order) - 1
                            b.pre = None
                            b.done = None
                            b.hd = hd
                            if ty == "cur":
                                b.w = 64
                                b.kT = kT[:, hd // 2, 0:64]
                                b.qkB = [qTzB[hd // 2], kTB]
                                b.mask = Ms[:, b_, :]
                                b.vs = [(vtok[0:64, 0, hd * 64:(hd + 1) * 64], 64)]
                                b.vB = [vtokB[0]]
                            else:
                                b.w = 512
                                b.mask = None
                                b.kT = ckbuf[:].rearrange("p (a t) -> p a t", t=PAST)[:, hd // 2, kb * 512:(kb + 1) * 512]
                                b.qkB = [qTzB[hd // 2], ckB]
                                v4 = cvbuf[:].rearrange("p (s h d) -> p s h d", h=NH, d=64)
                                b.vs = [(v4[:, kb * 4 + j, hd, :], 128) for j in range(4)]
                                b.vB = [cvB]
                            blocks.append(b)
                    run_attention(blocks, 128)
                bk = 4
                bkb = banksbf[bk]
                for b_ in range(2):
                    ov = osb[0:32, b_ * 512:(b_ + 1) * 512]
                    P.cp(evac_eng(), ov, banks[obk[b_]][0:32, :], [bankB[obk[b_]]], [osbB])
                    for p_ in range(4):
                        P.tr(bkb[:, p_ * 64 + 32 * b_:p_ * 64 + 32 * b_ + 32], ov[:, p_ * 128:(p_ + 1) * 128], identb[0:32, 0:32], [osbB, constB], [bankB[bk]])
                P.cp(evac_eng(), ycT[:, :, 0:64], bkb[:, 0:256].rearrange("p (a t) -> p a t", t=64), [bankB[bk]], [ycTB])
            chk(6)
            sgV = [sview("mg_sg%d" % i, i * 512, 512) for i in range(3)]
            mV = [sview("mg_m%d" % i, 1536 + i * 512, 512) for i in range(3)]
            for c in range(8):
                itM = ST_.get(pre + "mg%d" % c)
                wM = wslot(itM.slot, [8, 512])
                wMB = wringB[itM.slot]
                gb = []
                for i in range(3):
                    bk = nextbank()
                    gb.append(bk)
                    for kc in range(8):
                        P.mm(banks[bk][:, 0:T], wM[:, kc, i * 128:(i + 1) * 128], xnT[:, kc, 0:T], kc == 0, kc == 7, [wMB] + xall, [bankB[bk]])
                    P.act(sgV[i][0][:, 0:T], banks[bk][:, 0:T], AF.Sigmoid, [bankB[bk]], [sgV[i][1]])
                bb = []
                for i, (k0, nk, src, srcB) in enumerate(((0, 2, yaT, yaTB), (2, 2, ybT, ybTB), (4, 4, ycT, ycTB))):
                    bk = nextbank()
                    bb.append(bk)
                    for k in range(nk):
                        P.mm(banks[bk][:, 0:T], wM[:, k0 + k, 384:512], src[:, k, 0:T], k == 0, k == nk - 1, [wMB, srcB], [bankB[bk]])
                    P.tt("dve", mV[i][0][:, 0:T], banks[bk][:, 0:T], sgV[i][0][:, 0:T], ALU.mult, [bankB[bk], sgV[i][1]], [mV[i][1]])
                ST_.release(itM)
                P.tt("pool", mV[0][0][:, 0:T], mV[0][0][:, 0:T], mV[1][0][:, 0:T], ALU.add, [mV[0][1], mV[1][1]], [mV[0][1]])
                P.tt("pool", mgT[:, c, 0:T], mV[0][0][:, 0:T], mV[2][0][:, 0:T], ALU.add, [mV[0][1], mV[2][1]], [mgTB[c]])
            if getattr(cfg, "debug", False) and prompt and t == 0 and l == 0:
                P.dma("act", dap(DBG, 0, [(512, 128), (128 * 512, 2), (1, 512)]), yaT[:], [yaTB], [obuf("dbg")])
                P.dma("act", dap(DBG, 2 * 128 * 512, [(512, 128), (128 * 512, 2), (1, 512)]), ybT[:], [ybTB], [obuf("dbg")])
                P.dma("act", dap(DBG, 4 * 128 * 512, [(512, 128), (128 * 512, 4), (1, 512)]), ycT[:], [ycTB], [obuf("dbg")])
                P.dma("act", dap(DBG, 8 * 128 * 512, [(512, 128), (128 * 512, 8), (1, 512)]), mgT[:], mgTB, [obuf("dbg")])
            chk(7)
            for hh in range(2):
                itO = ST_.get(pre + "wo%d" % hh)
                wO = wslot(itO.slot, [8, 512])
                for s in range(nst):
                    bk = nextbank()
                    for kc in range(8):
                        P.mm(banks[bk][0:ST, :], mgT[:, kc, s * ST:(s + 1) * ST], wO[:, kc, :], kc == 0, kc == 7, [wringB[itO.slot], mgTB[kc]], [bankB[bk]])
                    P.tt("dve", h[0:ST, s, hh * 512:(hh + 1) * 512], h[0:ST, s, hh * 512:(hh + 1) * 512], banks[bk][0:ST, :], ALU.add, [hB[s], bankB[bk]], [hB[s]])
                ST_.release(itO)
            chk(8)
            norm_to_xnT(ST, nst)
            actT, actTB = sview("actT", 0, 5632, BF16)
            act3 = actT.rearrange("p (f t) -> p f t", t=512)
            slV = [sview("ffn_sl%d" % i, 5632 + i * 512, 512) for i in range(2)]
            for pc in range(11):
                itU = ST_.get(pre + "gu%d" % pc)
                wU = wslot(itU.slot, [8, 512])
                for sl in range(2):
                    f = 2 * pc + sl
                    bg, bu = nextbank(), nextbank()
                    for kc in range(8):
                        P.mm(banks[bg][:, 0:T], wU[:, kc, sl * 128:(sl + 1) * 128], xnT[:, kc, 0:T], kc == 0, kc == 7, [wringB[itU.slot]] + xall, [bankB[bg]])
                    for kc in range(8):
                        P.mm(banks[bu][:, 0:T], wU[:, kc, 256 + sl * 128:256 + (sl + 1) * 128], xnT[:, kc, 0:T], kc == 0, kc == 7, [wringB[itU.slot]] + xall, [bankB[bu]])
                    sv_, svB = slV[f % 2]
                    P.act(sv_[:, 0:T], banks[bg][:, 0:T], AF.Silu, [bankB[bg]], [svB])
                    P.tt("dve", act3[:, f, 0:T], banks[bu][:, 0:T], sv_[:, 0:T], ALU.mult, [bankB[bu], svB], [actTB])
                ST_.release(itU)
            for hh in range(2):
                bs = [nextbank((0, 1, 2, 3)) if hh == 0 else nextbank((4, 5, 6, 7)) for s in range(nst)]
                for fg, (f0, nf) in enumerate(((0, 8), (8, 8), (16, 6))):
                    itD = ST_.get(pre + "dn%d_%d" % (hh, fg))
                    wD = wslot(itD.slot, [nf, 512])
                    for s in range(nst):
                        for f in range(f0, f0 + nf):
                            P.mm(banks[bs[s]][0:ST, :], act3[:, f, s * ST:(s + 1) * ST], wD[:, f - f0, :], f == 0, f == NF - 1, [wringB[itD.slot], actTB], [bankB[bs[s]]])
                    ST_.release(itD)
                for s in range(nst):
                    P.tt("dve", h[0:ST, s, hh * 512:(hh + 1) * 512], h[0:ST, s, hh * 512:(hh + 1) * 512], banks[bs[s]][0:ST, :], ALU.add, [hB[s], bankB[bs[s]]], [hB[s]])

        class LazyBlk(Blk):
            @property
            def kT(self):
                s = self.held[self.lazy].slot
                return shaped(wring[:, s, 0:2048], [4, 512])[:, self.hd // 2, :]

            @property
            def qkB(self):
                return [qTzB[self.hd // 2], wringB[self.held[self.lazy].slot]]

            @property
            def vs(self):
                s = self.held[self.lazy].slot
                v4 = shaped(wring[:, s, 2048:4096], [4, 512])
                return [(v4[:, j, self.hd * 64:(self.hd + 1) * 64], 128) for j in range(4)]

            @property
            def vB(self):
                return [wringB[self.held[self.lazy].slot]]

        class LazyBlkS(Blk):
            @property
            def kT(self):
                nmk, nmv, kb = self.lazy_s
                s = self.held[nmk].slot
                return shaped(wring[:, s, 0:4 * self.PAST], [4, self.PAST])[:, self.hd // 2, kb * 512:(kb + 1) * 512]

            @property
            def qkB(self):
                return [qTzB[self.hd // 2], wringB[self.held[self.lazy_s[0]].slot]]

            @property
            def vs(self):
                nmk, nmv, kb = self.lazy_s
                s = self.held[nmv].slot
                v4 = shaped(wring[:, s, 0:self.PAST * 4], [self.PAST // 128, NH, 64])
                return [(v4[:, kb * 4 + j, self.hd, :], 128) for j in range(4)]

            @property
            def vB(self):
                return [wringB[self.held[self.lazy_s[1]].slot]]

        def final_out(kind, t):
            prompt = kind == "p"
            ST = 128 if prompt else 64
            nst = 4 if prompt else 1
            junk, junkB = sview("junk", 0, 512, BF16)
            for s in range(nst):
                yo, yoB = sview("yo%d" % (s % 2), 1024 + (s % 2) * 1024, 1024)
                c_ss, c_sq, c_rs = s, 4 + s, 8 + s
                P.act(junk[0:ST, :], h[0:ST, s, :], AF.Square, [hB[s]], [junkB, smallB[c_ss]], accum=small[0:ST, c_ss:c_ss + 1])
                P.act(small[0:ST, c_sq:c_sq + 1], small[0:ST, c_ss:c_ss + 1], AF.Sqrt, [smallB[c_ss]], [smallB[c_sq]], bias=EPS, scale=1.0 / D)
                P.add("dve", lambda e, o=small[0:ST, c_rs:c_rs + 1], i=small[0:ST, c_sq:c_sq + 1]: e.reciprocal(out=o, in_=i), [smallB[c_sq]], [smallB[c_rs]])
                P.stt(yo[0:ST, :], h[0:ST, s, :], small[0:ST, c_rs:c_rs + 1], gfin[0:ST, :], ALU.mult, ALU.mult, [hB[s], smallB[c_rs], constB], [yoB])
                if prompt:
                    P.dma("act", dap(O["yp"], (t * 512 + s * 128) * D, [(D, 128), (1, D)]), yo[:, :], [yoB], [obuf("yp")])
                else:
                    P.dma("act", dap(O["ys"], 0, [(D, 64), (1, D)]), yo[0:64, :], [yoB], [obuf("ys")])

        hbB = [Buf("hb%d" % i) for i in range(NT + 1)]
        for t in range(NT):
            P.dma("sp", dap(S["hbuf"], t * 512 * D, [(4 * D, 128), (1, 4 * D)]), dap(I["xp"], t * 512 * D, [(4 * D, 128), (1, 4 * D)]), [], [hbB[t]])
        P.dma("sp", dap(S["hbuf"], SEQ * D, [(D, 64), (1, D)]), dap(I["xs"], 0, [(D, 64), (1, D)]), [], [hbB[NT]])
        chk(2.2)
        P.barrier()
        P.loop_begin(L)
        l = None
        P.dma("sp", r_l[:], (lambda li: dap(S["rpj_d"], li * 1024, [(8, 128), (1, 8)])), [], [layB])
        P.dma("sp", d_l[:], (lambda li: dap(I["d_pm"], li * 2, [(L * 2, 128), (1, 2)])), [], [layB])
        P.dma("sp", s0re_l[:].rearrange("p a b -> p (a b)"), (lambda li: dap(I["s0_re"], li * 16, [(L * 16, 128), (1, 16)])), [], [layB])
        P.dma("sp", s0im_l[:].rearrange("p a b -> p (a b)"), (lambda li: dap(I["s0_im"], li * 16, [(L * 16, 128), (1, 16)])), [], [layB])
        P.dma("sp", sgn_l[:], (lambda li: dap(I["sgn_bc"], li * 256, [(L * 256, 128), (1, 256)])), [], [layB])
        P.dma("sp", sbbc_l[:].rearrange("p a b -> p (a b)"), (lambda li: dap(I["sb_bc"], li * 256, [(L * 256, 128), (1, 256)])), [], [layB])
        P.dma("sp", sbbcs_l[:].rearrange("p a b -> p (a b)"), (lambda li: dap(I["sb_bcs"], li * 128, [(L * 128, 128), (1, 128)])), [], [layB])
        P.memset("pool", sprev[:], 0.0, [sprevB])
        chk(2.5)
        for kind, t in tiles:
            if kind == "p":
                P.dma("sp", h[:], dap(S["hbuf"], t * 512 * D, [(D, 128), (128 * D, 4), (1, D)]), [hbB[t]], hB)
            else:
                P.dma("sp", h[0:64, 0, :], dap(S["hbuf"], SEQ * D, [(D, 64), (1, D)]), [hbB[NT]], [hB[0]])
            layer_tile(kind, t, l)
            if kind == "p":
                P.dma("act", dap(S["hbuf"], t * 512 * D, [(D, 128), (128 * D, 4), (1, D)]), h[:], hB, [hbB[t]])
            else:
                P.dma("act", dap(S["hbuf"], SEQ * D, [(D, 64), (1, D)]), h[0:64, 0, :], [hB[0]], [hbB[NT]])
        P.dma("act", dap(S["pre_s"], 0, [(8, 128), (1, 8)]), sprev[:, 0, :], [sprevB], [obuf("pre")])
        P.dma("act", dap(S["pim_s"], 0, [(8, 128), (1, 8)]), sprev[:, 1, :], [sprevB], [obuf("pim")])
        lay_out = list(outB)

        def cp_out(oname, sname, total):
            row = total // 128
            if row > 16384:
                dims = [(row, 128), (16384, row // 16384), (1, 16384)]
            else:
                dims = [(row, 128), (1, row)]
            P.dma("sp", (lambda li, oname=oname, total=total, dims=dims: dap(O[oname], li * total, dims)), dap(S[sname], 0, dims), lay_out, [obuf("o_" + oname)])
        cp_out("pk", "pk_s", NH * SEQ * 64)
        cp_out("pv", "pv_s", NH * SEQ * 64)
        if cfg.with_sample:
            for oname, total in (("sk", 2 * NH * 32 * 64), ("sv", 2 * NH * 32 * 64), ("svb", 64 * 256), ("sre", 2048), ("sim", 2048)):
                cp_out(oname, oname + "_s", total)
        cp_out("pre", "pre_s", 1024)
        cp_out("pim", "pim_s", 1024)
        P.barrier()
        P.loop_end()
        for kind, t in tiles:
            if kind == "p":
                P.dma("sp", h[:], dap(S["hbuf"], t * 512 * D, [(D, 128), (128 * D, 4), (1, D)]), [], hB)
            else:
                P.dma("sp", h[0:64, 0, :], dap(S["hbuf"], SEQ * D, [(D, 64), (1, D)]), [], [hB[0]])
            final_out(kind, t)
        assert ST_.cp_ == len(ST_.items), (ST_.cp_, len(ST_.items))
        P.final_wait("sp", outB)
        block = es.enter_context(nc.Block())
        P.replay(block)
        print("built: ops=%d" % P.nops)
        if getattr(cfg, "dump", None):
            with open(cfg.dump, "w") as f_:
                for rec in P.log:
                    f_.write(repr(rec) + "\n")
    return nc


def _prep_shared(inp, L):
    f = lambda a: np.ascontiguousarray(np.asarray(a, dtype=np.float32))
    sh = {}
    sh["w_in"] = f(inp["w_in"][:L])
    sh["w_gu"] = f(inp["w_gate_up"][:L])
    sh["w_dn"] = f(inp["w_down"][:L])
    sh["w_ba"] = f(inp["w_branch_a"][:L])
    sh["w_bb"] = f(inp["w_branch_b"][:L])
    sh["w_bc"] = f(inp["w_branch_c"][:L])
    sh["w_out"] = f(inp["w_out"][:L])
    sh["w_glu"] = f(inp["ssm_w_glu"][:L])
    pk = lambda a: f(np.asarray(a)[:L].reshape(L, 8, 128).transpose(2, 0, 1))
    sh["nm_pk"] = pk(inp["norm_mix"])
    sh["nf_pk"] = pk(inp["norm_ffn"])
    sh["gfin_bc"] = f(np.broadcast_to(np.asarray(inp["norm_final"])[None, :], (128, D)))
    def pj(a):
        a = np.asarray(a)[:L].reshape(L, 8, 2, 64)
        return f(a.transpose(2, 3, 0, 1).reshape(128, L, 8))
    sh["a_re_pj"] = pj(inp["ssm_a_re"])
    sh["a_im_pj"] = pj(inp["ssm_a_im"])
    ldt_full = np.repeat(np.asarray(inp["ssm_log_dt"])[:L, :, None], 64, axis=2)
    sh["ldt_pj"] = pj(ldt_full)
    sh["a_re_row"] = f(np.asarray(inp["ssm_a_re"])[:L].reshape(L, 1024))
    sh["a_im_row"] = f(np.asarray(inp["ssm_a_im"])[:L].reshape(L, 1024))
    sh["ldt_row"] = f(ldt_full.reshape(L, 1024))
    b_re, b_im = np.asarray(inp["ssm_b_re"])[:L], np.asarray(inp["ssm_b_im"])[:L]
    c_re, c_im = np.asarray(inp["ssm_c_re"])[:L], np.asarray(inp["ssm_c_im"])[:L]
    bb_re = np.zeros((L, 128, 8, 128), np.float32)
    bb_im = np.zeros((L, 128, 8, 128), np.float32)
    cb_re = np.zeros((L, 128, 8, 128), np.float32)
    cb_im = np.zeros((L, 128, 8, 128), np.float32)
    for g in range(16):
        j, gi = g // 2, g % 2
        ch0 = 16 * (g % 8)
        st0 = 64 * gi
        bb_re[:, ch0:ch0 + 16, j, st0:st0 + 64] = b_re[:, g].transpose(0, 2, 1)
        bb_im[:, ch0:ch0 + 16, j, st0:st0 + 64] = b_im[:, g].transpose(0, 2, 1)
        cb_re[:, st0:st0 + 64, j, ch0:ch0 + 16] = c_re[:, g].transpose(0, 2, 1)
        cb_im[:, st0:st0 + 64, j, ch0:ch0 + 16] = c_im[:, g].transpose(0, 2, 1)
    sh["bblk_re"] = bb_re.reshape(L, 128, 1024)
    sh["bblk_im"] = bb_im.reshape(L, 128, 1024)
    sh["cblk_re"] = cb_re.reshape(L, 128, 1024)
    sh["cblk_im"] = cb_im.reshape(L, 128, 1024)
    sh["d_pm"] = f(np.asarray(inp["ssm_d"])[:L].reshape(L, 2, 128).transpose(2, 0, 1))
    sh["sgn_bc"] = f(np.broadcast_to(np.asarray(inp["sgu_norm"])[:L][None], (128, L, 256)))
    sw = np.asarray(inp["sgu_w"])[:L]
    sh["swT"] = f(sw.transpose(0, 3, 1, 2))
    swTs = np.zeros((L, 64, 4, 64), np.float32)
    for b in range(2):
        swTs[:, 32 * b:32 * b + 32, :, 32 * b:32 * b + 32] = sw[:, :, :32, :32].transpose(0, 3, 1, 2)
    sh["swTs"] = swTs
    sbv = np.asarray(inp["sgu_b"])[:L]
    sb_bc = np.zeros((128, L, 2, 128), np.float32)
    sb_bcs = np.zeros((128, L, 2, 64), np.float32)
    for g in range(4):
        sb_bc[64 * (g % 2):64 * (g % 2) + 64, :, g // 2, :] = sbv[None, :, g, :]
        sb_bcs[64 * (g % 2):64 * (g % 2) + 64, :, g // 2, :] = np.concatenate([sbv[:, g, :32], sbv[:, g, :32]], axis=-1)[None]
    sh["sb_bc"] = sb_bc
    sh["sb_bcs"] = sb_bcs
    return sh


def _prep_core(inp, c, L, PAST, nprompt):
    f = lambda a: np.ascontiguousarray(np.asarray(a, dtype=np.float32))
    m = {}
    m["xp"] = f(inp["x_prompt"][c % nprompt])
    sb = slice(2 * c, 2 * c + 2)
    m["xs"] = f(np.asarray(inp["x_sample"])[sb].reshape(64, D))
    def st(a):
        a = np.asarray(a)[:L, sb].reshape(L, 2, 8, 2, 64)
        return f(a.transpose(3, 4, 0, 1, 2).reshape(128, L, 2, 8))
    m["s0_re"] = st(inp["state_ssm_re"])
    m["s0_im"] = st(inp["state_ssm_im"])
    ck = np.asarray(inp["cache_sb_k"])[:L, sb]
    ckT = ck.reshape(L, 2, 4, 2, PAST, 64).transpose(0, 1, 3, 5, 2, 4).reshape(L, 2, 128, 4 * PAST)
    cvv = np.asarray(inp["cache_sb_v"])[:L, sb]
    cv = cvv.reshape(L, 2, NH, PAST // 128, 128, 64).transpose(0, 1, 4, 3, 2, 5).reshape(L, 2, 128, PAST * 4)
    for b in range(2):
        m["ckT%d" % b] = f(ckT[:, b])
        for ci in range((PAST * 4) // 2048):
            m["cv%d_%d" % (b, ci)] = f(cv[:, b, :, ci * 2048:(ci + 1) * 2048])
    return m


def _unstate(a):
    sh = a.shape[:-2]
    a = a.reshape(sh + (2, 64, 8))
    return np.ascontiguousarray(np.moveaxis(a, -1, -3).reshape(sh + (16, 64)))


_NC_CACHE = {}


def run(inp, cfg, n_cores=8, nprompt=4):
    key = (cfg.L, cfg.SEQ, cfg.PAST, cfg.with_sample, getattr(cfg, 'debug', False), cfg.stop)
    if key not in _NC_CACHE:
        _NC_CACHE[key] = build(cfg)
    nc = _NC_CACHE[key]
    L = cfg.L
    sh = _prep_shared(inp, L)
    in_maps = []
    for c in range(n_cores):
        m = dict(sh)
        m.update(_prep_core(inp, c, L, cfg.PAST, nprompt))
        in_maps.append(m)
    res = run_bass_kernel_spmd(nc, in_maps, core_ids=list(range(n_cores)))
    R = res.results
    global LAST_R
    LAST_R = R
    npr = min(nprompt, n_cores)
    y_prompt = np.stack([R[b]["yp"] for b in range(npr)])
    y_sample = np.concatenate([R[c]["ys"].reshape(2, 32, D) for c in range(n_cores)], axis=0)
    p_re = np.stack([_unstate(R[b]["pre"]) for b in range(npr)], axis=1)
    p_im = np.stack([_unstate(R[b]["pim"]) for b in range(npr)], axis=1)
    p_k = np.stack([R[b]["pk"] for b in range(npr)], axis=1)
    p_v = np.stack([R[b]["pv"] for b in range(npr)], axis=1)
    s_re = np.concatenate([_unstate(R[c]["sre"]) for c in range(n_cores)], axis=1)
    s_im = np.concatenate([_unstate(R[c]["sim"]) for c in range(n_cores)], axis=1)
    s_k = np.concatenate([R[c]["sk"] for c in range(n_cores)], axis=1)
    s_v = np.concatenate([R[c]["sv"] for c in range(n_cores)], axis=1)
    s_vb = np.concatenate([R[c]["svb"].reshape(L, 2, 32, 256) for c in range(n_cores)], axis=1)
    f = lambda a: np.ascontiguousarray(a, dtype=np.float32)
    return tuple(f(a) for a in (y_prompt, y_sample, p_re, p_im, p_k, p_v, s_re, s_im, s_k, s_v, s_vb))


def kernel(**inputs):
    cfg = Cfg(L=4, SEQ=8192, PAST=1024, with_sample=True)
    return run(inputs, cfg, n_cores=8, nprompt=4)
```

```python
import numpy as np
from contextlib import ExitStack
import concourse.bass as bass
import concourse.mybir as mybir
from concourse.bass_utils import run_bass_kernel_spmd

F32 = mybir.dt.float32
BF16 = mybir.dt.bfloat16
I32 = mybir.dt.int32
AF = mybir.ActivationFunctionType
ALU = mybir.AluOpType

D = 1024
KC = 8
INW = 5376
FF = 2816
NF = 22
NH = 8
EPS = 1e-6
TWO_PI = 6.283185307179586
NEG = -30000.0


class Cfg:
    def __init__(self, L=4, SEQ=8192, PAST=1024, with_sample=True):
        self.L = L
        self.SEQ = SEQ
        self.PAST = PAST
        self.NT = SEQ // 512
        self.with_sample = with_sample
        self.stop = 99


class _Stop(Exception):
    pass


class Sem:
    def __init__(self, h):
        self.h = h
        self.v = 0


class Buf:
    __slots__ = ("name", "w", "r", "grp", "lo", "hi", "excl")

    def __init__(self, name, grp=None, lo=0, hi=0, excl=False):
        self.name = name
        self.excl = excl
        self.w = {}
        self.r = {}
        self.grp = grp
        self.lo = lo
        self.hi = hi
        if grp is not None:
            grp.append(self)
        Buf.ALL.append(self)


Buf.ALL = []


class Eng:
    def __init__(self, name, sem):
        self.name = name
        self.sem = sem
        self.ops = []
        self.seen = {}


class Prog:
    def __init__(self, nc, es, n_dma_sems=40):
        self.nc = nc
        self.eng = {}
        for n in ("pe", "act", "dve", "pool", "sp"):
            self.eng[n] = Eng(n, Sem(es.enter_context(nc.semaphore("sem_" + n))))
        self.dsems = [Sem(es.enter_context(nc.semaphore("dsem%d" % i))) for i in range(n_dma_sems)]
        self.barA = es.enter_context(nc.semaphore("barA"))
        self.barB = es.enter_context(nc.semaphore("barB"))
        self.drr = 0
        self.nops = 0
        self.stopped = False
        self.log = []
        self.cur_desc = ''
        self.semname = {id(e.sem): n for n, e in self.eng.items()}
        for i_, s_ in enumerate(self.dsems):
            self.semname[id(s_)] = 'd%d' % i_

    def _deps(self, reads, writes):
        deps = {}

        def need(d):
            for s, v in d.items():
                if deps.get(s, 0) < v:
                    deps[s] = v

        for b in reads:
            need(b.w)
            if b.excl:
                need(b.r)
        for b in writes:
            need(b.w)
            need(b.r)
            if b.grp is not None:
                for y in b.grp:
                    if y is not b and y.lo < b.hi and b.lo < y.hi:
                        need(y.w)
                        need(y.r)
        return deps

    def _waits(self, E, deps):
        waits = []
        for s, v in deps.items():
            if s is E.sem:
                if E.name == "pe":
                    continue
                if E.sem.v - v >= 4:
                    continue
            if E.seen.get(s, 0) >= v:
                continue
            E.seen[s] = v
            waits.append((s, v))
        return waits

    def add(self, en, fn, reads=(), writes=()):
        if self.stopped:
            return
        E = self.eng[en]
        waits = self._waits(E, self._deps(reads, writes))
        E.sem.v += 1
        val = E.sem.v
        E.ops.append((waits, fn, E.sem, 1))
        self.log.append((en, E.sem.v, [(self.semname.get(id(s_), '?'), v_) for s_, v_ in waits], self.cur_desc))
        for b in reads:
            if b.r.get(E.sem, 0) < val:
                b.r[E.sem] = val
        for b in writes:
            b.w = {E.sem: val}
            b.r = {}
        self.nops += 1

    def dma(self, q, out, in_, reads=(), writes=()):
        if self.stopped:
            return
        E = self.eng[q]
        sem = self.dsems[self.drr % len(self.dsems)]
        self.drr += 1
        deps = self._deps(reads, writes)
        if sem.v > 0 and deps.get(sem, 0) < sem.v:
            deps[sem] = sem.v
        waits = self._waits(E, deps)
        sem.v += 16
        val = sem.v
        E.ops.append((waits, ('DMA', out, in_), sem, 16))
        self.log.append((q + '-dma', (self.semname.get(id(sem)), val), [(self.semname.get(id(s_), '?'), v_) for s_, v_ in waits], 'dma'))
        for b in reads:
            if b.r.get(sem, 0) < val:
                b.r[sem] = val
        for b in writes:
            b.w = {sem: val}
            b.r = {}
        self.nops += 1

    def final_wait(self, q, bufs):
        E = self.eng[q]
        deps = {}
        for b in bufs:
            for d in (b.w, b.r):
                for s, v in d.items():
                    if deps.get(s, 0) < v:
                        deps[s] = v
        for e2 in self.eng.values():
            if e2.sem.v > 0:
                deps[e2.sem] = max(deps.get(e2.sem, 0), e2.sem.v) if e2 is not E else deps.get(e2.sem, 0)
        deps = {s: v for s, v in deps.items() if v > 0 and s is not E.sem}
        for s in self.dsems:
            if s.v > 0:
                deps[s] = s.v
        waits = [(s, v) for s, v in deps.items()]
        E.ops.append((waits, None, None, 0))

    def barrier(self):
        finals = [(s, s.v) for s in [e.sem for e in self.eng.values()] + self.dsems if s.v > 0]
        for E in self.eng.values():
            E.ops.append(([], ('BAR', finals), None, 0))
            E.seen = {}
        for s in [e.sem for e in self.eng.values()] + self.dsems:
            s.v = 0
        for b in Buf.ALL:
            b.w = {}
            b.r = {}

    def loop_begin(self, n):
        for E in self.eng.values():
            E.ops.append(([], ('LOOP', n), None, 0))

    def loop_end(self):
        for E in self.eng.values():
            E.ops.append(([], ('ENDLOOP',), None, 0))

    def replay(self, block):
        amap = {"pe": block.tensor, "act": block.scalar, "dve": block.vector, "pool": block.gpsimd, "sp": block.sync}
        NE = len(self.eng)
        allsems = [e.sem for e in self.eng.values()] + self.dsems
        for n, deco in amap.items():
            ops = self.eng[n].ops
            mysem = self.eng[n].sem

            def body(e, ops=ops, n=n, mysem=mysem):
                st = {"li": None, "nbar": 0, "ctx": None, "nloop": 0}

                def run(lst):
                    idx = 0
                    while idx < len(lst):
                        waits, fn, sem, inc = lst[idx]
                        idx += 1
                        for s, v in waits:
                            e.wait_ge(s.h, v)
                        if fn is None:
                            continue
                        if isinstance(fn, tuple):
                            kind = fn[0]
                            if kind == 'DMA':
                                o, i_ = fn[1], fn[2]
                                if callable(o):
                                    o = o(st["li"])
                                if callable(i_):
                                    i_ = i_(st["li"])
                                try:
                                    e.dma_start(out=o, in_=i_).then_inc(sem.h, inc)
                                except Exception:
                                    print('DMA FAIL', n, o.tensor.name, o.offset, list(o.ap), i_.tensor.name, i_.offset, list(i_.ap))
                                    raise
                            elif kind == 'BAR':
                                for s, v in fn[1]:
                                    if s is not mysem:
                                        e.wait_ge(s.h, v)
                                e.sem_inc(self.barA, 1)
                                if st["li"] is None:
                                    k1 = st["nbar"] + 1
                                else:
                                    k1 = st["li"] + (st["nbar"] + 1)
                                if n == "sp":
                                    e.wait_ge(self.barA, k1 * NE)
                                    for s in allsems:
                                        e.sem_clear(s.h)
                                    e.sem_inc(self.barB, 1)
                                e.wait_ge(self.barB, k1)
                                if st["li"] is None:
                                    st["nbar"] += 1
                            elif kind == 'LOOP':
                                depth, j = 1, idx
                                while True:
                                    f2 = lst[j][1]
                                    if isinstance(f2, tuple) and f2[0] == 'LOOP':
                                        depth += 1
                                    if isinstance(f2, tuple) and f2[0] == 'ENDLOOP':
                                        depth -= 1
                                        if depth == 0:
                                            break
                                    j += 1
                                inner = lst[idx:j]
                                nb_in = sum(1 for x in inner if isinstance(x[1], tuple) and x[1][0] == 'BAR')
                                with e.Fori(0, fn[1]) as li:
                                    st["li"] = li
                                    run(inner)
                                    st["li"] = None
                                st["nbar"] += nb_in * fn[1]
                                idx = j + 1
                            continue
                        fn(e).then_inc(sem.h, inc)

                run(ops)

            deco(body)

    def mm(self, out, lhsT, rhs, start, stop, reads, writes, skip=False):
        if skip:
            self.cur_desc = 'mm(skip)'
            self.add("pe", lambda e: e.matmul(out, lhsT=lhsT, rhs=rhs, start=start, stop=stop, skip_group_check=True), reads, writes)
            return
        self.cur_desc = 'mm %s <- %s x %s st=%s sp=%s' % (_d(out), _d(lhsT), _d(rhs), start, stop)
        self.add("pe", lambda e: e.matmul(out, lhsT=lhsT, rhs=rhs, start=start, stop=stop), reads, writes)

    def tr(self, out, in_, ident, reads, writes):
        self.cur_desc = 'tr %s <- %s' % (_d(out), _d(in_))
        self.add("pe", lambda e: e.transpose(out, in_, ident), reads, writes)

    def act(self, out, in_, func, reads, writes, bias=None, scale=None, accum=None):
        self.cur_desc = 'act %s %s <- %s b=%s' % (func, _d(out), _d(in_), bias if isinstance(bias, (float, type(None))) else _d(bias))
        kw = {}
        if bias is not None:
            kw["bias"] = bias
        if scale is not None:
            kw["scale"] = scale
        if accum is not None:
            kw["accum_out"] = accum
        self.add("act", lambda e: e.activation(out=out, in_=in_, func=func, **kw), reads, writes)

    def tt(self, en, out, in0, in1, op, reads, writes):
        self.cur_desc = 'tt %s %s <- %s , %s' % (op, _d(out), _d(in0), _d(in1))
        self.add(en, lambda e: e.tensor_tensor(out=out, in0=in0, in1=in1, op=op), reads, writes)

    def ts(self, en, out, in0, s1, op0, reads, writes, s2=None, op1=None):
        self.cur_desc = 'ts %s %s <- %s' % (op0, out.tensor.name, in0.tensor.name)
        if op1 is None:
            self.add(en, lambda e: e.tensor_scalar(out=out, in0=in0, scalar1=s1, scalar2=None, op0=op0), reads, writes)
        else:
            self.add(en, lambda e: e.tensor_scalar(out=out, in0=in0, scalar1=s1, scalar2=s2, op0=op0, op1=op1), reads, writes)

    def stt(self, out, in0, scalar, in1, op0, op1, reads, writes):
        self.add("dve", lambda e: e.scalar_tensor_tensor(out=out, in0=in0, scalar=scalar, in1=in1, op0=op0, op1=op1), reads, writes)

    def cp(self, en, out, in_, reads, writes):
        self.cur_desc = 'cp %s <- %s' % (_d(out), _d(in_))
        if en == "act":
            self.add(en, lambda e: e.activation(out=out, in_=in_, func=AF.Identity), reads, writes)
        else:
            self.add(en, lambda e: e.tensor_copy(out=out, in_=in_), reads, writes)

    def memset(self, en, ap, val, writes):
        self.add(en, lambda e: e.memset(ap, val), (), writes)


def _d(ap):
    return '%s@%s%s' % (ap.tensor.name, ap.offset, list(ap.ap))


def dap(t, off, dims):
    return bass.AP(t, off if not isinstance(off, (int, np.integer)) else int(off), [[int(s), int(c)] for s, c in dims])


def build(cfg):
    nc = bass.Bass("TRN2", target_bir_lowering=False)
    L, SEQ, PAST, NT = cfg.L, cfg.SEQ, cfg.PAST, cfg.NT
    NPB = PAST // 512

    def din(name, shape, dt=F32):
        return nc.dram_tensor(name, list(shape), dt, kind="ExternalInput")

    def dout(name, shape, dt=F32):
        return nc.dram_tensor(name, list(shape), dt, kind="ExternalOutput")

    def dscr(name, shape, dt):
        return nc.dram_tensor(name, list(shape), dt, kind="Internal")

    I = {}
    for name, shape in [
        ("xp", (SEQ, D)), ("xs", (64, D)),
        ("w_in", (L, D, INW)), ("w_gu", (L, D, 2 * FF)), ("w_dn", (L, FF, D)),
        ("w_ba", (L, 256, D)), ("w_bb", (L, 256, D)), ("w_bc", (L, 512, D)), ("w_out", (L, D, D)),
        ("w_glu", (L, 256, 512)),
        ("nm_pk", (128, L, 8)), ("nf_pk", (128, L, 8)), ("gfin_bc", (128, D)),
        ("a_re_pj", (128, L, 8)), ("a_im_pj", (128, L, 8)), ("ldt_pj", (128, L, 8)),
        ("a_re_row", (L, 1024)), ("a_im_row", (L, 1024)), ("ldt_row", (L, 1024)),
        ("bblk_re", (L, 128, 1024)), ("bblk_im", (L, 128, 1024)),
        ("cblk_re", (L, 128, 1024)), ("cblk_im", (L, 128, 1024)),
        ("d_pm", (128, L, 2)), ("s0_re", (128, L, 2, 8)), ("s0_im", (128, L, 2, 8)),
        ("sgn_bc", (128, L, 256)), ("swT", (L, 128, 4, 128)), ("swTs", (L, 64, 4, 64)),
        ("sb_bc", (128, L, 2, 128)), ("sb_bcs", (128, L, 2, 64)),
    ]:
        I[name] = din(name, shape)
    for b_ in range(2):
        I["ckT%d" % b_] = din("ckT%d" % b_, (L, 128, 4 * PAST))
        for ci in range((PAST * 4) // 2048):
            I["cv%d_%d" % (b_, ci)] = din("cv%d_%d" % (b_, ci), (L, 128, 2048))
    O = {}
    for name, shape in [
        ("yp", (SEQ, D)), ("ys", (64, D)),
        ("pre", (L, 128, 8)), ("pim", (L, 128, 8)),
        ("pk", (L, NH, SEQ, 64)), ("pv", (L, NH, SEQ, 64)),
        ("sre", (L, 2, 128, 8)), ("sim", (L, 2, 128, 8)),
        ("sk", (L, 2, NH, 32, 64)), ("sv", (L, 2, NH, 32, 64)),
        ("svb", (L, 64, 256)),
    ]:
        O[name] = dout(name, shape)
    S = {}
    NCH = (PAST * 4) // 2048
    recs = [("inA", 8 * 512), ("inB", 8 * 256), ("inQ", 8 * 512), ("inK", 8 * 512), ("inV", 8 * 512)]
    recs += [("mg%d" % c, 4096) for c in range(8)] + [("wo%d" % hh, 4096) for hh in range(2)]
    recs += [("gu%d" % pc, 4096) for pc in range(11)]
    recs += [("dn%d_%d" % (hh, fg), nf * 512) for hh in range(2) for fg, nf in enumerate((8, 8, 6))]
    for name, R in recs:
        S[name] = dscr("r_" + name, (L, 128, R), BF16)
    for name, shape, dt in [
        ("wglu_bf", (L, 256, 512), BF16),
        ("s5tab", (L, 128, 3, 1024), F32), ("s5mat", (L, 128, 4, 1024), BF16),
        ("swT_bf", (L, 128, 512), BF16), ("swTs_bf", (L, 64, 256), BF16),
        ("kT_hist", (4, 128, SEQ), BF16), ("v_hist", (SEQ, 512), BF16),
        ("rpj_d", (L, 128, 8), F32), ("hbuf", (SEQ + 64, D), F32),
        ("pk_s", (NH, SEQ, 64), F32), ("pv_s", (NH, SEQ, 64), F32),
        ("sk_s", (2, NH, 32, 64), F32), ("sv_s", (2, NH, 32, 64), F32), ("svb_s", (64, 256), F32),
        ("sre_s", (2, 128, 8), F32), ("sim_s", (2, 128, 8), F32), ("pre_s", (128, 8), F32), ("pim_s", (128, 8), F32),
    ]:
        S[name] = dscr(name, shape, dt)
    es = ExitStack()
    with es:
        P = Prog(nc, es)

        def sb(name, shape, dt):
            return es.enter_context(nc.sbuf_tensor("sb_" + name, list(shape), dt))

        zp_t = [es.enter_context(nc.psum_tensor("zpair%d" % i, [128, 1024], F32)) for i in range(2)]
        zpair = [zp_t[i][:, :] for i in range(2)]
        banks = [zpair[i // 2][:, (i % 2) * 512:(i % 2) * 512 + 512] for i in range(4)]
        banks += [es.enter_context(nc.psum_tensor("bank%d" % i, [128, 512], F32))[:, :] for i in range(4, 8)]
        bankB = [Buf("bank%d" % i, excl=True) for i in range(8)]
        banksbf = [banks[i].bitcast(BF16) for i in range(8)]
        bank_rr = [0]

        def nextbank(pool=(0, 1, 2, 3, 4, 5, 6, 7)):
            i = pool[bank_rr[0] % len(pool)]
            bank_rr[0] += 1
            return i

        h = sb("h", [128, 4, D], F32)
        hB = [Buf("h%d" % s) for s in range(4)]
        xnT = sb("xnT", [128, 8, 512], BF16)
        xnTB = [Buf("xnT%d" % s) for s in range(4)]
        uaT = sb("uaT", [128, 2, 512], F32)
        uaTB = Buf("uaT")
        uabf = sb("uabf", [128, 2, 512], BF16)
        uabfB = Buf("uabf")
        ubT = sb("ubT", [128, 2, 512], F32)
        ubTB = Buf("ubT")
        qTz = sb("qTz", [128, 4, 2, 512], BF16)
        qTzB = [Buf("qTz%d" % p) for p in range(4)]
        kT = sb("kT", [128, 4, 512], BF16)
        kTB = Buf("kT")
        vtok = sb("vtok", [128, 4, 512], BF16)
        vtokB = [Buf("vtok%d" % s) for s in range(4)]
        kout = sb("kout", [128, 512], F32)
        koutB = Buf("kout")
        vout = sb("vout", [128, 512], F32)
        voutB = Buf("vout")
        vnbf = sb("vnbf", [128, 2, 256], BF16)
        vnbfB = [Buf("vnbf0"), Buf("vnbf1")]
        vnf = sb("vnf", [128, 256], F32)
        vnfB = Buf("vnf")
        vn0 = sb("vn0", [128, 256], F32)
        vn0B = Buf("vn0")
        small = sb("small", [128, 64], F32)
        smallB = [Buf("small%d" % i) for i in range(64)]
        yaT = sb("yaT", [128, 2, 512], BF16)
        yaTB = Buf("yaT")
        ybT = sb("ybT", [128, 2, 512], BF16)
        ybTB = Buf("ybT")
        ycT = sb("ycT", [128, 4, 512], BF16)
        ycTB = Buf("ycT")
        mgT = sb("mgT", [128, 8, 512], BF16)
        mgTB = [Buf("mgT%d" % c) for c in range(8)]
        s5tab = sb("s5tab", [128, 3, 1024], F32)
        s5tabB = Buf("s5tab")
        s5mat = sb("s5mat", [128, 4, 1024], BF16)
        s5matB = Buf("s5mat")
        swTsb = sb("swTsb", [128, 512], BF16)
        swTsbB = Buf("swTsb")
        swTssb = sb("swTssb", [64, 256], BF16)
        swTssbB = Buf("swTssb")
        r_pj = sb("r_pj", [128, L, 8], F32)
        r_pjB = Buf("r_pj")
        d_l = sb("d_l", [128, 2], F32)
        s0re_l = sb("s0re_l", [128, 2, 8], F32)
        s0im_l = sb("s0im_l", [128, 2, 8], F32)
        sgn_l = sb("sgn_l", [128, 256], F32)
        sbbc_l = sb("sbbc_l", [128, 2, 128], F32)
        sbbcs_l = sb("sbbcs_l", [128, 2, 64], F32)
        r_l = sb("r_l", [128, 8], F32)
        layB = Buf("laycon")
        gfin = sb("gfin", [128, D], F32)
        constB = Buf("const")
        sprev = sb("sprev", [128, 2, 8], F32)
        sprevB = Buf("sprev")
        ssm_s = sb("ssm_s", [128, 2, 2, 8], F32)
        ssm_sB = Buf("ssm_s")
        nbias = sb("nbias", [128, 32, 2], F32)
        nbiasB = [[Buf("nb%d_%d" % (i, k)) for k in range(2)] for i in range(32)]
        tot = sb("tot", [128, 4], F32)
        totB = [Buf("tot%d" % i) for i in range(4)]
        identb = sb("identb", [128, 128], BF16)
        identf = sb("identf", [128, 128], F32)
        ones = sb("ones", [128, 512], F32)
        zerob = sb("zerob", [128, 512], BF16)
        zcol = sb("zcol", [128, 1], F32)
        Mfull = sb("Mfull", [128, 512], F32)
        Ms = sb("Ms", [32, 2, 64], F32)
        tau1 = sb("tau1", [128, 128], F32)
        tmask = sb("tmask", [128, 128], F32)
        trimask = sb("trimask", [128, 4, 128], F32)
        ckbuf = sb("ckbuf", [128, 4 * PAST], BF16)
        ckB = Buf("ckbuf")
        cvbuf = sb("cvbuf", [128, 4 * PAST], BF16)
        cvB = Buf("cvbuf")
        cstg = sb("cstg", [128, 2048], F32)
        cstgB = Buf("cstg")
        NW = 4
        wring = sb("wring", [128, NW, 4096], BF16)
        wringB = [Buf("wring%d" % i) for i in range(NW)]
        SCRW = 8208
        scr = sb("scr", [128, SCRW], F32)
        scr_grp = []
        sview_cache = {}

        class View:
            pass

        def sview(name, off_words, nwords, dt=F32):
            key = (name, off_words, nwords, dt == BF16)
            if key in sview_cache:
                return sview_cache[key]
            b = Buf(name, scr_grp, off_words, off_words + nwords)
            ap = scr[:, off_words:off_words + nwords]
            if dt == BF16:
                ap = ap.bitcast(BF16)
            sview_cache[key] = (ap, b)
            return ap, b

        def chk(n):
            if cfg.stop <= n:
                P.stopped = True

        P.memset("pool", identf[:], 0.0, [constB])
        P.add("pool", lambda e: e.affine_select(out=identf[:], in_=identf[:], pattern=[[-1, 128]], compare_op=ALU.not_equal,
                                                fill=1.0, base=0, channel_multiplier=1), [], [constB])
        P.cp("pool", identb[:], identf[:], [constB], [constB])
        P.memset("pool", ones[:], 1.0, [constB])
        P.memset("pool", zerob[:], 0.0, [constB])
        P.memset("pool", zcol[:], 0.0, [constB])
        P.memset("pool", Mfull[:], 0.0, [constB])
        P.add("pool", lambda e: e.affine_select(out=Mfull[:, 384:512], in_=Mfull[:, 384:512], pattern=[[-1, 128]],
                                                compare_op=ALU.is_gt, fill=NEG, base=0, channel_multiplier=1), [], [constB])
        P.memset("pool", Ms[:], NEG, [constB])
        for b in range(2):
            P.memset("pool", Ms[:, b, 32 * b:32 * b + 32], 0.0, [constB])
            P.add("pool", lambda e, b=b: e.affine_select(out=Ms[:, b, 32 * b:32 * b + 32], in_=Ms[:, b, 32 * b:32 * b + 32],
                                                         pattern=[[-1, 32]], compare_op=ALU.is_gt, fill=NEG, base=0,
                                                         channel_multiplier=1), [], [constB])
        P.add("pool", lambda e: e.iota(tau1[:], [[1, 128]], base=1, channel_multiplier=0, allow_small_or_imprecise_dtypes=True), [], [constB])
        P.memset("pool", tmask[:], 1.0, [constB])
        P.memset("pool", tmask[:, 0:1], 0.0, [constB])
        P.memset("pool", trimask[:], 1.0, [constB])
        for g in range(4):
            P.add("pool", lambda e, g=g: e.affine_select(out=trimask[:, g, :], in_=trimask[:, g, :], pattern=[[1, 128]],
                                                         compare_op=ALU.is_ge, fill=0.0, base=0, channel_multiplier=-1), [], [constB])
        P.memset("pool", qTz[:], 0.0, qTzB)
        P.memset("pool", nbias[:], 0.0, [b for bb in nbiasB for b in bb])
        P.dma("sp", gfin[:], I["gfin_bc"].ap(), [], [constB])

        chk(0)
        wscr = [[] for _ in range(L)]
        cur_l = [0]
        nmt, nmB = sview("nmt", 6400, 64)
        nft, nfB = sview("nft", 6464, 64)
        P.dma("sp", nmt[:, 0:L * 8], dap(I["nm_pk"], 0, [(L * 8, 128), (1, L * 8)]), [], [nmB])
        P.dma("sp", nft[:, 0:L * 8], dap(I["nf_pk"], 0, [(L * 8, 128), (1, L * 8)]), [], [nfB])
        NST = 2
        stg = [sview("stg%d" % i, i * 3072, 2048) for i in range(NST)]
        stb = [sview("stb%d" % i, i * 3072 + 2048, 1024, BF16) for i in range(NST)]
        conv_i = [0]

        def shaped(ap, dims):
            if len(dims) == 1:
                return ap
            if len(dims) == 2:
                return ap.rearrange("p (a b) -> p a b", b=dims[1])
            return ap.rearrange("p (a b c) -> p a b c", b=dims[1], c=dims[2])

        def conv(src, dst, dims, scale=None, mul=None, mask=None, rows=128):
            n = int(np.prod(dims))
            i = conv_i[0] % NST
            conv_i[0] += 1
            sa, sB = stg[i]
            ba, bB = stb[i]
            sv_ = shaped(sa[0:rows, 0:n], dims)
            bv_ = shaped(ba[0:rows, 0:n], dims)
            P.dma("sp", sv_, src, [], [sB])
            en = "dve" if (conv_i[0] % 2 == 0) else "pool"
            if scale is not None:
                sc_ap, scB = scale
                if en == "pool":
                    P.ts("pool", bv_, sv_, sc_ap, ALU.mult, [sB, scB], [bB], s2=0.0, op1=ALU.add)
                else:
                    P.ts("dve", bv_, sv_, sc_ap, ALU.mult, [sB, scB], [bB])
            elif mul is not None:
                P.ts("dve", bv_, sv_, float(mul), ALU.mult, [sB], [bB])
            elif mask is not None:
                P.tt("dve", bv_, sv_, mask, ALU.mult, [sB, constB], [bB])
            else:
                P.cp(en, bv_, sv_, [sB], [bB])
            wb_ = Buf("wscr")
            wscr[cur_l[0]].append(wb_)
            if isinstance(dst, list):
                for d_ap, sel in dst:
                    P.dma("act", d_ap, sel(bv_), [bB], [wb_])
            else:
                P.dma("act", dst, bv_, [bB], [wb_])

        for l in range(L):
            cur_l[0] = l
            win, wgu = I["w_in"], I["w_gu"]

            def rdst(name, R, off, dims):
                return dap(S[name], (l * 128) * R + off, [(R, 128)] + dims)
            for kc in range(KC):
                sc_m = (nmt[:, l * 8 + kc:l * 8 + kc + 1], nmB)
                sc_f = (nft[:, l * 8 + kc:l * 8 + kc + 1], nfB)
                ro = (l * D + kc * 128) * INW
                for nm, c0, ncol in (("inA", 0, 512), ("inB", 512, 256), ("inQ", 768, 512), ("inK", 1280, 512), ("inV", 1792, 512)):
                    conv(dap(win, ro + c0, [(INW, 128), (1, ncol)]), rdst(nm, 8 * ncol, kc * ncol, [(1, ncol)]), [ncol], scale=sc_m)
                for i in range(3):
                    conv(dap(win, ro + 2304 + i * 1024, [(INW, 128), (128, 8), (1, 128)]),
                         [(rdst("mg%d" % c, 4096, kc * 512 + i * 128, [(1, 128)]), (lambda v, c=c: v[:, c, :])) for c in range(8)],
                         [8, 128], scale=sc_m)
                rg = (l * D + kc * 128) * (2 * FF)
                for up in range(2):
                    for f0, nf in ((0, 16), (16, 6)):
                        conv(dap(wgu, rg + up * FF + f0 * 128, [(2 * FF, 128), (256, nf // 2), (128, 2), (1, 128)]),
                             [(rdst("gu%d" % (f0 // 2 + q), 4096, kc * 512 + up * 256, [(128, 2), (1, 128)]), (lambda v, q=q: v[:, q, :, :])) for q in range(nf // 2)],
                             [nf // 2, 2, 128], scale=sc_f)
            for kk in range(8):
                if kk < 2:
                    src_t, r0 = I["w_ba"], (l * 256 + kk * 128)
                elif kk < 4:
                    src_t, r0 = I["w_bb"], (l * 256 + (kk - 2) * 128)
                else:
                    src_t, r0 = I["w_bc"], (l * 512 + (kk - 4) * 128)
                conv(dap(src_t, r0 * D, [(D, 128), (128, 8), (1, 128)]),
                     [(rdst("mg%d" % c, 4096, kk * 512 + 384, [(1, 128)]), (lambda v, c=c: v[:, c, :])) for c in range(8)],
                     [8, 128])
            for kc in range(KC):
                conv(dap(I["w_out"], (l * D + kc * 128) * D, [(D, 128), (1, D)]),
                     [(rdst("wo%d" % hh, 4096, kc * 512, [(1, 512)]), (lambda v, hh=hh: v[:, hh * 512:(hh + 1) * 512])) for hh in range(2)], [D])
            for f in range(0, NF, 2):
                dsts = []
                for q in range(2):
                    fq = f + q
                    fg = 0 if fq < 8 else (1 if fq < 16 else 2)
                    f0, nf = ((0, 8), (8, 8), (16, 6))[fg]
                    for hh in range(2):
                        dsts.append((rdst("dn%d_%d" % (hh, fg), nf * 512, (fq - f0) * 512, [(1, 512)]), (lambda v, q=q, hh=hh: v[:, q, hh * 512:(hh + 1) * 512])))
                conv(dap(I["w_dn"], (l * FF + f * 128) * D, [(D, 128), (128 * D, 2), (1, D)]), dsts, [2, D])
            conv(dap(I["w_glu"], l * 256 * 512, [(512, 128), (128 * 512, 2), (1, 512)]),
                 dap(S["wglu_bf"], l * 256 * 512, [(512, 128), (128 * 512, 2), (1, 512)]), [2, 512])
            conv(dap(I["swT"], l * 128 * 512, [(512, 128), (1, 512)]),
                 dap(S["swT_bf"], l * 128 * 512, [(512, 128), (1, 512)]), [512],
                 mask=trimask[:].rearrange("p g t -> p (g t)"))
            conv(dap(I["swTs"], l * 64 * 256, [(256, 64), (64, 4), (1, 64)]),
                 dap(S["swTs_bf"], l * 64 * 256, [(256, 64), (64, 4), (1, 64)]), [4, 64],
                 mask=trimask[0:64, :, 0:64], rows=64)
            conv(dap(I["cblk_re"], l * 128 * 1024, [(1024, 128), (1, 1024)]),
                 dap(S["s5mat"], (l * 128) * 4096 + 2 * 1024, [(4096, 128), (1, 1024)]), [1024])
            conv(dap(I["cblk_im"], l * 128 * 1024, [(1024, 128), (1, 1024)]),
                 dap(S["s5mat"], (l * 128) * 4096 + 3 * 1024, [(4096, 128), (1, 1024)]), [1024], mul=-1.0)

        chk(1)
        def s5_common(pref, n, off, a_re, a_im, ldt, srcB):
            T = {}
            for k_, nm in enumerate(["dt", "mag", "ang", "y", "yi", "fr", "sn", "cs"]):
                T[nm] = sview(pref + nm, off + k_ * n, n)
            yi_ap = T["yi"][0].bitcast(I32)
            P.act(T["dt"][0], ldt, AF.Exp, [srcB], [T["dt"][1]])
            P.tt("dve", T["mag"][0], a_re, T["dt"][0], ALU.mult, [srcB, T["dt"][1]], [T["mag"][1]])
            P.act(T["mag"][0], T["mag"][0], AF.Exp, [T["mag"][1]], [T["mag"][1]])
            P.tt("dve", T["ang"][0], a_im, T["dt"][0], ALU.mult, [srcB, T["dt"][1]], [T["ang"][1]])

            def frac(dst, src, add):
                P.ts("dve", T["y"][0], src[0], 1.0 / TWO_PI, ALU.mult, [src[1]], [T["y"][1]], s2=add, op1=ALU.add)
                P.cp("dve", yi_ap, T["y"][0], [T["y"][1]], [T["yi"][1]])
                P.cp("dve", dst[0], yi_ap, [T["yi"][1]], [dst[1]])
                P.tt("dve", dst[0], T["y"][0], dst[0], ALU.subtract, [T["y"][1], dst[1]], [dst[1]])
                P.ts("dve", dst[0], dst[0], 0.5, ALU.min, [dst[1]], [dst[1]], s2=-0.5, op1=ALU.max)
            T["frac"] = frac
            return T

        for l in range(L):
            pj_src, pjB = sview("pjsrc", 6600, 3 * 8)
            P.dma("sp", pj_src[:, 0:8], dap(I["a_re_pj"], l * 8, [(L * 8, 128), (1, 8)]), [], [pjB])
            P.dma("sp", pj_src[:, 8:16], dap(I["a_im_pj"], l * 8, [(L * 8, 128), (1, 8)]), [], [pjB])
            P.dma("sp", pj_src[:, 16:24], dap(I["ldt_pj"], l * 8, [(L * 8, 128), (1, 8)]), [], [pjB])
            Tp = s5_common("pj_", 8, 6700, pj_src[:, 0:8], pj_src[:, 8:16], pj_src[:, 16:24], pjB)
            P.cp("dve", r_pj[:, l, :], Tp["mag"][0], [Tp["mag"][1]], [r_pjB])
            wb_ = Buf("wscr_rpj")
            wscr[l].append(wb_)
            P.dma("act", dap(S["rpj_d"], l * 1024, [(8, 128), (1, 8)]), r_pj[:, l, :], [r_pjB], [wb_])
            Tp["frac"](Tp["fr"], Tp["ang"], 0.0)
            ph, phB = sview("ph", 0, 1024)
            tb, tbB = sview("tb", 1024, 3072)
            yy, yyB = sview("yy", 4096, 1024)
            yyi, yyiB = sview("yyi", 5120, 1024)
            ph3 = ph.rearrange("p (j t) -> p j t", t=128)
            for j in range(8):
                P.ts("dve", ph3[:, j, :], tau1[:], Tp["fr"][0][:, j:j + 1], ALU.mult, [constB, Tp["fr"][1]], [phB])
                P.ts("dve", tb[:, 2048 + j * 128:2048 + (j + 1) * 128], tmask[:], r_pj[:, l, j:j + 1], ALU.mult,
                     [constB, r_pjB], [tbB])
            for which, add in ((1, 0.0), (0, 0.25)):
                dst = tb[:, which * 1024:(which + 1) * 1024]
                P.ts("dve", yy, ph, 1.0, ALU.mult, [phB], [yyB], s2=add, op1=ALU.add)
                P.cp("dve", yyi.bitcast(I32), yy, [yyB], [yyiB])
                P.cp("dve", dst, yyi.bitcast(I32), [yyiB], [tbB])
                P.tt("dve", dst, yy, dst, ALU.subtract, [yyB, tbB], [tbB])
                P.ts("dve", dst, dst, 0.5, ALU.min, [tbB], [tbB], s2=-0.5, op1=ALU.max)
                P.act(dst, dst, AF.Sin, [tbB], [tbB], scale=TWO_PI)
            wb_ = Buf("wscr_tab")
            wscr[l].append(wb_)
            P.dma("act", dap(S["s5tab"], l * 128 * 3072, [(3072, 128), (1, 3072)]), tb, [tbB], [wb_])
            rw, rwB = sview("rw", 0, 3072)
            P.dma("sp", rw[:, 0:1024], dap(I["a_re_row"], l * 1024, [(0, 128), (1, 1024)]), [], [rwB])
            P.dma("sp", rw[:, 1024:2048], dap(I["a_im_row"], l * 1024, [(0, 128), (1, 1024)]), [], [rwB])
            P.dma("sp", rw[:, 2048:3072], dap(I["ldt_row"], l * 1024, [(0, 128), (1, 1024)]), [], [rwB])
            a_re, a_im = rw[:, 0:1024], rw[:, 1024:2048]
            t0, t0B = sview("rt0", 3072, 1024)
            t1, t1B = sview("rt1", 4096, 1024)
            t2, t2B = sview("rt2", 5120, 1024)
            t3, t3B = sview("rt3", 6144, 1024)
            dt_ = rw[:, 2048:3072]
            P.act(dt_, dt_, AF.Exp, [rwB], [rwB])
            P.tt("dve", t1, a_re, dt_, ALU.mult, [rwB], [t1B])
            P.act(t1, t1, AF.Exp, [t1B], [t1B])
            P.tt("dve", t2, a_im, dt_, ALU.mult, [rwB], [t2B])

            def fracrow(dst, dstB, add):
                P.ts("dve", t0, t2, 1.0 / TWO_PI, ALU.mult, [t2B], [t0B], s2=add, op1=ALU.add)
                P.cp("dve", dst.bitcast(I32), t0, [t0B], [dstB])
                P.cp("dve", t3, dst.bitcast(I32), [dstB], [t3B])
                P.tt("dve", dst, t0, t3, ALU.subtract, [t0B, t3B], [dstB])
                P.ts("dve", dst, dst, 0.5, ALU.min, [dstB], [dstB], s2=-0.5, op1=ALU.max)
                P.act(dst, dst, AF.Sin, [dstB], [dstB], scale=TWO_PI)
            fracrow(dt_, rwB, 0.0)
            cs_, csB = sview("rcs", 6144, 1024)
            P.ts("dve", t0, t2, 1.0 / TWO_PI, ALU.mult, [t2B], [t0B], s2=0.25, op1=ALU.add)
            P.cp("dve", cs_.bitcast(I32), t0, [t0B], [csB])
            P.cp("dve", t2, cs_.bitcast(I32), [csB], [t2B])
            P.tt("dve", cs_, t0, t2, ALU.subtract, [t0B, t2B], [csB])
            P.ts("dve", cs_, cs_, 0.5, ALU.min, [csB], [csB], s2=-0.5, op1=ALU.max)
            P.act(cs_, cs_, AF.Sin, [csB], [csB], scale=TWO_PI)
            P.tt("dve", cs_, cs_, t1, ALU.mult, [csB, t1B], [csB])
            P.tt("dve", dt_, dt_, t1, ALU.mult, [rwB, t1B], [rwB])
            P.ts("dve", cs_, cs_, -1.0, ALU.add, [csB], [csB])
            P.tt("dve", t1, a_re, a_re, ALU.mult, [rwB], [t1B])
            P.tt("dve", t0, a_im, a_im, ALU.mult, [rwB], [t0B])
            P.tt("dve", t1, t1, t0, ALU.add, [t1B, t0B], [t1B])
            P.add("dve", lambda e, t1=t1: e.reciprocal(out=t1, in_=t1), [t1B], [t1B])
            P.tt("dve", t0, cs_, a_re, ALU.mult, [csB, rwB], [t0B])
            P.tt("dve", t2, dt_, a_im, ALU.mult, [rwB], [t2B])
            P.tt("dve", t0, t0, t2, ALU.add, [t0B, t2B], [t0B])
            P.tt("dve", t0, t0, t1, ALU.mult, [t0B, t1B], [t0B])
            P.tt("dve", t2, dt_, a_re, ALU.mult, [rwB], [t2B])
            P.tt("dve", cs_, cs_, a_im, ALU.mult, [csB, rwB], [csB])
            P.tt("dve", t2, t2, cs_, ALU.subtract, [t2B, csB], [t2B])
            P.tt("dve", t2, t2, t1, ALU.mult, [t2B, t1B], [t2B])
            P.dma("sp", rw[:, 0:1024], dap(I["bblk_re"], l * 128 * 1024, [(1024, 128), (1, 1024)]), [], [rwB])
            P.dma("sp", rw[:, 1024:2048], dap(I["bblk_im"], l * 128 * 1024, [(1024, 128), (1, 1024)]), [], [rwB])
            b_re, b_im = rw[:, 0:1024], rw[:, 1024:2048]
            bo, boB = sview("rbo", 2048, 1024, BF16)
            P.tt("dve", t1, t0, b_re, ALU.mult, [t0B, rwB], [t1B])
            P.tt("dve", cs_, t2, b_im, ALU.mult, [t2B, rwB], [csB])
            P.tt("dve", bo[:, 0:1024], t1, cs_, ALU.subtract, [t1B, csB], [boB])
            P.tt("dve", t1, t0, b_im, ALU.mult, [t0B, rwB], [t1B])
            P.tt("dve", cs_, t2, b_re, ALU.mult, [t2B, rwB], [csB])
            P.tt("dve", bo[:, 1024:2048], t1, cs_, ALU.add, [t1B, csB], [boB])
            wb_ = Buf("wscr_mat")
            wscr[l].append(wb_)
            P.dma("act", dap(S["s5mat"], (l * 128) * 4096, [(4096, 128), (1, 2048)]), bo, [boB], [wb_])
        chk(2)
        khB = [Buf("kh%d" % t) for t in range(NT)]
        vhB = [Buf("vh%d" % t) for t in range(NT)]
        outB = []

        def obuf(name):
            b = Buf(name)
            outB.append(b)
            return b

        class Item:
            pass

        class Stream:
            def __init__(self):
                self.items = []
                self.ip = 0
                self.cp_ = 0
                self.ring_issued = 0
                self.ring_released = 0

            def push(self, name, ring, dmas, q="sp", dstB=None):
                it = Item()
                it.name, it.ring, it.dmas, it.q, it.dstB, it.slot = name, ring, dmas, q, dstB, None
                self.items.append(it)

            def pump(self):
                while self.ip < len(self.items):
                    it = self.items[self.ip]
                    if it.ring:
                        if self.ring_issued - self.ring_released >= NW:
                            break
                        it.slot = self.ring_issued % NW
                        self.ring_issued += 1
                        for dst, src, rB in it.dmas(it.slot):
                            P.dma(it.q, dst, src, rB, [wringB[it.slot]])
                    else:
                        for dst, src, rB in it.dmas(None):
                            P.dma(it.q, dst, src, rB, [it.dstB])
                    self.ip += 1

            def get(self, name):
                it = self.items[self.cp_]
                assert it.name == name, (it.name, name)
                if self.ip <= self.cp_:
                    self.pump()
                assert self.ip > self.cp_, "stream stalled at " + name
                self.cp_ += 1
                return it

            def release(self, it):
                if it.ring:
                    self.ring_released += 1
                self.pump()

        ST_ = Stream()

        def wslot(slot, dims):
            n = int(np.prod(dims))
            return shaped(wring[:, slot, 0:n], dims)

        def push_layer_items(kind, t):
            W = []
            pre = "%s%d_" % (kind, t)
            ST_.push(pre + "s5tab", False, lambda s: [(s5tab[:].rearrange("p a b -> p (a b)"),
                                                      (lambda li: dap(S["s5tab"], li * (128 * 3072), [(3072, 128), (1, 3072)])), W)], dstB=s5tabB)
            ST_.push(pre + "s5mat", False, lambda s: [(s5mat[:].rearrange("p a b -> p (a b)"),
                                                      (lambda li: dap(S["s5mat"], li * (128 * 4096), [(4096, 128), (1, 4096)])), W)], dstB=s5matB)
            if kind == "p":
                ST_.push(pre + "swT", False, lambda s: [(swTsb[:], (lambda li: dap(S["swT_bf"], li * (128 * 512), [(512, 128), (1, 512)])), W)], dstB=swTsbB)
            else:
                ST_.push(pre + "swT", False, lambda s: [(swTssb[:], (lambda li: dap(S["swTs_bf"], li * (64 * 256), [(256, 64), (1, 256)])), W)], dstB=swTssbB)
            for nm, c0, ncol in (("A", 0, 512), ("B", 512, 256), ("Q", 768, 512), ("K", 1280, 512), ("V", 1792, 512)):
                ST_.push(pre + "in" + nm, True, lambda s, nm=nm, ncol=ncol: [
                    (wslot(s, [8 * ncol]), (lambda li, nm=nm, ncol=ncol: dap(S["in" + nm], li * (128 * 8 * ncol), [(8 * ncol, 128), (1, 8 * ncol)])), W)])
            ST_.push(pre + "glu", True, lambda s: [
                (wslot(s, [2, 512]), (lambda li: dap(S["wglu_bf"], li * (256 * 512), [(512, 128), (128 * 512, 2), (1, 512)])), W)])
            if kind == "p":
                for hf, kt in [(hf_, kt_) for hf_ in range(2) for kt_ in range(t - 1, -1, -1)]:
                    ST_.push(pre + "kv%d_%d" % (hf, kt), True, lambda s, kt=kt: [
                        (shaped(wring[:, s, 0:2048], [4, 512]), dap(S["kT_hist"], kt * 512, [(SEQ, 128), (128 * SEQ, 4), (1, 512)]), [khB[kt]]),
                        (shaped(wring[:, s, 2048:4096], [4, 512]), dap(S["v_hist"], (kt * 512) * 512, [(512, 128), (128 * 512, 4), (1, 512)]), [vhB[kt]])])
            for c in range(8):
                ST_.push(pre + "mg%d" % c, True, lambda s, c=c: [
                    (wslot(s, [4096]), (lambda li, c=c: dap(S["mg%d" % c], li * (128 * 4096), [(4096, 128), (1, 4096)])), W)])
            for hh in range(2):
                ST_.push(pre + "wo%d" % hh, True, lambda s, hh=hh: [
                    (wslot(s, [4096]), (lambda li, hh=hh: dap(S["wo%d" % hh], li * (128 * 4096), [(4096, 128), (1, 4096)])), W)])
            for pc in range(11):
                ST_.push(pre + "gu%d" % pc, True, lambda s, pc=pc: [
                    (wslot(s, [4096]), (lambda li, pc=pc: dap(S["gu%d" % pc], li * (128 * 4096), [(4096, 128), (1, 4096)])), W)])
            for hh in range(2):
                for fg, (f0, nf) in enumerate(((0, 8), (8, 8), (16, 6))):
                    ST_.push(pre + "dn%d_%d" % (hh, fg), True, lambda s, hh=hh, fg=fg, nf=nf: [
                        (wslot(s, [nf * 512]), (lambda li, hh=hh, fg=fg, nf=nf: dap(S["dn%d_%d" % (hh, fg)], li * (128 * nf * 512), [(nf * 512, 128), (1, nf * 512)])), W)])

        tiles = [("p", t) for t in range(NT)] + ([("s", 0)] if cfg.with_sample else [])
        for kind, t in tiles:
            push_layer_items(kind, t)

        evq = [0]

        def evac_eng():
            evq[0] += 1
            return "act" if evq[0] % 2 == 0 else "dve"

        def norm_to_xnT(ST, nst):
            junk, junkB = sview("junk", 0, 512, BF16)
            for s in range(nst):
                xnv, xnB = sview("xn%d" % (s % 2), 512 + (s % 2) * 512, 512, BF16)
                c_ss, c_sq, c_rs = s, 4 + s, 8 + s
                P.act(junk[0:ST, :], h[0:ST, s, :], AF.Square, [hB[s]], [junkB, smallB[c_ss]], accum=small[0:ST, c_ss:c_ss + 1])
                P.act(small[0:ST, c_sq:c_sq + 1], small[0:ST, c_ss:c_ss + 1], AF.Sqrt, [smallB[c_ss]], [smallB[c_sq]], bias=EPS, scale=1.0 / D)
                P.add("dve", lambda e, o=small[0:ST, c_rs:c_rs + 1], i=small[0:ST, c_sq:c_sq + 1]: e.reciprocal(out=o, in_=i), [smallB[c_sq]], [smallB[c_rs]])
                P.ts("pool", xnv[0:ST, :], h[0:ST, s, :], small[0:ST, c_rs:c_rs + 1], ALU.mult, [hB[s], smallB[c_rs]], [xnB], s2=0.0, op1=ALU.add)
                bk = nextbank()
                bkb = banksbf[bk]
                for kc in range(8):
                    P.tr(bkb[:, kc * 128:kc * 128 + ST], xnv[0:ST, kc * 128:(kc + 1) * 128], identb[0:ST, 0:ST], [xnB, constB], [bankB[bk]])
                P.cp(evac_eng(), xnT[:, :, s * ST:(s + 1) * ST], bkb.rearrange("p (k t) -> p k t", t=128)[:, :, 0:ST], [bankB[bk]], [xnTB[s]])

        def s5_segment(l, col0, n, st_re, st_im, stB, wg, wgB):
            xre, xreB = sview("xre", 0, 1024)
            xim, ximB = sview("xim", 1024, 1024)
            wre, wreB = sview("wre", 2048, 1024)
            wim, wimB = sview("wim", 3072, 1024)
            tt_ = [sview("s5t%d" % k, 4096 + k * 512, 512) for k in range(4)]
            srb, srbB = sview("srb", 6144, 512, BF16)
            sib, sibB = sview("sib", 6656, 512, BF16)
            m = 8 * n

            def v3(ap, j0=0, nj=8):
                return ap[:, 0:m].rearrange("p (j t) -> p j t", t=n)[:, j0:j0 + nj, :]

            def tab3(k, j0, nj):
                return s5tab[:, k, :].rearrange("p (j t) -> p j t", t=128)[:, j0:j0 + nj, 0:n]
            bks = [nextbank() for _ in range(4)]
            for ri in range(2):
                for j in range(8):
                    bk = bks[ri * 2 + j // 4]
                    jj = j % 4
                    P.mm(banks[bk][:, jj * 128:jj * 128 + n], s5mat[:, ri, j * 128:(j + 1) * 128], uabf[:, j // 4, col0:col0 + n],
                         True, True, [s5matB, uabfB], [bankB[bk]])
            for hf in range(2):
                bre = banks[bks[hf]][:, :].rearrange("p (j t) -> p j t", t=128)[:, :, 0:n]
                bim = banks[bks[2 + hf]][:, :].rearrange("p (j t) -> p j t", t=128)[:, :, 0:n]
                breB, bimB = bankB[bks[hf]], bankB[bks[2 + hf]]
                Ec, Es = tab3(0, 4 * hf, 4), tab3(1, 4 * hf, 4)
                tv = [tt_[k][0][:, 0:4 * n].rearrange("p (j t) -> p j t", t=n) for k in range(4)]
                tB = [tt_[k][1] for k in range(4)]
                P.tt("dve", tv[0], bre, Ec, ALU.mult, [breB, s5tabB], [tB[0]])
                P.tt("dve", tv[1], bim, Es, ALU.mult, [bimB, s5tabB], [tB[1]])
                P.tt("pool", v3(xre, 4 * hf, 4), tv[0], tv[1], ALU.add, [tB[0], tB[1]], [xreB])
                P.tt("dve", tv[2], bim, Ec, ALU.mult, [bimB, s5tabB], [tB[2]])
                P.tt("dve", tv[3], bre, Es, ALU.mult, [breB, s5tabB], [tB[3]])
                P.tt("pool", v3(xim, 4 * hf, 4), tv[2], tv[3], ALU.subtract, [tB[2], tB[3]], [ximB])
            P.tt("dve", small[:, 16:24], r_l[:, :], st_re, ALU.mult, [layB, stB], [smallB[16]])
            P.tt("dve", v3(xre)[:, :, 0], v3(xre)[:, :, 0], small[:, 16:24], ALU.add, [xreB, smallB[16]], [xreB])
            P.tt("dve", small[:, 24:32], r_l[:, :], st_im, ALU.mult, [layB, stB], [smallB[24]])
            P.tt("dve", v3(xim)[:, :, 0], v3(xim)[:, :, 0], small[:, 24:32], ALU.add, [ximB, smallB[24]], [ximB])
            if n == 128:
                rt, rtR = s5tab[:, 2, :], [s5tabB]
            else:
                rt = tt_[0][0][:, 0:m]
                P.cp("pool", rt.rearrange("p (j t) -> p j t", t=n), tab3(2, 0, 8), [s5tabB], [tt_[0][1]])
                rtR = [tt_[0][1]]
            P.add("dve", lambda e, o=wre[:, 0:m], d0=rt, d1=xre[:, 0:m]: e.tensor_tensor_scan(out=o, data0=d0, data1=d1, initial=0.0, op0=ALU.mult, op1=ALU.add),
                  rtR + [xreB], [wreB])
            P.add("dve", lambda e, o=wim[:, 0:m], d0=rt, d1=xim[:, 0:m]: e.tensor_tensor_scan(out=o, data0=d0, data1=d1, initial=0.0, op0=ALU.mult, op1=ALU.add),
                  rtR + [ximB], [wimB])
            for hf in range(2):
                Ec, Es = tab3(0, 4 * hf, 4), tab3(1, 4 * hf, 4)
                tv = [tt_[k][0][:, 0:4 * n].rearrange("p (j t) -> p j t", t=n) for k in range(4)]
                tB = [tt_[k][1] for k in range(4)]
                wr, wi = v3(wre, 4 * hf, 4), v3(wim, 4 * hf, 4)
                P.tt("dve", tv[0], Ec, wr, ALU.mult, [s5tabB, wreB], [tB[0]])
                P.tt("pool", tv[1], Es, wi, ALU.mult, [s5tabB, wimB], [tB[1]])
                P.tt("dve", v3(srb, 4 * hf, 4), tv[0], tv[1], ALU.subtract, [tB[0], tB[1]], [srbB])
                P.tt("pool", tv[2], Es, wr, ALU.mult, [s5tabB, wreB], [tB[2]])
                P.tt("dve", tv[3], Ec, wi, ALU.mult, [s5tabB, wimB], [tB[3]])
                P.tt("pool", v3(sib, 4 * hf, 4), tv[2], tv[3], ALU.add, [tB[2], tB[3]], [sibB])
            EcL, EsL = tab3(0, 0, 8)[:, :, n - 1], tab3(1, 0, 8)[:, :, n - 1]
            wrL, wiL = v3(wre)[:, :, n - 1], v3(wim)[:, :, n - 1]
            P.tt("dve", small[:, 32:40], EcL, wrL, ALU.mult, [s5tabB, wreB], [smallB[32]])
            P.tt("dve", small[:, 40:48], EsL, wiL, ALU.mult, [s5tabB, wimB], [smallB[40]])
            P.tt("dve", st_re, small[:, 32:40], small[:, 40:48], ALU.subtract, [smallB[32], smallB[40]], [stB])
            P.tt("dve", small[:, 32:40], EsL, wrL, ALU.mult, [s5tabB, wreB], [smallB[32]])
            P.tt("dve", small[:, 40:48], EcL, wiL, ALU.mult, [s5tabB, wimB], [smallB[40]])
            P.tt("dve", st_im, small[:, 32:40], small[:, 40:48], ALU.add, [smallB[32], smallB[40]], [stB])
            bky = nextbank()
            for mm_ in range(2):
                for j in range(4 * mm_, 4 * mm_ + 4):
                    P.mm(banks[bky][:, mm_ * 128:mm_ * 128 + n], s5mat[:, 2, j * 128:(j + 1) * 128], v3(srb)[:, j, :],
                         j == 4 * mm_, False, [s5matB, srbB], [bankB[bky]])
                    P.mm(banks[bky][:, mm_ * 128:mm_ * 128 + n], s5mat[:, 3, j * 128:(j + 1) * 128], v3(sib)[:, j, :],
                         False, j == 4 * mm_ + 3, [s5matB, sibB], [bankB[bky]])
            yv, yvB = tt_[0]
            y2, y2B = tt_[1]
            sg, sgB = tt_[2]
            gl, glB = sview("glbf", 4096 + 3 * 512, 256, BF16)
            yv3 = yv[:, 0:2 * n].rearrange("p (a t) -> p a t", t=n)
            y23 = y2[:, 0:2 * n].rearrange("p (a t) -> p a t", t=n)
            sg3 = sg[:, 0:2 * n].rearrange("p (a t) -> p a t", t=n)
            gl3 = gl[:, 0:2 * n].rearrange("p (a t) -> p a t", t=n)
            for mm_ in range(2):
                P.stt(yv3[:, mm_, :], uaT[:, mm_, col0:col0 + n], d_l[:, mm_:mm_ + 1], banks[bky][:, mm_ * 128:mm_ * 128 + n],
                      ALU.mult, ALU.add, [uaTB, layB, bankB[bky]], [yvB])
            P.tt("dve", y23, yv3, yv3, ALU.mult, [yvB], [y2B])
            P.ts("dve", y23, y23, 0.044715, ALU.mult, [y2B], [y2B], s2=1.0, op1=ALU.add)
            P.tt("dve", y23, y23, yv3, ALU.mult, [y2B, yvB], [y2B])
            P.act(sg3, y23, AF.Sigmoid, [y2B], [sgB], scale=1.5957691216057308)
            P.tt("dve", gl3, yv3, sg3, ALU.mult, [yvB, sgB], [glB])
            bkz = nextbank()
            for oc in range(4):
                for kc in range(2):
                    P.mm(banks[bkz][:, oc * 128:oc * 128 + n], wg[:, kc, oc * 128:(oc + 1) * 128], gl3[:, kc, :], kc == 0, kc == 1,
                         [wgB, glB], [bankB[bkz]])
            z3 = banks[bkz][:, :].rearrange("p (a t) -> p a t", t=128)
            P.act(sg3, z3[:, 2:4, 0:n], AF.Sigmoid, [bankB[bkz]], [sgB])
            P.tt("dve", yaT[:, :, col0:col0 + n], z3[:, 0:2, 0:n], sg3, ALU.mult, [bankB[bkz], sgB], [yaTB])

        class Blk:
            pass

        def run_attention(blocks, Kmax):
            eV = [sview("at_e%d" % i, i * 1024, 1024) for i in range(2)]
            sp, spB = sview("at_sp", 2048, 1024)
            pf, pfB = sview("at_pf", 3072, 1040)
            arg, argB = sview("at_arg", 4112, 1024)
            pbV = [sview("at_pb%d" % i, 5136 + i * 512, 512, BF16) for i in range(2)]
            ptV = [sview("at_pt%d" % i, 6160 + i * 512, 512, BF16) for i in range(2)]
            zmVV = [sview("at_zm%d" % i, 7184 + i * 512, 512) for i in range(2)]
            pf3 = pf.rearrange("p (g t) -> p g t", t=520)
            P.memset("pool", pf3[:, :, 0:1], 0.0, [pfB])
            nB = len(blocks)

            def zbuf(i):
                k = i % 2
                return zpair[k], [bankB[2 * k], bankB[2 * k + 1]]

            def S1(i):
                b = blocks[i]
                zt, zB = zbuf(i)
                for g, sub in enumerate(b.subs):
                    P.mm(zt[0:b.nq, g * 512:g * 512 + b.w], sub[0], b.kT, True, True, b.qkB, zB)
                if b.mask is not None:
                    zmV = zmVV[i % 2]
                    P.tt("dve", zmV[0][0:b.nq, 0:b.w], zt[0:b.nq, 0:b.w], b.mask, ALU.add, zB + [constB], [zmV[1]])

            def S2(i):
                b = blocks[i]
                zt, zB = zbuf(i)
                e, eB = eV[i % 2]
                G = len(b.subs)
                if G == 2:
                    P.act(e[0:b.nq, 0:1024], zt[0:b.nq, 0:1024], AF.Exp, zB, [eB])
                    P.act(sp[0:b.nq, 0:1024], e[0:b.nq, 0:1024], AF.Ln, [eB], [spB], bias=1.0)
                    return
                if b.mask is not None:
                    zmV = zmVV[i % 2]
                    zsrc, zR = zmV[0][0:b.nq, 0:b.w], [zmV[1]]
                else:
                    zsrc, zR = zt[0:b.nq, 0:b.w], zB
                P.act(e[0:b.nq, 0:b.w], zsrc, AF.Exp, zR, [eB])
                ts_ = i % 4
                P.act(sp[0:b.nq, 0:b.w], e[0:b.nq, 0:b.w], AF.Ln, [eB], [spB, totB[ts_]], bias=1.0, accum=tot[0:b.nq, ts_:ts_ + 1])
                if b.first:
                    old, oldB = zcol[0:b.nq, 0:1], constB
                else:
                    old, oldB = nbias[0:b.nq, b.idx0, 1 - b.par:2 - b.par], nbiasB[b.idx0][1 - b.par]
                P.tt("dve", nbias[0:b.nq, b.idx0, b.par:b.par + 1], old, tot[0:b.nq, ts_:ts_ + 1], ALU.subtract,
                     [oldB, totB[ts_]], [nbiasB[b.idx0][b.par]])

            def S3(i):
                b = blocks[i]
                e, eB = eV[i % 2]
                pb, pbB = pbV[i % 2]
                G = len(b.subs)
                for g in range(G):
                    P.add("dve", lambda e_, o=pf3[0:b.nq, g, 1:b.w + 1], d0=ones[0:b.nq, 0:b.w], d1=sp[0:b.nq, g * 512:g * 512 + b.w]:
                          e_.tensor_tensor_scan(out=o, data0=d0, data1=d1, initial=0.0, op0=ALU.mult, op1=ALU.add), [constB, spB], [pfB])
                if G == 2:
                    nbw = [nbiasB[b.idx0][b.par], nbiasB[b.idx0 + 1][b.par]]
                    nbr = [nbiasB[b.idx0][1 - b.par], nbiasB[b.idx0 + 1][1 - b.par]]
                    P.tt("dve", nbias[0:b.nq, b.idx0:b.idx0 + 2, b.par], nbias[0:b.nq, b.idx0:b.idx0 + 2, 1 - b.par], pf3[0:b.nq, 0:2, 512],
                         ALU.subtract, nbr + [pfB], nbw)
                    for g in range(2):
                        P.act(arg[0:b.nq, g * 512:(g + 1) * 512], pf3[0:b.nq, g, 0:512], AF.Exp, [pfB, nbw[g]], [argB],
                              bias=nbias[0:b.nq, b.idx0 + g, b.par:b.par + 1])
                    P.tt("pool", pb[0:b.nq, 0:1024], e[0:b.nq, 0:1024], arg[0:b.nq, 0:1024], ALU.mult, [eB, argB], [pbB])
                else:
                    P.act(arg[0:b.nq, 0:b.w], pf3[0:b.nq, 0, 0:b.w], AF.Exp, [pfB, nbiasB[b.idx0][b.par]], [argB], bias=nbias[0:b.nq, b.idx0, b.par:b.par + 1])
                    P.tt("pool", pb[0:b.nq, 0:b.w], e[0:b.nq, 0:b.w], arg[0:b.nq, 0:b.w], ALU.mult, [eB, argB], [pbB])

            def S4(i):
                b = blocks[i]
                pb, pbB = pbV[i % 2]
                pt, ptB = ptV[i % 2]
                bk = 4 + (i % 2)
                bkb = banksbf[bk]
                G = len(b.subs)
                vs = b.vs
                for g in range(G):
                    for j, (vap, K) in enumerate(vs):
                        P.tr(bkb[0:K, g * 512 + j * 128:g * 512 + j * 128 + b.nq], pb[0:b.nq, g * 512 + j * 128:g * 512 + j * 128 + K],
                             identb[0:b.nq, 0:b.nq], [pbB, constB], [bankB[bk]])
                ncol = 1024 if G == 2 else len(vs) * 128
                P.cp("act" if (G == 2 and i % 2 == 0) else "dve", pt[0:Kmax, 0:ncol], bkb[0:Kmax, 0:ncol], [bankB[bk]], [ptB])

            def S5(i):
                b = blocks[i]
                pt, ptB = ptV[i % 2]
                vs = b.vs
                nsub = len(vs)
                for g, sub in enumerate(b.subs):
                    for j, (vap, K) in enumerate(vs):
                        P.mm(sub[1], pt[0:K, g * 512 + j * 128:g * 512 + j * 128 + b.nq], vap, False, b.last and j == nsub - 1, [ptB] + b.vB, [sub[2]], skip=True)
                if b.done is not None:
                    b.done()

            for it in range(nB + 4):
                if 0 <= it - 4 < nB:
                    S5(it - 4)
                if 0 <= it - 3 < nB:
                    S4(it - 3)
                if 0 <= it - 2 < nB:
                    S3(it - 2)
                if 0 <= it - 1 < nB:
                    S2(it - 1)
                if it < nB:
                    if blocks[it].pre is not None:
                        blocks[it].pre()
                    S1(it)

        def layer_tile(kind, t, l):
            prompt = kind == "p"
            T = 512 if prompt else 64
            ST = 128 if prompt else 64
            nst = T // ST
            pre = "%s%d_" % (kind, t)
            xall = xnTB[0:nst]
            norm_to_xnT(ST, nst)
            chk(3)
            it_tab = ST_.get(pre + "s5tab")
            it_mat = ST_.get(pre + "s5mat")
            it_sw = ST_.get(pre + "swT")
            chk(3.05)
            itA = ST_.get(pre + "inA")
            chk(3.07)
            wA = wslot(itA.slot, [8, 512])
            wAB = wringB[itA.slot]
            for ct in range(4):
                bk = nextbank()
                for kc in range(8):
                    P.mm(banks[bk][:, 0:T], wA[:, kc, ct * 128:(ct + 1) * 128], xnT[:, kc, 0:T], kc == 0, kc == 7, [wAB] + xall, [bankB[bk]])
                if ct < 2:
                    P.cp("act", uaT[:, ct, 0:T], banks[bk][:, 0:T], [bankB[bk]], [uaTB])
                    P.cp("dve", uabf[:, ct, 0:T], banks[bk][:, 0:T], [bankB[bk]], [uabfB])
                else:
                    P.cp(evac_eng(), ubT[:, ct - 2, 0:T], banks[bk][:, 0:T], [bankB[bk]], [ubTB])
            ST_.release(itA)
            chk(3.1)
            itB = ST_.get(pre + "inB")
            wB_ = wslot(itB.slot, [8, 256])
            vb_banks = []
            for s in range(nst):
                bk = nextbank((2, 3, 4, 5))
                vb_banks.append(bk)
                for kc in range(8):
                    P.mm(banks[bk][0:ST, 0:256], xnT[:, kc, s * ST:(s + 1) * ST], wB_[:, kc, :], kc == 0, kc == 7, [wringB[itB.slot], xnTB[s]], [bankB[bk]])
            ST_.release(itB)
            chk(3.2)
            WT = swTsb[:].rearrange("p (g t) -> p g t", t=128) if prompt else swTssb[:].rearrange("p (g t) -> p g t", t=64)
            WTB = swTsbB if prompt else swTssbB
            sbias = sbbc_l if prompt else sbbcs_l
            sgt = [sview("sg_t%d" % k, 6144 + k * 128, 128) for k in range(2)]
            for s in range(nst):
                bk = vb_banks[s]
                vb = banks[bk][0:ST, 0:256]
                P.add("dve", lambda e, o=small[0:ST, 48:54], i=vb: e.bn_stats(out=o, in_=i), [bankB[bk]], [smallB[48]])
                P.add("dve", lambda e, o=small[0:ST, 54:56], i=small[0:ST, 48:54]: e.bn_aggr(out=o, in_=i), [smallB[48]], [smallB[54]])
                P.act(small[0:ST, 56:57], small[0:ST, 55:56], AF.Sqrt, [smallB[54]], [smallB[56]], bias=EPS, scale=1.0)
                P.add("dve", lambda e, o=small[0:ST, 57:58], i=small[0:ST, 56:57]: e.reciprocal(out=o, in_=i), [smallB[56]], [smallB[57]])
                P.ts("dve", small[0:ST, 58:59], small[0:ST, 54:55], small[0:ST, 57:58], ALU.mult, [smallB[54], smallB[57]], [smallB[58]], s2=-1.0, op1=ALU.mult)
                P.act(vn0[0:ST, :], vb, AF.Identity, [bankB[bk], smallB[57], smallB[58]], [vn0B], bias=small[0:ST, 58:59], scale=small[0:ST, 57:58])
                sl = s % 2
                if prompt:
                    P.tt("pool", vnbf[0:ST, sl, :], vn0[0:ST, :], sgn_l[0:ST, :], ALU.mult, [vn0B, layB], [vnbfB[sl]])
                else:
                    P.tt("pool", vnf[0:ST, :], vn0[0:ST, :], sgn_l[0:ST, :], ALU.mult, [vn0B, layB], [vnfB])
                    P.cp("pool", vnbf[0:ST, sl, :], vnf[0:ST, :], [vnfB], [vnbfB[sl]])
                    P.dma("act", dap(S["svb_s"], 0, [(256, 64), (1, 256)]), vnf[0:ST, :], [vnfB], [obuf("svb")])
                bkm = nextbank((0, 1, 6, 7))
                for g in range(4):
                    P.mm(banks[bkm][64 * (g % 2):64 * (g % 2) + 64, (g // 2) * 128:(g // 2) * 128 + ST],
                         vnbf[0:ST, sl, g * 64:(g + 1) * 64], WT[0:ST, g, 0:ST], True, True, [vnbfB[sl], WTB], [bankB[bkm]])
                for i2 in range(2):
                    tmp, tmpB = sgt[i2]
                    P.tt("dve", tmp[:, 0:ST], banks[bkm][:, i2 * 128:i2 * 128 + ST], sbias[:, i2, 0:ST], ALU.add, [bankB[bkm], layB], [tmpB])
                    P.tt("dve", ybT[:, i2, s * ST:(s + 1) * ST], tmp[:, 0:ST], ubT[:, i2, s * ST:(s + 1) * ST], ALU.mult, [tmpB, ubTB], [ybTB])
            ST_.release(it_sw)
            chk(3.3)
            itQ = ST_.get(pre + "inQ")
            wQ = wslot(itQ.slot, [8, 512])
            for p_ in range(4):
                bk = nextbank()
                for kc in range(8):
                    P.mm(banks[bk][:, 0:T], wQ[:, kc, p_ * 128:(p_ + 1) * 128], xnT[:, kc, 0:T], kc == 0, kc == 7, [wringB[itQ.slot]] + xall, [bankB[bk]])
                P.ts("dve", qTz[0:64, p_, 0, 0:T], banks[bk][0:64, 0:T], 0.125, ALU.mult, [bankB[bk]], [qTzB[p_]])
                P.act(qTz[64:128, p_, 1, 0:T], banks[bk][64:128, 0:T], AF.Identity, [bankB[bk]], [qTzB[p_]], scale=0.125)
            ST_.release(itQ)
            chk(3.4)
            itK = ST_.get(pre + "inK")
            wK = wslot(itK.slot, [8, 512])
            for p_ in range(4):
                bk = nextbank()
                for kc in range(8):
                    P.mm(banks[bk][:, 0:T], wK[:, kc, p_ * 128:(p_ + 1) * 128], xnT[:, kc, 0:T], kc == 0, kc == 7, [wringB[itK.slot]] + xall, [bankB[bk]])
                P.cp(evac_eng(), kT[:, p_, 0:T], banks[bk][:, 0:T], [bankB[bk]], [kTB])
            for s in range(nst):
                bk = nextbank()
                for kc in range(8):
                    P.mm(banks[bk][0:ST, :], xnT[:, kc, s * ST:(s + 1) * ST], wK[:, kc, :], kc == 0, kc == 7, [wringB[itK.slot], xnTB[s]], [bankB[bk]])
                P.cp("act", kout[0:ST, :], banks[bk][0:ST, :], [bankB[bk]], [koutB])
                if prompt:
                    tok0 = t * 512 + s * 128
                    P.dma("act", dap(S["pk_s"], tok0 * 64, [(64, 128), (SEQ * 64, NH), (1, 64)]),
                          kout[:].rearrange("p (h d) -> p h d", d=64), [koutB], [obuf("pk")])
                else:
                    for b in range(2):
                        P.dma("act", dap(S["sk_s"], b * NH * 32 * 64, [(64, 32), (32 * 64, NH), (1, 64)]),
                              kout[32 * b:32 * b + 32, :].rearrange("p (h d) -> p h d", d=64), [koutB], [obuf("sk")])
            ST_.release(itK)
            chk(3.5)
            if prompt and t < NT - 1:
                P.dma("act", dap(S["kT_hist"], t * 512, [(SEQ, 128), (128 * SEQ, 4), (1, 512)]), kT[:], [kTB], [khB[t]])
            itV = ST_.get(pre + "inV")
            wV = wslot(itV.slot, [8, 512])
            for s in range(nst):
                bk = nextbank()
                for kc in range(8):
                    P.mm(banks[bk][0:ST, :], xnT[:, kc, s * ST:(s + 1) * ST], wV[:, kc, :], kc == 0, kc == 7, [wringB[itV.slot], xnTB[s]], [bankB[bk]])
                P.cp("act", vout[0:ST, :], banks[bk][0:ST, :], [bankB[bk]], [voutB])
                P.cp("dve", vtok[0:ST, s, :], banks[bk][0:ST, :], [bankB[bk]], [vtokB[s]])
                if prompt:
                    tok0 = t * 512 + s * 128
                    P.dma("act", dap(S["pv_s"], tok0 * 64, [(64, 128), (SEQ * 64, NH), (1, 64)]),
                          vout[:].rearrange("p (h d) -> p h d", d=64), [voutB], [obuf("pv")])
                else:
                    for b in range(2):
                        P.dma("act", dap(S["sv_s"], b * NH * 32 * 64, [(64, 32), (32 * 64, NH), (1, 64)]),
                              vout[32 * b:32 * b + 32, :].rearrange("p (h d) -> p h d", d=64), [voutB], [obuf("sv")])
            ST_.release(itV)
            if prompt and t < NT - 1:
                P.dma("act", dap(S["v_hist"], (t * 512) * 512, [(512, 128), (128 * 512, 4), (1, 512)]), vtok[:], vtokB, [vhB[t]])
            chk(4)
            itG = ST_.get(pre + "glu")
            wg = wslot(itG.slot, [2, 512])
            if prompt:
                for s in range(nst):
                    s5_segment(l, s * 128, 128, sprev[:, 0, :], sprev[:, 1, :], sprevB, wg, wringB[itG.slot])
            else:
                for b in range(2):
                    P.cp("pool", ssm_s[:, b, 0, :], s0re_l[:, b, :], [layB], [ssm_sB])
                    P.cp("pool", ssm_s[:, b, 1, :], s0im_l[:, b, :], [layB], [ssm_sB])
                    s5_segment(l, 32 * b, 32, ssm_s[:, b, 0, :], ssm_s[:, b, 1, :], ssm_sB, wg, wringB[itG.slot])
                    P.dma("act", dap(S["sre_s"], b * 1024, [(8, 128), (1, 8)]), ssm_s[:, b, 0, :], [ssm_sB], [obuf("sre")])
                    P.dma("act", dap(S["sim_s"], b * 1024, [(8, 128), (1, 8)]), ssm_s[:, b, 1, :], [ssm_sB], [obuf("sim")])
            ST_.release(itG)
            ST_.release(it_tab)
            ST_.release(it_mat)
            chk(5)
            held = {}
            osb, osbB = sview("at_osb", 0, 1024, BF16)
            if prompt:
                obk = (6, 7)
                for half in range(2):
                    blocks = []
                    for o in obk:
                        P.mm(banks[o][:, :], zerob[:, 0:128], zerob[:, :], True, False, [constB], [bankB[o]], skip=True)
                    order = [("cur", None)] + [("hist", kt) for kt in range(t - 1, -1, -1)]
                    hds = range(4 * half, 4 * half + 4)
                    for oi, (ty, kt) in enumerate(order):
                        for hd in hds:
                            ob = obk[hd // 2 - 2 * half]
                            if ty == "cur":
                                for qs in range(4):
                                    w = 128 * (qs + 1)
                                    b = Blk()
                                    b.nq, b.w = 128, w
                                    b.subs = [(qTz[:, hd // 2, hd % 2, qs * 128:(qs + 1) * 128],
                                               banks[ob][:, qs * 128 + (hd % 2) * 64:qs * 128 + (hd % 2) * 64 + 64], bankB[ob])]
                                    b.idx0 = hd * 4 + qs
                                    b.par, b.first, b.last = oi % 2, True, oi == len(order) - 1
                                    b.pre, b.done = None, None
                                    b.kT = kT[:, hd // 2, 0:w]
                                    b.qkB = [qTzB[hd // 2], kTB]
                                    b.mask = Mfull[:, 512 - w:512]
                                    b.vs = [(vtok[:, j, hd * 64:(hd + 1) * 64], 128) for j in range(qs + 1)]
                                    b.vB = [vtokB[j] for j in range(qs + 1)]
                                    b.hd = hd
                                    blocks.append(b)
                            else:
                                nm = pre + "kv%d_%d" % (half, kt)
                                for pq in range(2):
                                    b = Blk()
                                    b.nq, b.w = 128, 512
                                    b.subs = []
                                    for qs in (2 * pq, 2 * pq + 1):
                                        b.subs.append((qTz[:, hd // 2, hd % 2, qs * 128:(qs + 1) * 128],
                                                       banks[ob][:, qs * 128 + (hd % 2) * 64:qs * 128 + (hd % 2) * 64 + 64], bankB[ob]))
                                    b.idx0 = hd * 4 + 2 * pq
                                    b.par, b.first, b.last = oi % 2, False, oi == len(order) - 1
                                    b.pre, b.done = None, None
                                    b.mask = None
                                    if hd == hds[0] and pq == 0:
                                        def pre_fn(nm=nm):
                                            held[nm] = ST_.get(nm)
                                        b.pre = pre_fn
                                    if hd == hds[-1] and pq == 1:
                                        def done_fn(nm=nm):
                                            ST_.release(held[nm])
                                        b.done = done_fn
                                    b.lazy = nm
                                    b.hd = hd
                                    b.__class__ = LazyBlk
                                    b.held = held
                                    blocks.append(b)
                    run_attention(blocks, 128)
                    for pi in range(2):
                        p_ = 2 * half + pi
                        o = obk[pi]
                        ov = osb[:, pi * 512:(pi + 1) * 512]
                        P.cp("dve", ov, banks[o][:, :], [bankB[o]], [osbB])
                        bk = 4 + pi
                        bkb = banksbf[bk]
                        for qs in range(4):
                            P.tr(bkb[:, qs * 128:(qs + 1) * 128], ov[:, qs * 128:(qs + 1) * 128], identb[:], [osbB, constB], [bankB[bk]])
                        P.cp("dve", ycT[:, p_, :], bkb[:, 0:512], [bankB[bk]], [ycTB])
            else:
                obk = (6, 7)
                for o in obk:
                    P.mm(banks[o][:, :], zerob[:, 0:128], zerob[:, :], True, False, [constB], [bankB[o]], skip=True)
                for b_ in range(2):
                    for (dstbuf, dstB_, nm_) in ((ckbuf, ckB, "ckT%d" % b_), (cvbuf, cvB, None)):
                        for ci in range(NCH):
                            if nm_ is not None:
                                src = (lambda li, nm_=nm_, ci=ci: dap(I[nm_], li * (128 * 4 * PAST) + ci * 2048, [(4 * PAST, 128), (1, 2048)]))
                            else:
                                src = (lambda li, b_=b_, ci=ci: dap(I["cv%d_%d" % (b_, ci)], li * (128 * 2048), [(2048, 128), (1, 2048)]))
                            P.dma("sp", cstg[:], src, [], [cstgB])
                            P.cp("pool", dstbuf[:, ci * 2048:(ci + 1) * 2048], cstg[:], [cstgB], [dstB_])
                    blocks = []
                    order = [("cur", None)] + [("hist", kb) for kb in range(NPB - 1, -1, -1)]
                    for oi, (ty, kb) in enumerate(order):
                        for hd in range(NH):
                            b = Blk()
                            b.nq = 32
                            b.subs = [(qTz[:, hd // 2, hd % 2, 32 * b_:32 * b_ + 32], banks[obk[b_]][0:32, hd * 64:(hd + 1) * 64], bankB[obk[b_]])]
                            b.idx0 = b_ * 8 + hd
                            b.par = oi % 2
                            b.first = oi == 0
                            b.last = oi == len(order) - 1
                            b.pre = None
                            b.done = None
                            b.hd = hd
                            if ty == "cur":
                                b.w = 64
                                b.kT = kT[:, hd // 2, 0:64]
                                b.qkB = [qTzB[hd // 2], kTB]
                                b.mask = Ms[:, b_, :]
                                b.vs = [(vtok[0:64, 0, hd * 64:(hd + 1) * 64], 64)]
                                b.vB = [vtokB[0]]
                            else:
                                b.w = 512
                                b.mask = None
                                b.kT = ckbuf[:].rearrange("p (a t) -> p a t", t=PAST)[:, hd // 2, kb * 512:(kb + 1) * 512]
                                b.qkB = [qTzB[hd // 2], ckB]
                                v4 = cvbuf[:].rearrange("p (s h d) -> p s h d", h=NH, d=64)
                                b.vs = [(v4[:, kb * 4 + j, hd, :], 128) for j in range(4)]
                                b.vB = [cvB]
                            blocks.append(b)
                    run_attention(blocks, 128)
                bk = 4
                bkb = banksbf[bk]
                for b_ in range(2):
                    ov = osb[0:32, b_ * 512:(b_ + 1) * 512]
                    P.cp(evac_eng(), ov, banks[obk[b_]][0:32, :], [bankB[obk[b_]]], [osbB])
                    for p_ in range(4):
                        P.tr(bkb[:, p_ * 64 + 32 * b_:p_ * 64 + 32 * b_ + 32], ov[:, p_ * 128:(p_ + 1) * 128], identb[0:32, 0:32], [osbB, constB], [bankB[bk]])
                P.cp(evac_eng(), ycT[:, :, 0:64], bkb[:, 0:256].rearrange("p (a t) -> p a t", t=64), [bankB[bk]], [ycTB])
            chk(6)
            sgV = [sview("mg_sg%d" % i, i * 512, 512) for i in range(3)]
            mV = [sview("mg_m%d" % i, 1536 + i * 512, 512) for i in range(3)]
            for c in range(8):
                itM = ST_.get(pre + "mg%d" % c)
                wM = wslot(itM.slot, [8, 512])
                wMB = wringB[itM.slot]
                gb = []
                for i in range(3):
                    bk = nextbank()
                    gb.append(bk)
                    for kc in range(8):
                        P.mm(banks[bk][:, 0:T], wM[:, kc, i * 128:(i + 1) * 128], xnT[:, kc, 0:T], kc == 0, kc == 7, [wMB] + xall, [bankB[bk]])
                    P.act(sgV[i][0][:, 0:T], banks[bk][:, 0:T], AF.Sigmoid, [bankB[bk]], [sgV[i][1]])
                bb = []
                for i, (k0, nk, src, srcB) in enumerate(((0, 2, yaT, yaTB), (2, 2, ybT, ybTB), (4, 4, ycT, ycTB))):
                    bk = nextbank()
                    bb.append(bk)
                    for k in range(nk):
                        P.mm(banks[bk][:, 0:T], wM[:, k0 + k, 384:512], src[:, k, 0:T], k == 0, k == nk - 1, [wMB, srcB], [bankB[bk]])
                    P.tt("dve", mV[i][0][:, 0:T], banks[bk][:, 0:T], sgV[i][0][:, 0:T], ALU.mult, [bankB[bk], sgV[i][1]], [mV[i][1]])
                ST_.release(itM)
                P.tt("pool", mV[0][0][:, 0:T], mV[0][0][:, 0:T], mV[1][0][:, 0:T], ALU.add, [mV[0][1], mV[1][1]], [mV[0][1]])
                P.tt("pool", mgT[:, c, 0:T], mV[0][0][:, 0:T], mV[2][0][:, 0:T], ALU.add, [mV[0][1], mV[2][1]], [mgTB[c]])
            if getattr(cfg, "debug", False) and prompt and t == 0 and l == 0:
                P.dma("act", dap(DBG, 0, [(512, 128), (128 * 512, 2), (1, 512)]), yaT[:], [yaTB], [obuf("dbg")])
                P.dma("act", dap(DBG, 2 * 128 * 512, [(512, 128), (128 * 512, 2), (1, 512)]), ybT[:], [ybTB], [obuf("dbg")])
                P.dma("act", dap(DBG, 4 * 128 * 512, [(512, 128), (128 * 512, 4), (1, 512)]), ycT[:], [ycTB], [obuf("dbg")])
                P.dma("act", dap(DBG, 8 * 128 * 512, [(512, 128), (128 * 512, 8), (1, 512)]), mgT[:], mgTB, [obuf("dbg")])
            chk(7)
            for hh in range(2):
                itO = ST_.get(pre + "wo%d" % hh)
                wO = wslot(itO.slot, [8, 512])
                for s in range(nst):
                    bk = nextbank()
                    for kc in range(8):
                        P.mm(banks[bk][0:ST, :], mgT[:, kc, s * ST:(s + 1) * ST], wO[:, kc, :], kc == 0, kc == 7, [wringB[itO.slot], mgTB[kc]], [bankB[bk]])
                    P.tt("dve", h[0:ST, s, hh * 512:(hh + 1) * 512], h[0:ST, s, hh * 512:(hh + 1) * 512], banks[bk][0:ST, :], ALU.add, [hB[s], bankB[bk]], [hB[s]])
                ST_.release(itO)
            chk(8)
            norm_to_xnT(ST, nst)
            actT, actTB = sview("actT", 0, 5632, BF16)
            act3 = actT.rearrange("p (f t) -> p f t", t=512)
            slV = [sview("ffn_sl%d" % i, 5632 + i * 512, 512) for i in range(2)]
            for pc in range(11):
                itU = ST_.get(pre + "gu%d" % pc)
                wU = wslot(itU.slot, [8, 512])
                for sl in range(2):
                    f = 2 * pc + sl
                    bg, bu = nextbank(), nextbank()
                    for kc in range(8):
                        P.mm(banks[bg][:, 0:T], wU[:, kc, sl * 128:(sl + 1) * 128], xnT[:, kc, 0:T], kc == 0, kc == 7, [wringB[itU.slot]] + xall, [bankB[bg]])
                    for kc in range(8):
                        P.mm(banks[bu][:, 0:T], wU[:, kc, 256 + sl * 128:256 + (sl + 1) * 128], xnT[:, kc, 0:T], kc == 0, kc == 7, [wringB[itU.slot]] + xall, [bankB[bu]])
                    sv_, svB = slV[f % 2]
                    P.act(sv_[:, 0:T], banks[bg][:, 0:T], AF.Silu, [bankB[bg]], [svB])
                    P.tt("dve", act3[:, f, 0:T], banks[bu][:, 0:T], sv_[:, 0:T], ALU.mult, [bankB[bu], svB], [actTB])
                ST_.release(itU)
            for hh in range(2):
                bs = [nextbank((0, 1, 2, 3)) if hh == 0 else nextbank((4, 5, 6, 7)) for s in range(nst)]
                for fg, (f0, nf) in enumerate(((0, 8), (8, 8), (16, 6))):
                    itD = ST_.get(pre + "dn%d_%d" % (hh, fg))
                    wD = wslot(itD.slot, [nf, 512])
                    for s in range(nst):
                        for f in range(f0, f0 + nf):
                            P.mm(banks[bs[s]][0:ST, :], act3[:, f, s * ST:(s + 1) * ST], wD[:, f - f0, :], f == 0, f == NF - 1, [wringB[itD.slot], actTB], [bankB[bs[s]]])
                    ST_.release(itD)
                for s in range(nst):
                    P.tt("dve", h[0:ST, s, hh * 512:(hh + 1) * 512], h[0:ST, s, hh * 512:(hh + 1) * 512], banks[bs[s]][0:ST, :], ALU.add, [hB[s], bankB[bs[s]]], [hB[s]])

        class LazyBlk(Blk):
            @property
            def kT(self):
                s = self.held[self.lazy].slot
                return shaped(wring[:, s, 0:2048], [4, 512])[:, self.hd // 2, :]

            @property
            def qkB(self):
                return [qTzB[self.hd // 2], wringB[self.held[self.lazy].slot]]

            @property
            def vs(self):
                s = self.held[self.lazy].slot
                v4 = shaped(wring[:, s, 2048:4096], [4, 512])
                return [(v4[:, j, self.hd * 64:(self.hd + 1) * 64], 128) for j in range(4)]

            @property
            def vB(self):
                return [wringB[self.held[self.lazy].slot]]

        class LazyBlkS(Blk):
            @property
            def kT(self):
                nmk, nmv, kb = self.lazy_s
                s = self.held[nmk].slot
                return shaped(wring[:, s, 0:4 * self.PAST], [4, self.PAST])[:, self.hd // 2, kb * 512:(kb + 1) * 512]

            @property
            def qkB(self):
                return [qTzB[self.hd // 2], wringB[self.held[self.lazy_s[0]].slot]]

            @property
            def vs(self):
                nmk, nmv, kb = self.lazy_s
                s = self.held[nmv].slot
                v4 = shaped(wring[:, s, 0:self.PAST * 4], [self.PAST // 128, NH, 64])
                return [(v4[:, kb * 4 + j, self.hd, :], 128) for j in range(4)]

            @property
            def vB(self):
                return [wringB[self.held[self.lazy_s[1]].slot]]

        def final_out(kind, t):
            prompt = kind == "p"
            ST = 128 if prompt else 64
            nst = 4 if prompt else 1
            junk, junkB = sview("junk", 0, 512, BF16)
            for s in range(nst):
                yo, yoB = sview("yo%d" % (s % 2), 1024 + (s % 2) * 1024, 1024)
                c_ss, c_sq, c_rs = s, 4 + s, 8 + s
                P.act(junk[0:ST, :], h[0:ST, s, :], AF.Square, [hB[s]], [junkB, smallB[c_ss]], accum=small[0:ST, c_ss:c_ss + 1])
                P.act(small[0:ST, c_sq:c_sq + 1], small[0:ST, c_ss:c_ss + 1], AF.Sqrt, [smallB[c_ss]], [smallB[c_sq]], bias=EPS, scale=1.0 / D)
                P.add("dve", lambda e, o=small[0:ST, c_rs:c_rs + 1], i=small[0:ST, c_sq:c_sq + 1]: e.reciprocal(out=o, in_=i), [smallB[c_sq]], [smallB[c_rs]])
                P.stt(yo[0:ST, :], h[0:ST, s, :], small[0:ST, c_rs:c_rs + 1], gfin[0:ST, :], ALU.mult, ALU.mult, [hB[s], smallB[c_rs], constB], [yoB])
                if prompt:
                    P.dma("act", dap(O["yp"], (t * 512 + s * 128) * D, [(D, 128), (1, D)]), yo[:, :], [yoB], [obuf("yp")])
                else:
                    P.dma("act", dap(O["ys"], 0, [(D, 64), (1, D)]), yo[0:64, :], [yoB], [obuf("ys")])

        hbB = [Buf("hb%d" % i) for i in range(NT + 1)]
        for t in range(NT):
            P.dma("sp", dap(S["hbuf"], t * 512 * D, [(4 * D, 128), (1, 4 * D)]), dap(I["xp"], t * 512 * D, [(4 * D, 128), (1, 4 * D)]), [], [hbB[t]])
        P.dma("sp", dap(S["hbuf"], SEQ * D, [(D, 64), (1, D)]), dap(I["xs"], 0, [(D, 64), (1, D)]), [], [hbB[NT]])
        chk(2.2)
        P.barrier()
        P.loop_begin(L)
        l = None
        P.dma("sp", r_l[:], (lambda li: dap(S["rpj_d"], li * 1024, [(8, 128), (1, 8)])), [], [layB])
        P.dma("sp", d_l[:], (lambda li: dap(I["d_pm"], li * 2, [(L * 2, 128), (1, 2)])), [], [layB])
        P.dma("sp", s0re_l[:].rearrange("p a b -> p (a b)"), (lambda li: dap(I["s0_re"], li * 16, [(L * 16, 128), (1, 16)])), [], [layB])
        P.dma("sp", s0im_l[:].rearrange("p a b -> p (a b)"), (lambda li: dap(I["s0_im"], li * 16, [(L * 16, 128), (1, 16)])), [], [layB])
        P.dma("sp", sgn_l[:], (lambda li: dap(I["sgn_bc"], li * 256, [(L * 256, 128), (1, 256)])), [], [layB])
        P.dma("sp", sbbc_l[:].rearrange("p a b -> p (a b)"), (lambda li: dap(I["sb_bc"], li * 256, [(L * 256, 128), (1, 256)])), [], [layB])
        P.dma("sp", sbbcs_l[:].rearrange("p a b -> p (a b)"), (lambda li: dap(I["sb_bcs"], li * 128, [(L * 128, 128), (1, 128)])), [], [layB])
        P.memset("pool", sprev[:], 0.0, [sprevB])
        chk(2.5)
        for kind, t in tiles:
            if kind == "p":
                P.dma("sp", h[:], dap(S["hbuf"], t * 512 * D, [(D, 128), (128 * D, 4), (1, D)]), [hbB[t]], hB)
            else:
                P.dma("sp", h[0:64, 0, :], dap(S["hbuf"], SEQ * D, [(D, 64), (1, D)]), [hbB[NT]], [hB[0]])
            layer_tile(kind, t, l)
            if kind == "p":
                P.dma("act", dap(S["hbuf"], t * 512 * D, [(D, 128), (128 * D, 4), (1, D)]), h[:], hB, [hbB[t]])
            else:
                P.dma("act", dap(S["hbuf"], SEQ * D, [(D, 64), (1, D)]), h[0:64, 0, :], [hB[0]], [hbB[NT]])
        P.dma("act", dap(S["pre_s"], 0, [(8, 128), (1, 8)]), sprev[:, 0, :], [sprevB], [obuf("pre")])
        P.dma("act", dap(S["pim_s"], 0, [(8, 128), (1, 8)]), sprev[:, 1, :], [sprevB], [obuf("pim")])
        lay_out = list(outB)

        def cp_out(oname, sname, total):
            row = total // 128
            if row > 16384:
                dims = [(row, 128), (16384, row // 16384), (1, 16384)]
            else:
                dims = [(row, 128), (1, row)]
            P.dma("sp", (lambda li, oname=oname, total=total, dims=dims: dap(O[oname], li * total, dims)), dap(S[sname], 0, dims), lay_out, [obuf("o_" + oname)])
        cp_out("pk", "pk_s", NH * SEQ * 64)
        cp_out("pv", "pv_s", NH * SEQ * 64)
        if cfg.with_sample:
            for oname, total in (("sk", 2 * NH * 32 * 64), ("sv", 2 * NH * 32 * 64), ("svb", 64 * 256), ("sre", 2048), ("sim", 2048)):
                cp_out(oname, oname + "_s", total)
        cp_out("pre", "pre_s", 1024)
        cp_out("pim", "pim_s", 1024)
        P.barrier()
        P.loop_end()
        for kind, t in tiles:
            if kind == "p":
                P.dma("sp", h[:], dap(S["hbuf"], t * 512 * D, [(D, 128), (128 * D, 4), (1, D)]), [], hB)
            else:
                P.dma("sp", h[0:64, 0, :], dap(S["hbuf"], SEQ * D, [(D, 64), (1, D)]), [], [hB[0]])
            final_out(kind, t)
        assert ST_.cp_ == len(ST_.items), (ST_.cp_, len(ST_.items))
        P.final_wait("sp", outB)
        block = es.enter_context(nc.Block())
        P.replay(block)
        print("built: ops=%d" % P.nops)
        if getattr(cfg, "dump", None):
            with open(cfg.dump, "w") as f_:
                for rec in P.log:
                    f_.write(repr(rec) + "\n")
    return nc


def _prep_shared(inp, L):
    f = lambda a: np.ascontiguousarray(np.asarray(a, dtype=np.float32))
    sh = {}
    sh["w_in"] = f(inp["w_in"][:L])
    sh["w_gu"] = f(inp["w_gate_up"][:L])
    sh["w_dn"] = f(inp["w_down"][:L])
    sh["w_ba"] = f(inp["w_branch_a"][:L])
    sh["w_bb"] = f(inp["w_branch_b"][:L])
    sh["w_bc"] = f(inp["w_branch_c"][:L])
    sh["w_out"] = f(inp["w_out"][:L])
    sh["w_glu"] = f(inp["ssm_w_glu"][:L])
    pk = lambda a: f(np.asarray(a)[:L].reshape(L, 8, 128).transpose(2, 0, 1))
    sh["nm_pk"] = pk(inp["norm_mix"])
    sh["nf_pk"] = pk(inp["norm_ffn"])
    sh["gfin_bc"] = f(np.broadcast_to(np.asarray(inp["norm_final"])[None, :], (128, D)))
    def pj(a):
        a = np.asarray(a)[:L].reshape(L, 8, 2, 64)
        return f(a.transpose(2, 3, 0, 1).reshape(128, L, 8))
    sh["a_re_pj"] = pj(inp["ssm_a_re"])
    sh["a_im_pj"] = pj(inp["ssm_a_im"])
    ldt_full = np.repeat(np.asarray(inp["ssm_log_dt"])[:L, :, None], 64, axis=2)
    sh["ldt_pj"] = pj(ldt_full)
    sh["a_re_row"] = f(np.asarray(inp["ssm_a_re"])[:L].reshape(L, 1024))
    sh["a_im_row"] = f(np.asarray(inp["ssm_a_im"])[:L].reshape(L, 1024))
    sh["ldt_row"] = f(ldt_full.reshape(L, 1024))
    b_re, b_im = np.asarray(inp["ssm_b_re"])[:L], np.asarray(inp["ssm_b_im"])[:L]
    c_re, c_im = np.asarray(inp["ssm_c_re"])[:L], np.asarray(inp["ssm_c_im"])[:L]
    bb_re = np.zeros((L, 128, 8, 128), np.float32)
    bb_im = np.zeros((L, 128, 8, 128), np.float32)
    cb_re = np.zeros((L, 128, 8, 128), np.float32)
    cb_im = np.zeros((L, 128, 8, 128), np.float32)
    for g in range(16):
        j, gi = g // 2, g % 2
        ch0 = 16 * (g % 8)
        st0 = 64 * gi
        bb_re[:, ch0:ch0 + 16, j, st0:st0 + 64] = b_re[:, g].transpose(0, 2, 1)
        bb_im[:, ch0:ch0 + 16, j, st0:st0 + 64] = b_im[:, g].transpose(0, 2, 1)
        cb_re[:, st0:st0 + 64, j, ch0:ch0 + 16] = c_re[:, g].transpose(0, 2, 1)
        cb_im[:, st0:st0 + 64, j, ch0:ch0 + 16] = c_im[:, g].transpose(0, 2, 1)
    sh["bblk_re"] = bb_re.reshape(L, 128, 1024)
    sh["bblk_im"] = bb_im.reshape(L, 128, 1024)
    sh["cblk_re"] = cb_re.reshape(L, 128, 1024)
    sh["cblk_im"] = cb_im.reshape(L, 128, 1024)
    sh["d_pm"] = f(np.asarray(inp["ssm_d"])[:L].reshape(L, 2, 128).transpose(2, 0, 1))
    sh["sgn_bc"] = f(np.broadcast_to(np.asarray(inp["sgu_norm"])[:L][None], (128, L, 256)))
    sw = np.asarray(inp["sgu_w"])[:L]
    sh["swT"] = f(sw.transpose(0, 3, 1, 2))
    swTs = np.zeros((L, 64, 4, 64), np.float32)
    for b in range(2):
        swTs[:, 32 * b:32 * b + 32, :, 32 * b:32 * b + 32] = sw[:, :, :32, :32].transpose(0, 3, 1, 2)
    sh["swTs"] = swTs
    sbv = np.asarray(inp["sgu_b"])[:L]
    sb_bc = np.zeros((128, L, 2, 128), np.float32)
    sb_bcs = np.zeros((128, L, 2, 64), np.float32)
    for g in range(4):
        sb_bc[64 * (g % 2):64 * (g % 2) + 64, :, g // 2, :] = sbv[None, :, g, :]
        sb_bcs[64 * (g % 2):64 * (g % 2) + 64, :, g // 2, :] = np.concatenate([sbv[:, g, :32], sbv[:, g, :32]], axis=-1)[None]
    sh["sb_bc"] = sb_bc
    sh["sb_bcs"] = sb_bcs
    return sh


def _prep_core(inp, c, L, PAST, nprompt):
    f = lambda a: np.ascontiguousarray(np.asarray(a, dtype=np.float32))
    m = {}
    m["xp"] = f(inp["x_prompt"][c % nprompt])
    sb = slice(2 * c, 2 * c + 2)
    m["xs"] = f(np.asarray(inp["x_sample"])[sb].reshape(64, D))
    def st(a):
        a = np.asarray(a)[:L, sb].reshape(L, 2, 8, 2, 64)
        return f(a.transpose(3, 4, 0, 1, 2).reshape(128, L, 2, 8))
    m["s0_re"] = st(inp["state_ssm_re"])
    m["s0_im"] = st(inp["state_ssm_im"])
    ck = np.asarray(inp["cache_sb_k"])[:L, sb]
    ckT = ck.reshape(L, 2, 4, 2, PAST, 64).transpose(0, 1, 3, 5, 2, 4).reshape(L, 2, 128, 4 * PAST)
    cvv = np.asarray(inp["cache_sb_v"])[:L, sb]
    cv = cvv.reshape(L, 2, NH, PAST // 128, 128, 64).transpose(0, 1, 4, 3, 2, 5).reshape(L, 2, 128, PAST * 4)
    for b in range(2):
        m["ckT%d" % b] = f(ckT[:, b])
        for ci in range((PAST * 4) // 2048):
            m["cv%d_%d" % (b, ci)] = f(cv[:, b, :, ci * 2048:(ci + 1) * 2048])
    return m


def _unstate(a):
    sh = a.shape[:-2]
    a = a.reshape(sh + (2, 64, 8))
    return np.ascontiguousarray(np.moveaxis(a, -1, -3).reshape(sh + (16, 64)))


_NC_CACHE = {}


def run(inp, cfg, n_cores=8, nprompt=4):
    key = (cfg.L, cfg.SEQ, cfg.PAST, cfg.with_sample, getattr(cfg, 'debug', False), cfg.stop)
    if key not in _NC_CACHE:
        _NC_CACHE[key] = build(cfg)
    nc = _NC_CACHE[key]
    L = cfg.L
    sh = _prep_shared(inp, L)
    in_maps = []
    for c in range(n_cores):
        m = dict(sh)
        m.update(_prep_core(inp, c, L, cfg.PAST, nprompt))
        in_maps.append(m)
    res = run_bass_kernel_spmd(nc, in_maps, core_ids=list(range(n_cores)))
    R = res.results
    global LAST_R
    LAST_R = R
    npr = min(nprompt, n_cores)
    y_prompt = np.stack([R[b]["yp"] for b in range(npr)])
    y_sample = np.concatenate([R[c]["ys"].reshape(2, 32, D) for c in range(n_cores)], axis=0)
    p_re = np.stack([_unstate(R[b]["pre"]) for b in range(npr)], axis=1)
    p_im = np.stack([_unstate(R[b]["pim"]) for b in range(npr)], axis=1)
    p_k = np.stack([R[b]["pk"] for b in range(npr)], axis=1)
    p_v = np.stack([R[b]["pv"] for b in range(npr)], axis=1)
    s_re = np.concatenate([_unstate(R[c]["sre"]) for c in range(n_cores)], axis=1)
    s_im = np.concatenate([_unstate(R[c]["sim"]) for c in range(n_cores)], axis=1)
    s_k = np.concatenate([R[c]["sk"] for c in range(n_cores)], axis=1)
    s_v = np.concatenate([R[c]["sv"] for c in range(n_cores)], axis=1)
    s_vb = np.concatenate([R[c]["svb"].reshape(L, 2, 32, 256) for c in range(n_cores)], axis=1)
    f = lambda a: np.ascontiguousarray(a, dtype=np.float32)
    return tuple(f(a) for a in (y_prompt, y_sample, p_re, p_im, p_k, p_v, s_re, s_im, s_k, s_v, s_vb))


def kernel(**inputs):
    cfg = Cfg(L=4, SEQ=8192, PAST=1024, with_sample=True)
    return run(inputs, cfg, n_cores=8, nprompt=4)
```
